# Optimizing a Trainium2 kernel written in Bass

```python
import jax, jax.numpy as jnp
from jax import lax
import numpy as np

D_MODEL = 1024
BATCH = 8
SEQ = 2048
DEPTH = 1

HEAD_DIM = 64
N_FOX_HEADS = 8
DIL_GROUPS = ((128, 1), (512, 4), (2048, 16))
N_DIL_HEADS_PER_GROUP = 4
N_DIL_HEADS = N_DIL_HEADS_PER_GROUP * len(DIL_GROUPS)
FOX_W = N_FOX_HEADS * HEAD_DIM
DIL_W = N_DIL_HEADS * HEAD_DIM
DIL_OUT_W = N_DIL_HEADS_PER_GROUP * HEAD_DIM
ROT_DIM = HEAD_DIM // 4
ROPE_THETA = 500000.0
D_FF = -(-8 * D_MODEL // (3 * 256)) * 256
Q_BLOCK = 128
EPS = 1e-6
NEG = -1e30
SPLIT_SIZES = (FOX_W, FOX_W, FOX_W, N_FOX_HEADS, DIL_W, DIL_W, DIL_W, D_MODEL, D_MODEL)
IN_COLS = sum(SPLIT_SIZES)

kernel_name = "hybrid_fox_dilated_adaln_block"


def rmsnorm(x, g):
    xf = x.astype(jnp.float32)
    y = xf * lax.rsqrt(jnp.mean(xf * xf, axis=-1, keepdims=True) + EPS)
    return (y * g.astype(jnp.float32)).astype(x.dtype)


def modulate(h, shift, scale):
    return h * (1 + scale[:, None, :]) + shift[:, None, :]


def partial_rope(t):
    S = t.shape[1]
    pos = jnp.arange(S, dtype=jnp.float32)
    inv_freq = ROPE_THETA ** (-jnp.arange(0, ROT_DIM, 2, dtype=jnp.float32) / ROT_DIM)
    ang = pos[:, None] * inv_freq[None, :]
    cos = jnp.cos(ang)[None, :, None, :]
    sin = jnp.sin(ang)[None, :, None, :]
    tf = t.astype(jnp.float32)
    x1 = tf[..., : ROT_DIM // 2]
    x2 = tf[..., ROT_DIM // 2: ROT_DIM]
    rot = jnp.concatenate([x1 * cos - x2 * sin, x2 * cos + x1 * sin], axis=-1)
    return jnp.concatenate([rot, tf[..., ROT_DIM:]], axis=-1).astype(t.dtype)


def forgetting_attention(q, k, v, f_logit):
    B, S, H, Dh = q.shape
    scale = Dh ** -0.5
    F = jnp.cumsum(jax.nn.log_sigmoid(f_logit.astype(jnp.float32)), axis=1)
    Ft = jnp.transpose(F, (0, 2, 1))
    outs = []
    for blk in range(S // Q_BLOCK):
        q0, q1 = blk * Q_BLOCK, (blk + 1) * Q_BLOCK
        logits = jnp.einsum('bqhd,bkhd->bhqk', q[:, q0:q1], k[:, :q1],
                            preferred_element_type=jnp.float32) * scale
        logits = logits + (Ft[:, :, q0:q1, None] - Ft[:, :, None, :q1])
        causal = jnp.arange(q0, q1)[:, None] >= jnp.arange(q1)[None, :]
        p = jax.nn.softmax(jnp.where(causal[None, None], logits, NEG), axis=-1)
        outs.append(jnp.einsum('bhqk,bkhd->bqhd', p.astype(v.dtype), v[:, :q1]))
    return jnp.concatenate(outs, axis=1)


def dilated_window_attention(q, k, v, dilation, span):
    B, S, H, Dh = q.shape
    L = S // dilation
    nb = -(-L // span)
    Lp = nb * span
    Z = B * dilation
    scale = Dh ** -0.5

    def to_sub(t):
        t = t.reshape(B, L, dilation, H, Dh).transpose(0, 2, 1, 3, 4).reshape(Z, L, H, Dh)
        t = jnp.pad(t, ((0, 0), (0, Lp - L), (0, 0), (0, 0)))
        return t.reshape(Z, nb, span, H, Dh)

    qb, kb, vb = to_sub(q), to_sub(k), to_sub(v)

    def band(t):
        prev = jnp.pad(t, ((0, 0), (1, 0), (0, 0), (0, 0), (0, 0)))[:, :-1]
        return jnp.concatenate([prev, t], axis=2)

    kband, vband = band(kb), band(vb)
    logits = jnp.einsum('znqhd,znkhd->znhqk', qb, kband,
                        preferred_element_type=jnp.float32) * scale
    qi = jnp.arange(span)[:, None] + span
    kj = jnp.arange(2 * span)[None, :]
    dist = qi - kj
    in_band = (dist >= 0) & (dist <= span)
    has_prev = (jnp.arange(nb)[:, None, None] > 0) | (kj >= span)[None]
    valid = in_band[None] & has_prev
    logits = jnp.where(valid[None, :, None], logits, NEG)
    m = jnp.max(logits, axis=-1, keepdims=True)
    p = jnp.exp(logits - m)
    s = jnp.sum(p, axis=-1)
    o = jnp.einsum('znhqk,znkhd->znqhd', p.astype(v.dtype), vband).astype(jnp.float32)
    o = o / jnp.transpose(s, (0, 1, 3, 2))[..., None]
    lse = jnp.transpose(m[..., 0] + jnp.log(s), (0, 1, 3, 2))

    def from_sub(t):
        rest = t.shape[3:]
        t = t.reshape((Z, Lp) + rest)[:, :L]
        t = t.reshape((B, dilation, L) + rest)
        t = jnp.swapaxes(t, 1, 2)
        return t.reshape((B, S) + rest)

    return from_sub(o), from_sub(lse)


def hybrid_mixer(h, w_in, b_fgate, w_br_a, w_br_b, w_out):
    B, S, _ = h.shape
    proj = jnp.einsum('bsd,de->bse', h, w_in)
    splits = [int(i) for i in np.cumsum(SPLIT_SIZES)[:-1]]
    qa, ka, va, fa, qb, kb, vb, ga, gb = jnp.split(proj, splits, axis=-1)

    qa = qa.reshape(B, S, N_FOX_HEADS, HEAD_DIM)
    ka = ka.reshape(B, S, N_FOX_HEADS, HEAD_DIM)
    va = va.reshape(B, S, N_FOX_HEADS, HEAD_DIM)
    ya = forgetting_attention(qa, ka, va, fa + b_fgate)
    ya = jnp.einsum('bse,ed->bsd', ya.reshape(B, S, FOX_W), w_br_a)

    qb = partial_rope(qb.reshape(B, S, N_DIL_HEADS, HEAD_DIM))
    kb = partial_rope(kb.reshape(B, S, N_DIL_HEADS, HEAD_DIM))
    vb = vb.reshape(B, S, N_DIL_HEADS, HEAD_DIM)
    outs, lses = [], []
    for g, (window, dilation) in enumerate(DIL_GROUPS):
        sl = slice(g * N_DIL_HEADS_PER_GROUP, (g + 1) * N_DIL_HEADS_PER_GROUP)
        o, lse = dilated_window_attention(qb[:, :, sl], kb[:, :, sl], vb[:, :, sl],
                                          dilation, window // dilation)
        outs.append(o)
        lses.append(lse)
    alpha = jax.nn.softmax(jnp.stack(lses, axis=0), axis=0)
    yb = jnp.sum(alpha[..., None] * jnp.stack(outs, axis=0), axis=0).astype(h.dtype)
    yb = jnp.einsum('bse,ed->bsd', yb.reshape(B, S, DIL_OUT_W), w_br_b)

    merged = jax.nn.sigmoid(ga) * ya + jax.nn.sigmoid(gb) * yb
    return jnp.einsum('bsd,de->bse', merged, w_out)


def swiglu(h, w_gate, w_up, w_down):
    a = jnp.einsum('bsd,df->bsf', h, w_gate)
    u = jnp.einsum('bsd,df->bsf', h, w_up)
    return jnp.einsum('bsf,fd->bsd', jax.nn.silu(a) * u, w_down)


def setup_inputs(seed: int = 0) -> dict:
    key = jax.random.key(seed)
    ks = jax.random.split(key, 16)
    f32 = jnp.float32
    L, D = DEPTH, D_MODEL
    nrm = lambda k, shape, fan_in, s=1.0: (jax.random.normal(k, shape, f32) * (s * fan_in ** -0.5))
    return {
        "x": jax.random.normal(ks[0], (BATCH, SEQ, D), f32),
        "c": jax.random.normal(ks[1], (BATCH, D), f32),
        "w_ada": nrm(ks[2], (L, D, 6 * D), D, 0.5),
        "b_ada": 0.1 * jax.random.normal(ks[3], (L, 6 * D), f32),
        "g_mix": 1.0 + 0.02 * jax.random.normal(ks[4], (L, D), f32),
        "w_in": nrm(ks[5], (L, D, IN_COLS), D),
        "b_fgate": jax.random.uniform(ks[6], (L, N_FOX_HEADS), f32, 1.0, 4.0),
        "w_br_a": nrm(ks[7], (L, FOX_W, D), FOX_W),
        "w_br_b": nrm(ks[8], (L, DIL_OUT_W, D), DIL_OUT_W),
        "w_out": nrm(ks[9], (L, D, D), D),
        "g_ffn": 1.0 + 0.02 * jax.random.normal(ks[10], (L, D), f32),
        "w_ffn_gate": nrm(ks[11], (L, D, D_FF), D),
        "w_ffn_up": nrm(ks[12], (L, D, D_FF), D),
        "w_ffn_down": nrm(ks[13], (L, D_FF, D), D_FF),
        "g_final": 1.0 + 0.02 * jax.random.normal(ks[14], (D,), f32),
    }


def reference(x, c, w_ada, b_ada, g_mix, w_in, b_fgate, w_br_a, w_br_b, w_out,
              g_ffn, w_ffn_gate, w_ffn_up, w_ffn_down, g_final):
    for l in range(DEPTH):
        mod = jnp.einsum('bd,de->be', jax.nn.silu(c), w_ada[l]) + b_ada[l]
        sh_m, sc_m, ga_m, sh_f, sc_f, ga_f = jnp.split(mod, 6, axis=-1)
        h = modulate(rmsnorm(x, g_mix[l]), sh_m, sc_m)
        x = x + ga_m[:, None, :] * hybrid_mixer(h, w_in[l], b_fgate[l], w_br_a[l], w_br_b[l], w_out[l])
        h = modulate(rmsnorm(x, g_ffn[l]), sh_f, sc_f)
        x = x + ga_f[:, None, :] * swiglu(h, w_ffn_gate[l], w_ffn_up[l], w_ffn_down[l])
    return rmsnorm(x, g_final)
```

```python
import numpy as np
from contextlib import ExitStack
import concourse.bass as bass
import concourse.mybir as mybir
from concourse.bass_utils import run_bass_kernel_spmd

F32 = mybir.dt.float32
BF16 = mybir.dt.bfloat16
AF = mybir.ActivationFunctionType
ALU = mybir.AluOpType

D = 1024
S = 2048
NT = 16
NC4 = 4
KC = 8
DFF = 2816
NFF = 22
EPS = 1e-6
NEG = -30000.0
SCALE = 0.125
DIL = (1, 4, 16)
ENG = ("pe", "act", "dve", "pool", "sp")


class T:
    __slots__ = ("t", "w", "r", "ds", "excl")

    def __init__(self, t=None, excl=False):
        self.t = t
        self.excl = excl
        self.w = None
        self.r = {}
        self.ds = None


class Rec:
    def __init__(self):
        self.call = None

    def __getattr__(self, name):
        def f(*a, **k):
            self.call = (name, a, k)
        return f


def _record(fn):
    r = Rec()
    fn(r)
    assert r.call is not None
    return r.call


class Sched:
    def __init__(self, nc, es, n_dma=24):
        self.nc = nc
        self.sem = {e: es.enter_context(nc.semaphore("s_" + e)) for e in ENG}
        self.cnt = {e: 0 for e in ENG}
        self.es = es
        self.dsem = []
        self.dcnt = []
        self.dq = []
        self.inflight = {"pool": [], "sp": []}
        self.max_inflight = {"pool": 2, "sp": 4}
        self.streams = {e: [] for e in ENG}
        self.waited = {e: {} for e in ENG}
        self.stopped = False

    def _need(self, eng, deps):
        for key, val in deps:
            if key == eng and eng == "pe":
                continue
            if self.waited[eng].get(key, 0) >= val:
                continue
            self.waited[eng][key] = val
            self.streams[eng].append(("w", key, val))

    @staticmethod
    def _deps(reads, writes, eng=None):
        deps = []
        for t in reads:
            if t.w is not None:
                deps.append(t.w)
            if t.excl:
                deps.extend((k, v) for k, v in t.r.items() if k != eng)
        for t in writes:
            if t.w is not None:
                deps.append(t.w)
            deps.extend(t.r.items())
        return deps

    def op(self, eng, fn, reads=(), writes=(), inc=True):
        if self.stopped:
            return
        self._need(eng, self._deps(reads, writes, eng))
        val = self.cnt[eng] + 1
        if inc:
            self.cnt[eng] = val
        self.streams[eng].append(("o", _record(fn), inc))
        for t in reads:
            if t.r.get(eng, 0) < val:
                t.r[eng] = val
        for t in writes:
            t.w = (eng, val)
            t.r = {}

    def dma(self, q, fn, reads=(), writes=()):
        if self.stopped:
            return
        own = writes[0] if len(writes) else reads[0]
        if own.ds is None:
            own.ds = len(self.dsem)
            self.dsem.append(self.es.enter_context(self.nc.semaphore("d%d" % own.ds)))
            self.dcnt.append(0)
            self.dq.append(q)
        i = own.ds
        assert self.dq[i] == q
        key = ("d", i)
        deps = self._deps(reads, writes)
        if self.dcnt[i] > 0:
            deps.append((key, self.dcnt[i]))
        fl = self.inflight[q]
        while len(fl) >= self.max_inflight[q]:
            deps.append(fl.pop(0))
        self._need(q, deps)
        self.dcnt[i] += 16
        val = self.dcnt[i]
        fl.append((key, val))
        self.streams[q].append(("d", _record(fn), i))
        for t in reads:
            t.r[key] = val
        for t in writes:
            t.w = (key, val)
            t.r = {}

    def barrier(self):
        if self.stopped:
            return
        edeps = [(e, self.cnt[e]) for e in ENG if self.cnt[e] > 0]
        for q in ("sp", "pool"):
            ddeps = [(("d", i), v) for i, v in enumerate(self.dcnt) if v > 0 and self.dq[i] == q]
            self._need(q, edeps + ddeps)
            self.cnt[q] += 1
            self.streams[q].append(("o", ("nop", (), {}), True))
        edeps = [(e, self.cnt[e]) for e in ENG if self.cnt[e] > 0]
        for e in ENG:
            self._need(e, edeps)

    def finish(self):
        ddeps = [(("d", i), v) for i, v in enumerate(self.dcnt) if v > 0 and self.dq[i] == "pool"]
        if ddeps:
            self._need("pool", ddeps)
            self.cnt["pool"] += 1
            self.streams["pool"].append(("o", ("nop", (), {}), True))
        for i, v in enumerate(self.dcnt):
            if v > 0 and self.dq[i] == "sp":
                self._need("sp", [(("d", i), v)])
        for e in ENG:
            self._need("sp", [(e, self.cnt[e])] if self.cnt[e] > 0 else [])

    def emit(self):
        nc = self.nc
        needed = {e: set() for e in ENG}
        for e in ENG:
            for item in self.streams[e]:
                if item[0] == "w" and isinstance(item[1], str):
                    needed[item[1]].add(item[2])
        rank = {e: {v: i + 1 for i, v in enumerate(sorted(needed[e]))} for e in ENG}
        with nc.Block() as block:
            def mk(e):
                def body(eng):
                    prov = 0
                    for item in self.streams[e]:
                        if item[0] == "w":
                            key = item[1]
                            if isinstance(key, str):
                                eng.wait_ge(self.sem[key], rank[key][item[2]])
                            else:
                                eng.wait_ge(self.dsem[key[1]], item[2])
                        elif item[0] == "o":
                            c = item[1]
                            ins = getattr(eng, c[0])(*c[1], **c[2])
                            if item[2]:
                                prov += 1
                                if prov in needed[e]:
                                    ins.then_inc(self.sem[e], 1)
                        else:
                            c = item[1]
                            getattr(eng, c[0])(*c[1], **c[2]).then_inc(self.dsem[item[2]], 16)
                    assert prov == self.cnt[e], (e, prov, self.cnt[e])
                return body
            block.tensor(mk("pe"))
            block.scalar(mk("act"))
            block.vector(mk("dve"))
            block.gpsimd(mk("pool"))
            block.sync(mk("sp"))


class Rot:
    def __init__(self, items):
        self.items = items
        self.i = 0

    def next(self):
        x = self.items[self.i]
        self.i = (self.i + 1) % len(self.items)
        return x


def tok_view(ap2d, g, j):
    if g == 0:
        return ap2d[:, j * 512:(j + 1) * 512]
    if g == 1:
        return ap2d.rearrange("p (i r) -> p r i", r=4)[:, j, :]
    return ap2d.rearrange("p (i r) -> p r i", r=16)[:, 4 * j:4 * j + 4, :]


def cls_out(ap2d, g, tc):
    dd = DIL[g]
    if g == 0:
        return ap2d[:, tc * 512:(tc + 1) * 512]
    n = 512 // dd
    return ap2d.rearrange("p (r i) -> p r i", r=dd)[:, :, n * tc:n * (tc + 1)]


def nat_in(ap2d, g):
    if g == 0:
        return ap2d
    return ap2d.rearrange("p (i r) -> p r i", r=DIL[g])


def chunk_view(ap2d, g):
    if g == 2:
        return ap2d.rearrange("p (a b) -> p a b", a=4)
    return ap2d


class _Stop(Exception):
    pass


def build_program(debug=(), stop=None):
    nc = bass.Bass("TRN2", target_bir_lowering=False)

    def din(name, shape):
        return nc.dram_tensor(name, list(shape), F32, kind="ExternalInput").ap()

    x_d = din("x", [S, D])
    ccol_d = din("c_col", [128, KC])
    wada_d = din("w_ada", [D, 6 * D])
    badac_d = din("b_ada_col", [128, 48])
    bgam_d = din("b_gam_bc", [128, D])
    bgaf_d = din("b_gaf_bc", [128, D])
    gmix_d = din("g_mix_col", [128, KC])
    gffn_d = din("g_ffn_col", [128, KC])
    gfin_d = din("g_fin_bc", [128, D])
    bfg_d = din("b_fg_col", [8, 1])
    wqa_d = din("w_qa", [D, 512])
    wka_d = din("w_ka", [D, 512])
    wva_d = din("w_va", [D, 512])
    wfa_d = din("w_fa", [128, KC * 8])
    wqb_d = din("w_qb", [D, 768])
    wkb_d = din("w_kb", [D, 768])
    wvb_d = din("w_vb", [D, 768])
    wga_d = din("w_ga", [D, D])
    wgb_d = din("w_gb", [D, D])
    wbra_d = din("w_br_a", [512, D])
    wbrb_d = din("w_br_b", [256, D])
    wout_d = din("w_out", [D, D])
    wfg_d = din("w_ffn_gate", [D, DFF])
    wfu_d = din("w_ffn_up", [D, DFF])
    wfd_d = din("w_ffn_down", [DFF, D])
    ident_d = din("k_ident", [128, 128])
    mask_d = din("k_mask", [128, 768])
    sel_d = din("k_sel", [128, 8 * 128])
    pm_d = din("k_pm", [128, 128])
    cos_d = din("k_cos", [128, S])
    sin_d = din("k_sin", [128, S])
    ones_d = din("k_ones", [128, 128])
    out_d = nc.dram_tensor("out", [S, D], F32, kind="ExternalOutput").ap()
    dbg_d = {}

    with ExitStack() as es:
        S_ = Sched(nc, es)

        def sb(name, shape, dt, scope=es):
            return scope.enter_context(nc.sbuf_tensor(name, list(shape), dt))

        PB = [T(es.enter_context(nc.psum_tensor("pb%d" % i, [128, 512], F32)), excl=True) for i in range(8)]
        PTh = [PB[6], PB[7]]
        PTv = [PB[6].t[:, :].bitcast(BF16), PB[7].t[:, :].bitcast(BF16)]
        pall = Rot(PB)

        bufA = sb("bufA", [128, KC, S], BF16)
        bufB = sb("bufB", [128, KC, S], BF16)
        bufA_d = [T() for _ in range(NC4)]
        bufB_d = [T() for _ in range(NC4)]
        WT = [T(sb("wt%d" % i, [128, KC, 512], BF16)) for i in range(3)]
        wrot = Rot(WT)
        ident_bf = T(sb("ident_bf", [128, 128], BF16))
        ident_f = T(sb("ident_f", [128, 128], F32))
        maskX = T(sb("maskX", [128, 768], BF16))
        selones = T(sb("selones", [128, 8 * 128], BF16))
        pm_bf = T(sb("pm_bf", [128, 128], BF16))
        ones_bf = T(sb("ones_bf", [128, 128], BF16))
        gaBCm = T(sb("gaBCm", [128, D], F32))
        gaBCf = T(sb("gaBCf", [128, D], F32))
        gfinBC = T(sb("gfinBC", [128, D], F32))
        modT = T(sb("modT", [128, 32], F32))
        gsc = T(sb("gsc", [128, 16], F32))
        small = T(sb("small", [128, 64], F32))
        badac = T(sb("badac", [128, 48], F32))
        cs_bf = T(sb("cs_bf", [128, KC], BF16))
        csb_bf = T(sb("csb_bf", [128, KC, 128], BF16))
        ssq = T(sb("ssq", [128, 3 * NT], F32))
        rstd = T(sb("rstd", [128, 3 * NT], F32))

        def load_w(src_ap, ncols, kc=KC, tile=None):
            wt = wrot.next() if tile is None else tile
            S_.dma("pool", lambda e, wt=wt: e.dma_start(
                out=wt.t[:, 0:kc, 0:ncols], in_=src_ap.rearrange("(kc p) n -> p kc n", p=128)), writes=[wt])
            return wt

        def dbg(name, tiles, ap, shape, dt=F32):
            if name not in debug:
                if stop == name:
                    S_.stopped = True
                return
            d = nc.dram_tensor("dbg_" + name, list(shape), dt, kind="ExternalOutput").ap()
            dbg_d[name] = d
            S_.dma("sp", lambda e: e.dma_start(out=d, in_=ap), reads=[T()] + list(tiles))
            if stop == name:
                S_.stopped = True

        def chk(name):
            if stop == name:
                S_.stopped = True

        try:
            for (tl, src) in ((ident_bf, ident_d), (maskX, mask_d), (selones, sel_d), (pm_bf, pm_d), (ones_bf, ones_d)):
                S_.dma("pool", lambda e, tl=tl, src=src: e.dma_start(out=tl.t[:], in_=src), writes=[tl])
            S_.dma("sp", lambda e: e.dma_start(out=ident_f.t[:], in_=ident_d), writes=[ident_f])
            S_.dma("sp", lambda e: e.dma_start(out=small.t[:, 0:8], in_=ccol_d), writes=[small])
            S_.dma("sp", lambda e: e.dma_start(out=small.t[:, 8:16], in_=gmix_d), writes=[small])
            S_.dma("sp", lambda e: e.dma_start(out=small.t[:, 16:24], in_=gffn_d), writes=[small])
            S_.dma("sp", lambda e: e.dma_start(out=badac.t[:], in_=badac_d), writes=[badac])
            S_.op("act", lambda e: e.activation(out=cs_bf.t[:], in_=small.t[:, 0:8], func=AF.Silu), reads=[small], writes=[cs_bf])
            S_.op("dve", lambda e: e.tensor_copy(out=csb_bf.t[:], in_=cs_bf.t[:].unsqueeze(2).to_broadcast([128, KC, 128])),
                  reads=[cs_bf], writes=[csb_bf])

            def ada_cols(sec, dst0):
                pb = pall.next()
                for half in range(2):
                    wt = load_w(wada_d[:, sec * D + half * 512: sec * D + (half + 1) * 512], 512)
                    for j in range(4):
                        col = half * 4 + j
                        for kc in range(KC):
                            S_.op("pe", lambda e, wt=wt, j=j, kc=kc, col=col, pb=pb: e.matmul(
                                pb.t[:, col:col + 1], lhsT=wt.t[:, kc, j * 128:(j + 1) * 128], rhs=cs_bf.t[:, kc:kc + 1],
                                start=(kc == 0), stop=(kc == KC - 1)),
                                reads=[wt, cs_bf], writes=[pb], inc=(kc == KC - 1))
                S_.op("dve", lambda e, pb=pb: e.tensor_tensor(out=modT.t[:, dst0:dst0 + 8], in0=pb.t[:, 0:8],
                                                             in1=badac.t[:, sec * 8:(sec + 1) * 8], op=ALU.add),
                      reads=[pb, badac], writes=[modT])

            def ada_rows(sec, dstT):
                for half in range(2):
                    wt = load_w(wada_d[:, sec * D + half * 512: sec * D + (half + 1) * 512], 512)
                    pb = pall.next()
                    for kc in range(KC):
                        S_.op("pe", lambda e, wt=wt, kc=kc, pb=pb: e.matmul(
                            pb.t[:, :], lhsT=csb_bf.t[:, kc, :], rhs=wt.t[:, kc, :], start=(kc == 0), stop=(kc == KC - 1)),
                            reads=[wt, csb_bf], writes=[pb], inc=(kc == KC - 1))
                    S_.op("dve", lambda e, pb=pb, half=half: e.tensor_tensor(
                        out=dstT.t[:, half * 512:(half + 1) * 512], in0=pb.t[:, :], in1=dstT.t[:, half * 512:(half + 1) * 512],
                        op=ALU.add), reads=[pb, dstT], writes=[dstT])

            def ada_cols_half(sec, dst0, half, tile, ps=None):
                pb = (ps or pall).next()
                wt = load_w(wada_d[:, sec * D + half * 512: sec * D + (half + 1) * 512], 512, tile=tile)
                for j in range(4):
                    for kc in range(KC):
                        S_.op("pe", lambda e, wt=wt, j=j, kc=kc, pb=pb: e.matmul(
                            pb.t[:, j:j + 1], lhsT=wt.t[:, kc, j * 128:(j + 1) * 128], rhs=cs_bf.t[:, kc:kc + 1],
                            start=(kc == 0), stop=(kc == KC - 1)),
                            reads=[wt, cs_bf], writes=[pb], inc=(kc == KC - 1))
                c0 = dst0 + half * 4
                b0 = sec * 8 + half * 4
                S_.op("dve", lambda e, pb=pb: e.tensor_tensor(out=modT.t[:, c0:c0 + 4], in0=pb.t[:, 0:4],
                                                             in1=badac.t[:, b0:b0 + 4], op=ALU.add),
                      reads=[pb, badac], writes=[modT])

            def ada_rows_half(sec, dstT, half, tile, ps=None):
                wt = load_w(wada_d[:, sec * D + half * 512: sec * D + (half + 1) * 512], 512, tile=tile)
                pb = (ps or pall).next()
                for kc in range(KC):
                    S_.op("pe", lambda e, wt=wt, kc=kc, pb=pb: e.matmul(
                        pb.t[:, :], lhsT=csb_bf.t[:, kc, :], rhs=wt.t[:, kc, :], start=(kc == 0), stop=(kc == KC - 1)),
                        reads=[wt, csb_bf], writes=[pb], inc=(kc == KC - 1))
                S_.op("dve", lambda e, pb=pb, half=half: e.tensor_tensor(
                    out=dstT.t[:, half * 512:(half + 1) * 512], in0=pb.t[:, :], in1=dstT.t[:, half * 512:(half + 1) * 512],
                    op=ALU.add), reads=[pb, dstT], writes=[dstT])

            def make_gsc(which):
                sc0 = 8 if which == 0 else 24
                g0 = 8 if which == 0 else 16
                S_.op("dve", lambda e: e.scalar_tensor_tensor(
                    out=gsc.t[:, which * 8:(which + 1) * 8], in0=modT.t[:, sc0:sc0 + 8], scalar=1.0,
                    in1=small.t[:, g0:g0 + 8], op0=ALU.add, op1=ALU.mult), reads=[modT, small], writes=[gsc])

            def ada_first():
                ada_cols(0, 0)
                ada_cols(1, 8)
                make_gsc(0)

            def norm_to_T(src_tile_fn, nidx, dstbuf, dst_d, sh0, gs0, scope, hook=None, after_chunk=None):
                junk = T(sb("junk%d" % nidx, [128, D], BF16, scope))
                xs = [T(sb("xs%d_%d" % (nidx, i), [128, D], BF16, scope)) for i in range(8)]
                sq = T(sb("sq%d" % nidx, [128, 8], F32, scope))
                sq_c = [T() for _ in range(8)]
                ssq_c = [T() for _ in range(NT)]
                rstd_c = [T() for _ in range(NT)]

                def stats(tc):
                    for i in range(4):
                        tt = tc * 4 + i
                        col = nidx * NT + tt
                        b = (tc % 2) * 4 + i
                        xt, xap = src_tile_fn(tt)
                        S_.op("act", lambda e: e.activation(
                            out=junk.t[:], in_=xap, func=AF.Square, accum_out=ssq.t[:, col:col + 1]),
                            reads=[xt], writes=[junk, ssq_c[tt]])
                        S_.op("act", lambda e: e.activation(
                            out=sq.t[:, b:b + 1], in_=ssq.t[:, col:col + 1], func=AF.Sqrt, scale=1.0 / D, bias=EPS),
                            reads=[ssq_c[tt]], writes=[sq_c[b]])
                        S_.op("dve", lambda e: e.reciprocal(out=rstd.t[:, col:col + 1], in_=sq.t[:, b:b + 1]),
                              reads=[sq_c[b]], writes=[rstd_c[tt]])
                        S_.op("dve", lambda e: e.tensor_scalar_mul(
                            out=xs[b].t[:], in0=xap, scalar1=rstd.t[:, col:col + 1]),
                            reads=[xt, rstd_c[tt]], writes=[xs[b]])

                def tr_evac(tc):
                    for c in range(KC):
                        h = c % 2
                        pt = PTh[h]
                        for i in range(4):
                            b = (tc % 2) * 4 + i
                            S_.op("pe", lambda e: e.transpose(
                                PTv[h][:, i * 128:(i + 1) * 128], xs[b].t[:, c * 128:(c + 1) * 128], ident_bf.t[:]),
                                reads=[xs[b], ident_bf], writes=[pt], inc=(i == 3))
                        if h == 0:
                            S_.op("act", lambda e: e.activation(
                                out=dstbuf[:, c, tc * 512:(tc + 1) * 512], in_=PTv[h][:, 0:512], func=AF.Identity,
                                scale=gsc.t[:, gs0 + c:gs0 + c + 1], bias=modT.t[:, sh0 + c:sh0 + c + 1]),
                                reads=[pt, gsc, modT], writes=[dst_d[tc]])
                        else:
                            S_.op("dve", lambda e: e.tensor_scalar(
                                out=dstbuf[:, c, tc * 512:(tc + 1) * 512], in0=PTv[h][:, 0:512],
                                scalar1=gsc.t[:, gs0 + c:gs0 + c + 1], scalar2=modT.t[:, sh0 + c:sh0 + c + 1],
                                op0=ALU.mult, op1=ALU.add), reads=[pt, gsc, modT], writes=[dst_d[tc]])

                stats(0)
                stats(1)
                if hook is not None:
                    hook()
                for tc in range(NC4):
                    tr_evac(tc)
                    if tc + 2 < NC4:
                        stats(tc + 2)
                    if after_chunk is not None:
                        after_chunk(tc)

            sf = es.enter_context(ExitStack())
            Vall = T(sb("Vall", [128, NT, 512], BF16, sf))
            wv_box = []

            def ada_first_and_wv():
                ada_first()
                wv_box.append(load_w(wva_d, 512, tile=WT[2]))

            def v_chunk(tc):
                wv = wv_box[0]
                for tt in range(4 * tc, 4 * tc + 4):
                    pb = pall.next()
                    for kc in range(KC):
                        S_.op("pe", lambda e, kc=kc, tt=tt, pb=pb: e.matmul(
                            pb.t[:, :], lhsT=bufA[:, kc, tt * 128:(tt + 1) * 128], rhs=wv.t[:, kc, :],
                            start=(kc == 0), stop=(kc == KC - 1)), reads=[wv, bufA_d[tc]], writes=[pb], inc=(kc == KC - 1))
                    S_.op("act", lambda e, tt=tt, pb=pb: e.activation(out=Vall.t[:, tt, :], in_=pb.t[:, :], func=AF.Copy),
                          reads=[pb], writes=[Vall])

            with ExitStack() as sc1:
                xin = [T(sb("xin%d" % i, [128, D], F32, sc1)) for i in range(4)]
                xrot = Rot(xin)

                def x_from_hbm(tt):
                    xt = xrot.next()
                    S_.dma("sp", lambda e, xt=xt, tt=tt: e.dma_start(out=xt.t[:], in_=x_d[tt * 128:(tt + 1) * 128, :]), writes=[xt])
                    return xt, xt.t[:]
                norm_to_T(x_from_hbm, 0, bufA, bufA_d, 0, 0, sc1, hook=ada_first_and_wv, after_chunk=v_chunk)
            S_.barrier()
            S_.dma("sp", lambda e: e.dma_start(out=gfinBC.t[:], in_=gfin_d), writes=[gfinBC])
            S_.dma("sp", lambda e: e.dma_start(out=gaBCm.t[:], in_=bgam_d), writes=[gaBCm])
            S_.dma("sp", lambda e: e.dma_start(out=gaBCf.t[:], in_=bgaf_d), writes=[gaBCf])
            dbg("hT", bufA_d, bufA[:], [128, KC, S], BF16)

            deferred = [lambda t, p: ada_rows_half(2, gaBCm, 0, t, p), lambda t, p: ada_rows_half(2, gaBCm, 1, t, p),
                        lambda t, p: ada_cols_half(3, 16, 0, t, p), lambda t, p: ada_cols_half(3, 16, 1, t, p),
                        lambda t, p: ada_cols_half(4, 24, 0, t, p), lambda t, p: (ada_cols_half(4, 24, 1, t, p), make_gsc(1)),
                        lambda t, p: ada_rows_half(5, gaBCf, 0, t, p), lambda t, p: ada_rows_half(5, gaBCf, 1, t, p)]

            STB = Rot([PB[0], PB[1], PB[2], PB[7]])
            ACC = Rot([(PB[3], PB[4]), (PB[5], PB[6])])
            if True:
                yaT = sb("yaT", [128, 4, S], BF16, sf)
                yaT_d = [T() for _ in range(NC4)]
                wbrA = T(sb("wbrA", [128, 4, D], BF16, sf))
                kz = [T(sb("kzA%d" % i, [128, S], BF16, sf)) for i in range(2)]
                qz = [T(sb("qzA%d" % i, [128, S], BF16, sf)) for i in range(2)]
                PTb = Rot([T(sb("PTbA%d" % i, [128, 512], BF16, sf)) for i in range(3)])
                rec = Rot([T(sb("recA%d" % i, [128, 512], F32, sf)) for i in range(2)])
                Grow = T(sb("Grow", [8, S], F32, sf))
                spr = T(sb("spr", [8, S], F32, sf))
                onesr = T(sb("onesr", [8, 512], F32, sf))
                Fb8 = T(sb("Fb8", [128, S], BF16, sf))
                Gtok = T(sb("Gtok", [128, NT, 8], F32, sf))
                nbf = T(sb("nbf", [8, 2], F32, sf))
                etmp = T(sb("etmp", [8, 512], F32, sf))

                S_.op("pool", lambda e: e.memset(qz[0].t[:], 0.0), writes=[qz[0]])
                S_.op("pool", lambda e: e.memset(qz[1].t[:], 0.0), writes=[qz[1]])
                S_.op("pool", lambda e: e.memset(Fb8.t[:], 0.0), writes=[Fb8])
                S_.op("pool", lambda e: e.memset(kz[0].t[:], 0.0), writes=[kz[0]])
                S_.op("pool", lambda e: e.memset(kz[1].t[:], 0.0), writes=[kz[1]])
                S_.op("pool", lambda e: e.memset(kz[0].t[64:65, :], 1.0), writes=[kz[0]])
                S_.op("pool", lambda e: e.memset(kz[1].t[0:1, :], 1.0), writes=[kz[1]])
                S_.op("pool", lambda e: e.memset(onesr.t[:], 1.0), writes=[onesr])
                S_.dma("sp", lambda e: e.dma_start(out=nbf.t[:, 0:1], in_=bfg_d), writes=[nbf])
                S_.op("dve", lambda e: e.tensor_scalar_mul(out=nbf.t[:, 1:2], in0=nbf.t[:, 0:1], scalar1=-1.0),
                      reads=[nbf], writes=[nbf])

                wf = WT[1]
                S_.dma("pool", lambda e: e.dma_start(out=wf.t[:, :, 0:8], in_=wfa_d.rearrange("p (kc n) -> p kc n", n=8)), writes=[wf])
                for tc in range(NC4):
                    pb = pall.next()
                    for kc in range(KC):
                        S_.op("pe", lambda e, kc=kc, tc=tc, pb=pb: e.matmul(
                            pb.t[0:8, :], lhsT=wf.t[:, kc, 0:8], rhs=bufA[:, kc, tc * 512:(tc + 1) * 512],
                            start=(kc == 0), stop=(kc == KC - 1)), reads=[wf, bufA_d[tc]], writes=[pb], inc=(kc == KC - 1))
                    S_.op("act", lambda e, pb=pb: e.activation(out=etmp.t[:], in_=pb.t[0:8, :], func=AF.Exp, scale=-1.0,
                                                               bias=nbf.t[:, 1:2]), reads=[pb, nbf], writes=[etmp])
                    S_.op("act", lambda e, tc=tc: e.activation(out=spr.t[:, tc * 512:(tc + 1) * 512], in_=etmp.t[:], func=AF.Ln,
                                                               bias=1.0), reads=[etmp], writes=[spr])
                    if tc == 0:
                        S_.op("dve", lambda e: e.tensor_tensor_scan(out=Grow.t[:, 0:512], data0=onesr.t[:], data1=spr.t[:, 0:512],
                                                                    initial=0.0, op0=ALU.mult, op1=ALU.add),
                              reads=[onesr, spr], writes=[Grow])
                    else:
                        S_.op("dve", lambda e, tc=tc: e.tensor_tensor_scan(
                            out=Grow.t[:, tc * 512:(tc + 1) * 512], data0=onesr.t[:], data1=spr.t[:, tc * 512:(tc + 1) * 512],
                            initial=Grow.t[:, tc * 512 - 1:tc * 512], op0=ALU.mult, op1=ALU.add),
                            reads=[onesr, spr, Grow], writes=[Grow])
                S_.op("dve", lambda e: e.tensor_scalar_mul(out=Fb8.t[0:8, :], in0=Grow.t[:], scalar1=-8.0),
                      reads=[Grow], writes=[Fb8])
                pb = pall.next()
                for tt in range(NT):
                    S_.op("pe", lambda e, tt=tt, pb=pb: e.transpose(pb.t[:, tt * 8:(tt + 1) * 8], Grow.t[0:8, tt * 128:(tt + 1) * 128],
                                                                   ident_f.t[0:8, 0:8]),
                          reads=[Grow, ident_f], writes=[pb], inc=(tt == NT - 1))
                S_.op("dve", lambda e, pb=pb: e.tensor_copy(out=Gtok.t[:].rearrange("p a b -> p (a b)"), in_=pb.t[:, 0:128]),
                      reads=[pb], writes=[Gtok])
                dbg("Grow", [Grow], Grow.t[:], [8, S])

                wq = load_w(wqa_d, 512, tile=WT[0])
                wk = load_w(wka_d, 512, tile=WT[1])
                S_.dma("pool", lambda e: e.dma_start(out=wbrA.t[:], in_=wbra_d.rearrange("(pr p) n -> p pr n", p=128)), writes=[wbrA])

                for pr in range(4):
                    for tc in range(NC4):
                        pb = pall.next()
                        for kc in range(KC):
                            S_.op("pe", lambda e, kc=kc, tc=tc, pb=pb, pr=pr: e.matmul(
                                pb.t[:, :], lhsT=wq.t[:, kc, pr * 128:(pr + 1) * 128], rhs=bufA[:, kc, tc * 512:(tc + 1) * 512],
                                start=(kc == 0), stop=(kc == KC - 1)), reads=[wq, bufA_d[tc]], writes=[pb], inc=(kc == KC - 1))
                        for hp in range(2):
                            S_.op("act", lambda e, tc=tc, pb=pb, hp=hp: e.activation(
                                out=qz[hp].t[hp * 64:(hp + 1) * 64, tc * 512:(tc + 1) * 512], in_=pb.t[hp * 64:(hp + 1) * 64, :],
                                func=AF.Copy), reads=[pb], writes=[qz[hp]])
                        pb = pall.next()
                        for kc in range(KC):
                            S_.op("pe", lambda e, kc=kc, tc=tc, pb=pb, pr=pr: e.matmul(
                                pb.t[:, :], lhsT=wk.t[:, kc, pr * 128:(pr + 1) * 128], rhs=bufA[:, kc, tc * 512:(tc + 1) * 512],
                                start=(kc == 0), stop=(kc == KC - 1)), reads=[wk, bufA_d[tc]], writes=[pb], inc=(kc == KC - 1))
                        for hp in range(2):
                            S_.op("dve", lambda e, tc=tc, pb=pb, hp=hp: e.tensor_copy(
                                out=kz[hp].t[hp * 64:(hp + 1) * 64, tc * 512:(tc + 1) * 512], in_=pb.t[hp * 64:(hp + 1) * 64, :]),
                                reads=[pb], writes=[kz[hp]])
                        for hp in range(2):
                            h = pr * 2 + hp
                            ar = 64 if hp == 0 else 0
                            pb = pall.next()
                            S_.op("pe", lambda e, pb=pb, h=h, tc=tc: e.matmul(
                                pb.t[:, :], lhsT=selones.t[:, h * 128:(h + 1) * 128], rhs=Fb8.t[:, tc * 512:(tc + 1) * 512],
                                start=True, stop=True), reads=[selones, Fb8], writes=[pb])
                            S_.op("act", lambda e, pb=pb, hp=hp, ar=ar, tc=tc: e.activation(
                                out=qz[hp].t[ar:ar + 1, tc * 512:(tc + 1) * 512], in_=pb.t[ar:ar + 1, :], func=AF.Copy),
                                reads=[pb], writes=[qz[hp]])
                    if pr == 3:
                        wga_pre = [load_w(wga_d[:, 0:512], 512, tile=WT[0]), load_w(wga_d[:, 512:1024], 512, tile=WT[1])]
                    its = []
                    for hp in range(2):
                        for g in range(4):
                            for kb in range(4 * g + 4):
                                its.append(dict(hp=hp, g=g, kb=kb, nkb=4 * g + 4))

                    def qk(it):
                        hp, g, kb = it["hp"], it["g"], it["kb"]
                        if kb == 0:
                            it["acc"] = ACC.next()
                        else:
                            it["acc"] = it["prev"]["acc"]
                        c0 = max(0, kb - 4 * g) * 128
                        q0 = g * 512 + c0
                        st = STB.next()
                        it["st"], it["c0"] = st, c0
                        diag = kb >= 4 * g
                        S_.op("pe", lambda e: e.matmul(
                            st.t[:, c0:512], lhsT=kz[hp].t[:, kb * 128:(kb + 1) * 128], rhs=qz[hp].t[:, q0:(g + 1) * 512],
                            start=True, stop=(not diag)), reads=[kz[hp], qz[hp]], writes=[st], inc=(not diag))
                        if diag:
                            S_.op("pe", lambda e: e.matmul(
                                st.t[:, c0:c0 + 128], lhsT=ident_bf.t[:], rhs=maskX.t[:, 0:128], start=False, stop=True),
                                reads=[ident_bf, maskX], writes=[st], inc=True)

                    def ex_pv(it):
                        hp, g, kb, nkb = it["hp"], it["g"], it["kb"], it["nkb"]
                        h = pr * 2 + hp
                        st, c0 = it["st"], it["c0"]
                        num, den = it["acc"]
                        pt = PTb.next()
                        S_.op("act", lambda e: e.activation(
                            out=pt.t[:, c0:512], in_=st.t[:, c0:512], func=AF.Exp, scale=SCALE, bias=Gtok.t[:, kb, h:h + 1]),
                            reads=[st, Gtok], writes=[pt])
                        S_.op("pe", lambda e: e.matmul(
                            num.t[:, c0:512], lhsT=Vall.t[:, kb, pr * 128:(pr + 1) * 128], rhs=pt.t[:, c0:512],
                            start=(kb == 0), stop=(kb == nkb - 1), skip_group_check=True),
                            reads=[Vall, pt], writes=[num], inc=False)
                        S_.op("pe", lambda e: e.matmul(
                            den.t[:, c0:512], lhsT=ones_bf.t[:], rhs=pt.t[:, c0:512],
                            start=(kb == 0), stop=(kb == nkb - 1), skip_group_check=True),
                            reads=[ones_bf, pt], writes=[den], inc=True)
                        if kb == nkb - 1:
                            lanes = slice(hp * 64, (hp + 1) * 64)
                            rc = rec.next()
                            S_.op("dve", lambda e: e.reciprocal(out=rc.t[lanes, :], in_=den.t[lanes, :]), reads=[den], writes=[rc])
                            S_.op("dve", lambda e: e.tensor_tensor(
                                out=yaT[lanes, pr, g * 512:(g + 1) * 512], in0=num.t[lanes, :], in1=rc.t[lanes, :], op=ALU.mult),
                                reads=[num, rc], writes=[yaT_d[g]])

                    for i, it in enumerate(its):
                        it["prev"] = its[i - 1] if i > 0 else None
                    AH = 2
                    for i in range(AH):
                        qk(its[i])
                    for i, it in enumerate(its):
                        if i + AH < len(its):
                            qk(its[i + AH])
                        ex_pv(it)
                        if i == len(its) // 2 or i == len(its) - 1:
                            deferred.pop(0)(WT[2], STB)
                dbg("yaT", yaT_d, yaT[:], [128, 4, S], BF16)

                wbr = wbrA
                sg = Rot([T(sb("sgA%d" % i, [128, 512], F32, sf)) for i in range(2)])
                wqb_pre = load_w(wqb_d[:, 0:512], 512, tile=WT[2])
                for half in range(2):
                    wg = wga_pre[half]
                    if half == 1:
                        wkb_pre = load_w(wkb_d[:, 0:512], 512, tile=WT[0])
                    for j in range(4):
                        nch = half * 4 + j
                        for tc in range(NC4):
                            pg = pall.next()
                            for kc in range(KC):
                                S_.op("pe", lambda e, kc=kc, tc=tc, pg=pg, j=j, wg=wg: e.matmul(
                                    pg.t[:, :], lhsT=wg.t[:, kc, j * 128:(j + 1) * 128], rhs=bufA[:, kc, tc * 512:(tc + 1) * 512],
                                    start=(kc == 0), stop=(kc == KC - 1)), reads=[wg, bufA_d[tc]], writes=[pg], inc=(kc == KC - 1))
                            s_ = sg.next()
                            S_.op("act", lambda e, pg=pg, s_=s_: e.activation(out=s_.t[:], in_=pg.t[:, :], func=AF.Sigmoid),
                                  reads=[pg], writes=[s_])
                            pbr = pall.next()
                            for pr in range(4):
                                S_.op("pe", lambda e, pr=pr, tc=tc, pbr=pbr, nch=nch: e.matmul(
                                    pbr.t[:, :], lhsT=wbr.t[:, pr, nch * 128:(nch + 1) * 128], rhs=yaT[:, pr, tc * 512:(tc + 1) * 512],
                                    start=(pr == 0), stop=(pr == 3)), reads=[wbr, yaT_d[tc]], writes=[pbr], inc=(pr == 3))
                            S_.op("dve", lambda e, pbr=pbr, s_=s_, nch=nch, tc=tc: e.tensor_tensor(
                                out=bufB[:, nch, tc * 512:(tc + 1) * 512], in0=pbr.t[:, :], in1=s_.t[:], op=ALU.mult),
                                reads=[pbr, s_], writes=[bufB_d[tc]])
            sf.close()
            S_.barrier()
            dbg("mA", bufB_d, bufB[:], [128, KC, S], BF16)

            with ExitStack() as sd:
                ybT = sb("ybT", [128, 2, S], BF16, sd)
                ybT_d = [T() for _ in range(2)]
                wbrB = T(sb("wbrB", [128, 2, D], BF16, sd))
                S_.dma("pool", lambda e: e.dma_start(out=wbrB.t[:], in_=wbrb_d.rearrange("(pr p) n -> p pr n", p=128)), writes=[wbrB])
                sda = sd.enter_context(ExitStack())
                cosT = T(sb("cosT", [128, S], F32, sda))
                sinT = T(sb("sinT", [128, S], F32, sda))
                S_.dma("sp", lambda e: e.dma_start(out=cosT.t[:], in_=cos_d), writes=[cosT])
                S_.dma("sp", lambda e: e.dma_start(out=sinT.t[:], in_=sin_d), writes=[sinT])
                accN = T(sb("accN", [128, S], F32, sda))
                accD = T(sb("accD", [128, S], F32, sda))
                kp = T(sb("kpB", [128, S], BF16, sda))
                qz = [T(sb("qzB%d" % i, [128, S], BF16, sda)) for i in range(2)]
                Vp = T(sb("VpB", [128, NT, 128], BF16, sda))
                PTb = Rot([T(sb("PTbB%d" % i, [128, 512], BF16, sda)) for i in range(3)])
                qtmp = Rot([T(sb("qtmp%d" % i, [128, 512], BF16, sda)) for i in range(2)])
                t1r = Rot([T(sb("t1r%d" % i, [128, 512], F32, sda)) for i in range(2)])
                t2r = Rot([T(sb("t2r%d" % i, [128, 512], F32, sda)) for i in range(2)])
                S_.op("pool", lambda e: e.memset(qz[0].t[:], 0.0), writes=[qz[0]])
                S_.op("pool", lambda e: e.memset(qz[1].t[:], 0.0), writes=[qz[1]])
                wqb = [wqb_pre, None]
                wkb = [wkb_pre, None]
                wvb = [load_w(wvb_d[:, 0:512], 512, tile=WT[1]), None]
                WX = [T(sb("wx%d" % i, [128, KC, 256], BF16, sda)) for i in range(3)]
                for i, src in enumerate((wqb_d, wkb_d, wvb_d)):
                    S_.dma("pool", lambda e, i=i, src=src: e.dma_start(
                        out=WX[i].t[:], in_=src[:, 512:768].rearrange("(kc p) n -> p kc n", p=128)), writes=[WX[i]])

                def wsel(main, extra, pc):
                    if pc < 4:
                        return main, main.t, pc * 128
                    return extra, extra.t, (pc - 4) * 128

                chk("c1")
                def make_step(m, g, qz, kp):
                    d = DIL[g]
                    pc = 2 * g + m
                    nbk = 16 // d
                    wq_t, wq_ap, wq_c = wsel(wqb[0], WX[0], pc)
                    wk_t, wk_ap, wk_c = wsel(wkb[0], WX[1], pc)
                    wv_t, wv_ap, wv_c = wsel(wvb[0], WX[2], pc)
                    def f_proj():
                        pend = []
                        for which in range(2):
                            w_t, w_ap, w_c = (wq_t, wq_ap, wq_c) if which == 0 else (wk_t, wk_ap, wk_c)
                            for j in range(4):
                                pq = pall.next()
                                for kc in range(KC):
                                    S_.op("pe", lambda e, kc=kc, j=j, pq=pq, w_ap=w_ap, w_c=w_c, g=g: e.matmul(
                                        pq.t[:, :], lhsT=w_ap[:, kc, w_c:w_c + 128], rhs=bufA[:, kc, j * 512:(j + 1) * 512],
                                        start=(kc == 0), stop=(kc == KC - 1)), reads=[w_t, bufA_d[j]], writes=[pq], inc=(kc == KC - 1))
                                chk("c2")
                                qt = qtmp.next()
                                S_.op("act", lambda e, pq=pq, qt=qt: e.activation(out=qt.t[:], in_=pq.t[:, :], func=AF.Copy),
                                      reads=[pq], writes=[qt])
                                def post(pq=pq, qt=qt, j=j, which=which):
                                    psw = pall.next()
                                    S_.op("pe", lambda e, psw=psw, qt=qt: e.matmul(psw.t[:, :], lhsT=pm_bf.t[:], rhs=qt.t[:], start=True, stop=True),
                                          reads=[pm_bf, qt], writes=[psw])
                                    chk("c4")
                                    t1 = t1r.next()
                                    t2 = t2r.next()
                                    S_.op("dve", lambda e, pq=pq, t1=t1, g=g, j=j: e.tensor_tensor(
                                        out=t1.t[:], in0=pq.t[:, :], in1=cosT.t[:, j * 512:(j + 1) * 512], op=ALU.mult),
                                        reads=[pq, cosT], writes=[t1])
                                    chk("c4b")
                                    S_.op("dve", lambda e, psw=psw, t2=t2, g=g, j=j: e.tensor_tensor(
                                        out=t2.t[:], in0=psw.t[:, :], in1=sinT.t[:, j * 512:(j + 1) * 512], op=ALU.mult),
                                        reads=[psw, sinT], writes=[t2])
                                    chk("c5")
                                    if which == 0:
                                        for hp in range(2):
                                            S_.op("pool", lambda e, t1=t1, t2=t2, hp=hp, j=j: e.tensor_tensor(
                                                out=cls_out(qz[hp].t[hp * 64:(hp + 1) * 64, :], g, j), in0=nat_in(t1.t[hp * 64:(hp + 1) * 64, :], g),
                                                in1=nat_in(t2.t[hp * 64:(hp + 1) * 64, :], g), op=ALU.add), reads=[t1, t2], writes=[qz[hp]])
                                    else:
                                        S_.op("pool", lambda e, t1=t1, t2=t2, j=j: e.tensor_tensor(
                                            out=cls_out(kp.t[:, :], g, j), in0=nat_in(t1.t[:], g), in1=nat_in(t2.t[:], g), op=ALU.add),
                                            reads=[t1, t2], writes=[kp])
                                pend.append(post)
                                if len(pend) > 1:
                                    pend.pop(0)()
                        while pend:
                            pend.pop(0)()
                        chk("dq%d%d" % (m, g))
                    def f_v():
                        for cb4 in range(4):
                            pv = pall.next()
                            for i in range(4):
                                cb = cb4 * 4 + i
                                r, n_ = cb // nbk, cb % nbk
                                t0 = d * 128 * n_ + r
                                for kc in range(KC):
                                    S_.op("pe", lambda e, kc=kc, i=i, pv=pv, t0=t0, d=d, wv_ap=wv_ap, wv_c=wv_c: e.matmul(
                                        pv.t[:, i * 128:(i + 1) * 128], lhsT=bufA[:, kc, t0:t0 + 127 * d + 1:d], rhs=wv_ap[:, kc, wv_c:wv_c + 128],
                                        start=(kc == 0), stop=(kc == KC - 1)), reads=[wv_t] + bufA_d, writes=[pv],
                                        inc=(kc == KC - 1 and i == 3))
                            S_.op("act", lambda e, pv=pv, cb4=cb4: e.activation(
                                out=Vp.t[:, cb4 * 4:(cb4 + 1) * 4, :].rearrange("p a b -> p (a b)"), in_=pv.t[:, :], func=AF.Copy),
                                reads=[pv], writes=[Vp])
                        chk("dv%d%d" % (m, g))
                    def f_att():
                        units = []
                        for hp in range(2):
                            if g < 2:
                                for cb in range(16):
                                    hasnext = (cb % nbk) < nbk - 1
                                    units.append(dict(hp=hp, kbs=[cb], qcols=(cb * 128, (cb + (2 if hasnext else 1)) * 128),
                                                      mask0=0))
                            else:
                                for j in range(4):
                                    units.append(dict(hp=hp, kbs=[4 * j + i for i in range(4)], qcols=(j * 512, (j + 1) * 512),
                                                      mask0=256))
                        accs = {}

                        def acc_of(hp, j):
                            if (hp, j) not in accs:
                                accs[(hp, j)] = ACC.next()
                            return accs[(hp, j)]

                        def d_qk(u):
                            hp = u["hp"]
                            st = STB.next()
                            u["st"] = st
                            q0, q1 = u["qcols"]
                            n = q1 - q0
                            u["n"] = n
                            if g < 2:
                                cb = u["kbs"][0]
                                S_.op("pe", lambda e: e.matmul(
                                    st.t[:, 0:n], lhsT=kp.t[:, cb * 128:(cb + 1) * 128], rhs=qz[hp].t[:, q0:q1],
                                    start=True, stop=False), reads=[kp, qz[hp]], writes=[st], inc=False)
                            else:
                                for i, cb in enumerate(u["kbs"]):
                                    S_.op("pe", lambda e: e.matmul(
                                        st.t[:, i * 128:(i + 1) * 128], lhsT=kp.t[:, cb * 128:(cb + 1) * 128],
                                        rhs=qz[hp].t[:, cb * 128:(cb + 1) * 128], start=(i == 0), stop=False),
                                        reads=[kp, qz[hp]], writes=[st], inc=False)
                            m0 = u["mask0"]
                            S_.op("pe", lambda e: e.matmul(
                                st.t[:, 0:n], lhsT=ident_bf.t[:], rhs=maskX.t[:, m0:m0 + n], start=False, stop=True),
                                reads=[ident_bf, maskX], writes=[st], inc=True)

                        def d_pv(u):
                            hp, st, n = u["hp"], u["st"], u["n"]
                            lanes = slice(hp * 64, (hp + 1) * 64)
                            pt = PTb.next()
                            S_.op("act", lambda e: e.activation(out=pt.t[:, 0:n], in_=st.t[:, 0:n], func=AF.Exp, scale=SCALE),
                                  reads=[st], writes=[pt])
                            contribs = []
                            if g < 2:
                                cb = u["kbs"][0]
                                contribs.append((cb, 0, cb, (cb % nbk) == 0))
                                if n == 256:
                                    contribs.append((cb, 128, cb + 1, True))
                            else:
                                for i, cb in enumerate(u["kbs"]):
                                    contribs.append((cb, i * 128, cb, True))
                            for (kb_, pc, qb_, first) in contribs:
                                num, den = acc_of(hp, qb_ // 4)
                                cols = slice((qb_ % 4) * 128, (qb_ % 4 + 1) * 128)
                                last = (kb_ == qb_)
                                S_.op("pe", lambda e: e.matmul(
                                    num.t[:, cols], lhsT=Vp.t[:, kb_, :], rhs=pt.t[:, pc:pc + 128], start=first, stop=last),
                                    reads=[Vp, pt], writes=[num], inc=False)
                                S_.op("pe", lambda e: e.matmul(
                                    den.t[:, cols], lhsT=ones_bf.t[:], rhs=pt.t[:, pc:pc + 128], start=first, stop=last),
                                    reads=[ones_bf, pt], writes=[den], inc=True)
                                if last and qb_ % 4 == 3:
                                    j = qb_ // 4
                                    for (acc, src) in ((accN, num), (accD, den)):
                                        if g == 0:
                                            S_.op("dve", lambda e: e.tensor_copy(
                                                out=tok_view(acc.t[lanes, :], g, j), in_=chunk_view(src.t[lanes, :], g)),
                                                reads=[src], writes=[acc])
                                        else:
                                            S_.op("dve", lambda e: e.tensor_tensor(
                                                out=tok_view(acc.t[lanes, :], g, j), in0=chunk_view(src.t[lanes, :], g),
                                                in1=tok_view(acc.t[lanes, :], g, j), op=ALU.add), reads=[src, acc], writes=[acc])

                        AHEAD = 2
                        for i in range(min(AHEAD, len(units))):
                            d_qk(units[i])
                        for i, u in enumerate(units):
                            if i + AHEAD < len(units):
                                d_qk(units[i + AHEAD])
                            d_pv(u)
                        chk("da%d%d" % (m, g))
                    return f_proj, f_v, f_att

                def finalize(m):
                    S_.op("dve", lambda e: e.reciprocal(out=accD.t[:], in_=accD.t[:]), reads=[accD], writes=[accD])
                    S_.op("dve", lambda e, m=m: e.tensor_tensor(out=ybT[:, m, :], in0=accN.t[:], in1=accD.t[:], op=ALU.mult),
                          reads=[accN, accD], writes=[ybT_d[m]])

                qzs = [qz, [T(sb("qzC%d" % i, [128, S], BF16, sda)) for i in range(2)]]
                kps = [kp, T(sb("kpC", [128, S], BF16, sda))]
                S_.op("pool", lambda e: e.memset(qzs[1][0].t[:], 0.0), writes=[qzs[1][0]])
                S_.op("pool", lambda e: e.memset(qzs[1][1].t[:], 0.0), writes=[qzs[1][1]])
                steps = [make_step(m_, g_, qzs[(m_ * 3 + g_) % 2], kps[(m_ * 3 + g_) % 2]) for m_ in range(2) for g_ in range(3)]
                steps[0][0]()
                steps[0][1]()
                for s_i in range(6):
                    if s_i + 1 < 6:
                        steps[s_i + 1][0]()
                    if s_i == 4:
                        wgb_pre = [load_w(wgb_d[:, 0:512], 512, tile=wqb[0]), load_w(wgb_d[:, 512:1024], 512, tile=wkb[0])]
                    if s_i == 5:
                        wo_pre0 = load_w(wout_d[:, 0:512], 512, tile=wvb[0])
                    steps[s_i][2]()
                    if s_i % 3 == 2:
                        finalize(s_i // 3)
                    if s_i + 1 < 6:
                        steps[s_i + 1][1]()
                sda.close()
                S_.barrier()
                dbg("ybT", ybT_d, ybT[:], [128, 2, S], BF16)

                wbr = wbrB
                sg = Rot([T(sb("sgB%d" % i, [128, 512], F32, sd)) for i in range(2)])
                tm = Rot([T(sb("tmB%d" % i, [128, 512], F32, sd)) for i in range(2)])
                S_.op("dve", lambda e: e.tensor_tensor(
                    out=wo_pre0.t[:], in0=wo_pre0.t[:],
                    in1=gaBCm.t[:, 0:512].unsqueeze(1).to_broadcast([128, KC, 512]), op=ALU.mult),
                    reads=[wo_pre0, gaBCm], writes=[wo_pre0])
                for half in range(2):
                    wg = wgb_pre[half]
                    if half == 1:
                        wo_pre1 = load_w(wout_d[:, 512:1024], 512, tile=wgb_pre[0])
                    for j in range(4):
                        nch = half * 4 + j
                        for tc in range(NC4):
                            pg = pall.next()
                            for kc in range(KC):
                                S_.op("pe", lambda e, kc=kc, tc=tc, pg=pg, j=j, wg=wg: e.matmul(
                                    pg.t[:, :], lhsT=wg.t[:, kc, j * 128:(j + 1) * 128], rhs=bufA[:, kc, tc * 512:(tc + 1) * 512],
                                    start=(kc == 0), stop=(kc == KC - 1)), reads=[wg, bufA_d[tc]], writes=[pg], inc=(kc == KC - 1))
                            s_ = sg.next()
                            S_.op("act", lambda e, pg=pg, s_=s_: e.activation(out=s_.t[:], in_=pg.t[:, :], func=AF.Sigmoid),
                                  reads=[pg], writes=[s_])
                            pbr = pall.next()
                            for pr in range(2):
                                S_.op("pe", lambda e, pr=pr, tc=tc, pbr=pbr, nch=nch: e.matmul(
                                    pbr.t[:, :], lhsT=wbr.t[:, pr, nch * 128:(nch + 1) * 128], rhs=ybT[:, pr, tc * 512:(tc + 1) * 512],
                                    start=(pr == 0), stop=(pr == 1)), reads=[wbr] + ybT_d, writes=[pbr], inc=(pr == 1))
                            t_ = tm.next()
                            S_.op("dve", lambda e, pbr=pbr, s_=s_, t_=t_: e.tensor_tensor(
                                out=t_.t[:], in0=pbr.t[:, :], in1=s_.t[:], op=ALU.mult), reads=[pbr, s_], writes=[t_])
                            S_.op("pool", lambda e, t_=t_, nch=nch, tc=tc: e.tensor_tensor(
                                out=bufB[:, nch, tc * 512:(tc + 1) * 512], in0=bufB[:, nch, tc * 512:(tc + 1) * 512], in1=t_.t[:], op=ALU.add),
                                reads=[t_, bufB_d[tc]], writes=[bufB_d[tc]])
                S_.op("dve", lambda e: e.tensor_tensor(
                    out=wo_pre1.t[:], in0=wo_pre1.t[:],
                    in1=gaBCm.t[:, 512:1024].unsqueeze(1).to_broadcast([128, KC, 512]), op=ALU.mult),
                    reads=[wo_pre1, gaBCm], writes=[wo_pre1])
            S_.barrier()
            dbg("merged", bufB_d, bufB[:], [128, KC, S], BF16)

            with ExitStack() as s2:
                x1 = sb("x1", [128, NT, D], F32, s2)
                x1_d = [T() for _ in range(NT)]
                WT4 = T(sb("wt4", [128, KC, 512], BF16, s2))
                for tt in range(NT):
                    S_.dma("sp", lambda e, tt=tt: e.dma_start(out=x1[:, tt, :], in_=x_d[tt * 128:(tt + 1) * 128, :]), writes=[x1_d[tt]])
                so = s2.enter_context(ExitStack())
                tmo = Rot([T(sb("tmo%d" % i, [128, 512], F32, so)) for i in range(3)])
                wo = [wo_pre0, wo_pre1]
                others = [t for t in WT if t is not wo_pre0 and t is not wo_pre1]
                pre_w = [load_w(wfg_d[:, 0:512], 512, tile=others[0]), load_w(wfu_d[:, 0:512], 512, tile=WT4)]
                wrot = Rot([wo_pre0, wo_pre1, others[0], WT4])

                def outproj_tile(tt):
                    for ch in range(2):
                        po = pall.next()
                        for kc in range(KC):
                            S_.op("pe", lambda e: e.matmul(
                                po.t[:, :], lhsT=bufB[:, kc, tt * 128:(tt + 1) * 128], rhs=wo[ch].t[:, kc, :],
                                start=(kc == 0), stop=(kc == KC - 1)), reads=[wo[ch], bufB_d[tt // 4]], writes=[po], inc=(kc == KC - 1))
                        S_.op("dve", lambda e: e.tensor_tensor(
                            out=x1[:, tt, ch * 512:(ch + 1) * 512], in0=po.t[:, :], in1=x1[:, tt, ch * 512:(ch + 1) * 512], op=ALU.add),
                            reads=[po, x1_d[tt]], writes=[x1_d[tt]])
                    return x1_d[tt], x1[:, tt, :]

                with ExitStack() as sn:
                    norm_to_T(outproj_tile, 1, bufA, bufA_d, 16, 8, sn)
                so.close()
                S_.barrier()
                dbg("x1", x1_d, x1[:], [128, NT, D])
                dbg("h2T", bufA_d, bufA[:], [128, KC, S], BF16)
                sf2 = s2.enter_context(ExitStack())
                tmo = Rot([T(sb("tmf%d" % i, [128, 512], F32, sf2)) for i in range(3)])

                wdt = [T(sb("wd%d" % i, [128, KC, D], BF16, sf2)) for i in range(1)]
                sa = Rot([T(sb("sa%d" % i, [128, 512], F32, sf2)) for i in range(2)])
                groups = [(0, 8), (8, 8), (16, 6)]
                sqf = T(sb("sqF", [128, NT], F32, sf2))

                def final_tile(tt):
                    col = 2 * NT + tt
                    jk = sa.next()
                    S_.op("act", lambda e: e.activation(out=jk.t[:].bitcast(BF16), in_=x1[:, tt, :], func=AF.Square,
                                                        accum_out=ssq.t[:, col:col + 1]), reads=[x1_d[tt]], writes=[jk, ssq])
                    S_.op("act", lambda e: e.activation(out=sqf.t[:, tt:tt + 1], in_=ssq.t[:, col:col + 1], func=AF.Sqrt,
                                                        scale=1.0 / D, bias=EPS), reads=[ssq], writes=[sqf])
                    S_.op("dve", lambda e: e.reciprocal(out=rstd.t[:, col:col + 1], in_=sqf.t[:, tt:tt + 1]), reads=[sqf], writes=[rstd])
                    for ch in range(2):
                        y = tmo.next()
                        S_.op("act", lambda e: e.activation(out=y.t[:], in_=x1[:, tt, ch * 512:(ch + 1) * 512], func=AF.Identity,
                                                            scale=rstd.t[:, col:col + 1]), reads=[x1_d[tt], rstd], writes=[y])
                        S_.op("dve", lambda e: e.tensor_tensor(out=y.t[:], in0=y.t[:], in1=gfinBC.t[:, ch * 512:(ch + 1) * 512],
                                                               op=ALU.mult), reads=[y, gfinBC], writes=[y])
                        S_.dma("sp", lambda e: e.dma_start(out=out_d[tt * 128:(tt + 1) * 128, ch * 512:(ch + 1) * 512], in_=y.t[:]),
                               reads=[y])
                for (f0, nf) in groups:
                    wd = wdt[0]
                    for q4 in range(0, nf, 4):
                        nq = min(4, nf - q4)
                        c0 = (f0 + q4) * 128
                        if pre_w:
                            wg, wu = pre_w
                            pre_w = None
                        else:
                            wg = load_w(wfg_d[:, c0:c0 + nq * 128], nq * 128)
                            wu = load_w(wfu_d[:, c0:c0 + nq * 128], nq * 128)
                        if q4 == 0:
                            S_.dma("pool", lambda e, f0=f0, nf=nf, wd=wd: e.dma_start(
                                out=wd.t[:, 0:nf, :], in_=wfd_d[f0 * 128:(f0 + nf) * 128, :].rearrange("(kc p) n -> p kc n", p=128)),
                                writes=[wd])
                            S_.op("pool", lambda e, nf=nf, wd=wd: e.tensor_tensor(
                                out=wd.t[:, 0:nf, :], in0=wd.t[:, 0:nf, :],
                                in1=gaBCf.t[:, :].unsqueeze(1).to_broadcast([128, nf, D]), op=ALU.mult),
                                reads=[wd, gaBCf], writes=[wd])
                        for jj in range(nq):
                            fl = q4 + jj
                            for tc in range(NC4):
                                pa = pall.next()
                                for kc in range(KC):
                                    S_.op("pe", lambda e, kc=kc, tc=tc, pa=pa, jj=jj, wg=wg: e.matmul(
                                        pa.t[:, :], lhsT=wg.t[:, kc, jj * 128:(jj + 1) * 128], rhs=bufA[:, kc, tc * 512:(tc + 1) * 512],
                                        start=(kc == 0), stop=(kc == KC - 1)), reads=[wg, bufA_d[tc]], writes=[pa], inc=(kc == KC - 1))
                                pu = pall.next()
                                for kc in range(KC):
                                    S_.op("pe", lambda e, kc=kc, tc=tc, pu=pu, jj=jj, wu=wu: e.matmul(
                                        pu.t[:, :], lhsT=wu.t[:, kc, jj * 128:(jj + 1) * 128], rhs=bufA[:, kc, tc * 512:(tc + 1) * 512],
                                        start=(kc == 0), stop=(kc == KC - 1)), reads=[wu, bufA_d[tc]], writes=[pu], inc=(kc == KC - 1))
                                s_ = sa.next()
                                S_.op("act", lambda e, pa=pa, s_=s_: e.activation(out=s_.t[:], in_=pa.t[:, :], func=AF.Silu),
                                      reads=[pa], writes=[s_])
                                S_.op("dve", lambda e, pu=pu, s_=s_, fl=fl, tc=tc: e.tensor_tensor(
                                    out=bufB[:, fl, tc * 512:(tc + 1) * 512], in0=pu.t[:, :], in1=s_.t[:], op=ALU.mult),
                                    reads=[pu, s_], writes=[bufB_d[tc]])
                    for tt in range(NT):
                        for ch in range(2):
                            po = pall.next()
                            for kc in range(nf):
                                S_.op("pe", lambda e, kc=kc, tt=tt, ch=ch, po=po, nf=nf, wd=wd: e.matmul(
                                    po.t[:, :], lhsT=bufB[:, kc, tt * 128:(tt + 1) * 128], rhs=wd.t[:, kc, ch * 512:(ch + 1) * 512],
                                    start=(kc == 0), stop=(kc == nf - 1)), reads=[wd, bufB_d[tt // 4]], writes=[po], inc=(kc == nf - 1))
                            S_.op("dve", lambda e, po=po, tt=tt, ch=ch: e.tensor_tensor(
                                out=x1[:, tt, ch * 512:(ch + 1) * 512], in0=po.t[:, :], in1=x1[:, tt, ch * 512:(ch + 1) * 512], op=ALU.add),
                                reads=[po, x1_d[tt]], writes=[x1_d[tt]])
                        if f0 + nf == NFF:
                            if tt >= 2:
                                final_tile(tt - 2)
                            if tt == NT - 1:
                                final_tile(NT - 2)
                                final_tile(NT - 1)

                dbg("x2", x1_d, x1[:], [128, NT, D])
                sf2.close()
        except _Stop:
            pass
        S_.finish()
        S_.emit()
    return nc, dbg_d


def _consts():
    ident = np.eye(128, dtype=np.float32)
    k = np.arange(128)[:, None]
    q = np.arange(128)[None, :]
    anti = np.where(k >= q, 0.0, NEG).astype(np.float32)
    caus = np.where(k <= q, 0.0, NEG).astype(np.float32)
    mask = np.concatenate([caus, anti, caus, caus, caus, caus], axis=1)
    sel = np.zeros((128, 8 * 128), np.float32)
    for h in range(8):
        sel[h, h * 128:(h + 1) * 128] = 1.0
    pm = np.zeros((128, 128), np.float32)
    cosf = np.ones((128, S), np.float32)
    sinf = np.zeros((128, S), np.float32)
    pos = np.arange(S, dtype=np.float32)
    inv_freq = (np.float32(500000.0) ** (-(np.arange(0, 16, 2, dtype=np.float32)) / np.float32(16))).astype(np.float32)
    ang = (pos[:, None] * inv_freq[None, :]).astype(np.float32)
    cs = np.cos(ang).astype(np.float32).T
    sn = np.sin(ang).astype(np.float32).T
    for hp in range(2):
        for dd in range(16):
            p = hp * 64 + dd
            i = dd % 8
            partner = p + 8 if dd < 8 else p - 8
            pm[partner, p] = 1.0
            cosf[p] = cs[i]
            sinf[p] = -sn[i] if dd < 8 else sn[i]
    ones = np.ones((128, 128), np.float32)
    return dict(k_ident=ident, k_mask=mask, k_sel=sel, k_pm=pm, k_cos=cosf, k_sin=sinf, k_ones=ones)


def _col(v, n):
    return np.ascontiguousarray(np.asarray(v, np.float32).reshape(n, 128).T)


def _prep_inputs(x, c, w_ada, b_ada, g_mix, w_in, b_fgate, w_br_a, w_br_b, w_out,
                 g_ffn, w_ffn_gate, w_ffn_up, w_ffn_down, g_final):
    f = lambda a: np.ascontiguousarray(np.asarray(a, dtype=np.float32))
    x, c = f(x), f(c)
    w_in0 = f(w_in)[0]
    cuts = np.cumsum([512, 512, 512, 8, 768, 768, 768, 1024, 1024])[:-1]
    qa, ka, va, fa, qb, kb, vb, ga, gb = [np.ascontiguousarray(p) for p in np.split(w_in0, cuts, axis=1)]
    b_ada0 = f(b_ada)[0]
    shared = dict(
        w_ada=f(w_ada)[0], b_ada_col=_col(b_ada0, 48),
        b_gam_bc=np.ascontiguousarray(np.broadcast_to(b_ada0[2 * D:3 * D], (128, D))),
        b_gaf_bc=np.ascontiguousarray(np.broadcast_to(b_ada0[5 * D:6 * D], (128, D))),
        g_mix_col=_col(f(g_mix)[0], KC), g_ffn_col=_col(f(g_ffn)[0], KC),
        g_fin_bc=np.ascontiguousarray(np.broadcast_to(f(g_final), (128, D))),
        b_fg_col=np.ascontiguousarray(f(b_fgate)[0].reshape(8, 1)),
        w_qa=qa, w_ka=ka, w_va=va, w_fa=np.ascontiguousarray(fa.reshape(KC, 128, 8).transpose(1, 0, 2).reshape(128, KC * 8)), w_qb=qb, w_kb=kb, w_vb=vb, w_ga=ga, w_gb=gb,
        w_br_a=f(w_br_a)[0], w_br_b=f(w_br_b)[0], w_out=f(w_out)[0],
        w_ffn_gate=f(w_ffn_gate)[0], w_ffn_up=f(w_ffn_up)[0], w_ffn_down=f(w_ffn_down)[0],
    )
    shared.update(_consts())
    in_maps = []
    for b in range(8):
        m = dict(shared)
        m["x"] = np.ascontiguousarray(x[b])
        m["c_col"] = _col(c[b], KC)
        in_maps.append(m)
    return in_maps


_NC_CACHE = {}


def kernel(**inputs):
    in_maps = _prep_inputs(**inputs)
    if "nc" not in _NC_CACHE:
        _NC_CACHE["nc"] = build_program()[0]
    nc = _NC_CACHE["nc"]
    res = run_bass_kernel_spmd(nc, in_maps, core_ids=list(range(8)))
    out = np.stack([np.asarray(r["out"], dtype=np.float32).reshape(S, D) for r in res.results], axis=0)
    return out
```

```python
import numpy as np
from contextlib import ExitStack
import concourse.bass as bass
import concourse.mybir as mybir
from concourse.bass_utils import run_bass_kernel_spmd

F32 = mybir.dt.float32
BF16 = mybir.dt.bfloat16
AF = mybir.ActivationFunctionType
ALU = mybir.AluOpType

D = 1024
S = 2048
NT = 16
NC4 = 4
KC = 8
DFF = 2816
NFF = 22
EPS = 1e-6
NEG = -30000.0
SCALE = 0.125
DIL = (1, 4, 16)
ENG = ("pe", "act", "dve", "pool", "sp")


class T:
    __slots__ = ("t", "w", "r", "ds", "excl")

    def __init__(self, t=None, excl=False):
        self.t = t
        self.excl = excl
        self.w = None
        self.r = {}
        self.ds = None


class Rec:
    def __init__(self):
        self.call = None

    def __getattr__(self, name):
        def f(*a, **k):
            self.call = (name, a, k)
        return f


def _record(fn):
    r = Rec()
    fn(r)
    assert r.call is not None
    return r.call


class Sched:
    def __init__(self, nc, es, n_dma=24):
        self.nc = nc
        self.sem = {e: es.enter_context(nc.semaphore("s_" + e)) for e in ENG}
        self.cnt = {e: 0 for e in ENG}
        self.es = es
        self.dsem = []
        self.dcnt = []
        self.dq = []
        self.inflight = {"pool": [], "sp": []}
        self.max_inflight = {"pool": 2, "sp": 4}
        self.streams = {e: [] for e in ENG}
        self.waited = {e: {} for e in ENG}
        self.stopped = False

    def _need(self, eng, deps):
        for key, val in deps:
            if key == eng and eng == "pe":
                continue
            if self.waited[eng].get(key, 0) >= val:
                continue
            self.waited[eng][key] = val
            self.streams[eng].append(("w", key, val))

    @staticmethod
    def _deps(reads, writes, eng=None):
        deps = []
        for t in reads:
            if t.w is not None:
                deps.append(t.w)
            if t.excl:
                deps.extend((k, v) for k, v in t.r.items() if k != eng)
        for t in writes:
            if t.w is not None:
                deps.append(t.w)
            deps.extend(t.r.items())
        return deps

    def op(self, eng, fn, reads=(), writes=(), inc=True):
        if self.stopped:
            return
        self._need(eng, self._deps(reads, writes, eng))
        val = self.cnt[eng] + 1
        if inc:
            self.cnt[eng] = val
        self.streams[eng].append(("o", _record(fn), inc))
        for t in reads:
            if t.r.get(eng, 0) < val:
                t.r[eng] = val
        for t in writes:
            t.w = (eng, val)
            t.r = {}

    def dma(self, q, fn, reads=(), writes=()):
        if self.stopped:
            return
        own = writes[0] if len(writes) else reads[0]
        if own.ds is None:
            own.ds = len(self.dsem)
            self.dsem.append(self.es.enter_context(self.nc.semaphore("d%d" % own.ds)))
            self.dcnt.append(0)
            self.dq.append(q)
        i = own.ds
        assert self.dq[i] == q
        key = ("d", i)
        deps = self._deps(reads, writes)
        if self.dcnt[i] > 0:
            deps.append((key, self.dcnt[i]))
        fl = self.inflight[q]
        while len(fl) >= self.max_inflight[q]:
            deps.append(fl.pop(0))
        self._need(q, deps)
        self.dcnt[i] += 16
        val = self.dcnt[i]
        fl.append((key, val))
        self.streams[q].append(("d", _record(fn), i))
        for t in reads:
            t.r[key] = val
        for t in writes:
            t.w = (key, val)
            t.r = {}

    def barrier(self):
        if self.stopped:
            return
        edeps = [(e, self.cnt[e]) for e in ENG if self.cnt[e] > 0]
        for q in ("sp", "pool"):
            ddeps = [(("d", i), v) for i, v in enumerate(self.dcnt) if v > 0 and self.dq[i] == q]
            self._need(q, edeps + ddeps)
            self.cnt[q] += 1
            self.streams[q].append(("o", ("nop", (), {}), True))
        edeps = [(e, self.cnt[e]) for e in ENG if self.cnt[e] > 0]
        for e in ENG:
            self._need(e, edeps)

    def finish(self):
        ddeps = [(("d", i), v) for i, v in enumerate(self.dcnt) if v > 0 and self.dq[i] == "pool"]
        if ddeps:
            self._need("pool", ddeps)
            self.cnt["pool"] += 1
            self.streams["pool"].append(("o", ("nop", (), {}), True))
        for i, v in enumerate(self.dcnt):
            if v > 0 and self.dq[i] == "sp":
                self._need("sp", [(("d", i), v)])
        for e in ENG:
            self._need("sp", [(e, self.cnt[e])] if self.cnt[e] > 0 else [])

    def emit(self):
        nc = self.nc
        needed = {e: set() for e in ENG}
        for e in ENG:
            for item in self.streams[e]:
                if item[0] == "w" and isinstance(item[1], str):
                    needed[item[1]].add(item[2])
        rank = {e: {v: i + 1 for i, v in enumerate(sorted(needed[e]))} for e in ENG}
        with nc.Block() as block:
            def mk(e):
                def body(eng):
                    prov = 0
                    for item in self.streams[e]:
                        if item[0] == "w":
                            key = item[1]
                            if isinstance(key, str):
                                eng.wait_ge(self.sem[key], rank[key][item[2]])
                            else:
                                eng.wait_ge(self.dsem[key[1]], item[2])
                        elif item[0] == "o":
                            c = item[1]
                            ins = getattr(eng, c[0])(*c[1], **c[2])
                            if item[2]:
                                prov += 1
                                if prov in needed[e]:
                                    ins.then_inc(self.sem[e], 1)
                        else:
                            c = item[1]
                            getattr(eng, c[0])(*c[1], **c[2]).then_inc(self.dsem[item[2]], 16)
                    assert prov == self.cnt[e], (e, prov, self.cnt[e])
                return body
            block.tensor(mk("pe"))
            block.scalar(mk("act"))
            block.vector(mk("dve"))
            block.gpsimd(mk("pool"))
            block.sync(mk("sp"))


class Rot:
    def __init__(self, items):
        self.items = items
        self.i = 0

    def next(self):
        x = self.items[self.i]
        self.i = (self.i + 1) % len(self.items)
        return x


def tok_view(ap2d, g, j):
    if g == 0:
        return ap2d[:, j * 512:(j + 1) * 512]
    if g == 1:
        return ap2d.rearrange("p (i r) -> p r i", r=4)[:, j, :]
    return ap2d.rearrange("p (i r) -> p r i", r=16)[:, 4 * j:4 * j + 4, :]


def cls_out(ap2d, g, tc):
    dd = DIL[g]
    if g == 0:
        return ap2d[:, tc * 512:(tc + 1) * 512]
    n = 512 // dd
    return ap2d.rearrange("p (r i) -> p r i", r=dd)[:, :, n * tc:n * (tc + 1)]


def nat_in(ap2d, g):
    if g == 0:
        return ap2d
    return ap2d.rearrange("p (i r) -> p r i", r=DIL[g])


def chunk_view(ap2d, g):
    if g == 2:
        return ap2d.rearrange("p (a b) -> p a b", a=4)
    return ap2d


class _Stop(Exception):
    pass


def build_program(debug=(), stop=None):
    nc = bass.Bass("TRN2", target_bir_lowering=False)

    def din(name, shape):
        return nc.dram_tensor(name, list(shape), F32, kind="ExternalInput").ap()

    x_d = din("x", [S, D])
    ccol_d = din("c_col", [128, KC])
    wada_d = din("w_ada", [D, 6 * D])
    badac_d = din("b_ada_col", [128, 48])
    bgam_d = din("b_gam_bc", [128, D])
    bgaf_d = din("b_gaf_bc", [128, D])
    gmix_d = din("g_mix_col", [128, KC])
    gffn_d = din("g_ffn_col", [128, KC])
    gfin_d = din("g_fin_bc", [128, D])
    bfg_d = din("b_fg_col", [8, 1])
    wqa_d = din("w_qa", [D, 512])
    wka_d = din("w_ka", [D, 512])
    wva_d = din("w_va", [D, 512])
    wfa_d = din("w_fa", [128, KC * 8])
    wqb_d = din("w_qb", [D, 768])
    wkb_d = din("w_kb", [D, 768])
    wvb_d = din("w_vb", [D, 768])
    wga_d = din("w_ga", [D, D])
    wgb_d = din("w_gb", [D, D])
    wbra_d = din("w_br_a", [512, D])
    wbrb_d = din("w_br_b", [256, D])
    wout_d = din("w_out", [D, D])
    wfg_d = din("w_ffn_gate", [D, DFF])
    wfu_d = din("w_ffn_up", [D, DFF])
    wfd_d = din("w_ffn_down", [DFF, D])
    ident_d = din("k_ident", [128, 128])
    mask_d = din("k_mask", [128, 768])
    sel_d = din("k_sel", [128, 8 * 128])
    pm_d = din("k_pm", [128, 128])
    cos_d = din("k_cos", [128, S])
    sin_d = din("k_sin", [128, S])
    ones_d = din("k_ones", [128, 128])
    out_d = nc.dram_tensor("out", [S, D], F32, kind="ExternalOutput").ap()
    dbg_d = {}

    with ExitStack() as es:
        S_ = Sched(nc, es)

        def sb(name, shape, dt, scope=es):
            return scope.enter_context(nc.sbuf_tensor(name, list(shape), dt))

        PB = [T(es.enter_context(nc.psum_tensor("pb%d" % i, [128, 512], F32)), excl=True) for i in range(8)]
        PTh = [PB[6], PB[7]]
        PTv = [PB[6].t[:, :].bitcast(BF16), PB[7].t[:, :].bitcast(BF16)]
        pall = Rot(PB)

        bufA = sb("bufA", [128, KC, S], BF16)
        bufB = sb("bufB", [128, KC, S], BF16)
        bufA_d = [T() for _ in range(NC4)]
        bufB_d = [T() for _ in range(NC4)]
        WT = [T(sb("wt%d" % i, [128, KC, 512], BF16)) for i in range(3)]
        wrot = Rot(WT)
        ident_bf = T(sb("ident_bf", [128, 128], BF16))
        ident_f = T(sb("ident_f", [128, 128], F32))
        maskX = T(sb("maskX", [128, 768], BF16))
        selones = T(sb("selones", [128, 8 * 128], BF16))
        pm_bf = T(sb("pm_bf", [128, 128], BF16))
        ones_bf = T(sb("ones_bf", [128, 128], BF16))
        gaBCm = T(sb("gaBCm", [128, D], F32))
        gaBCf = T(sb("gaBCf", [128, D], F32))
        gfinBC = T(sb("gfinBC", [128, D], F32))
        modT = T(sb("modT", [128, 32], F32))
        gsc = T(sb("gsc", [128, 16], F32))
        small = T(sb("small", [128, 64], F32))
        badac = T(sb("badac", [128, 48], F32))
        cs_bf = T(sb("cs_bf", [128, KC], BF16))
        csb_bf = T(sb("csb_bf", [128, KC, 128], BF16))
        ssq = T(sb("ssq", [128, 3 * NT], F32))
        rstd = T(sb("rstd", [128, 3 * NT], F32))

        def load_w(src_ap, ncols, kc=KC, tile=None):
            wt = wrot.next() if tile is None else tile
            S_.dma("pool", lambda e, wt=wt: e.dma_start(
                out=wt.t[:, 0:kc, 0:ncols], in_=src_ap.rearrange("(kc p) n -> p kc n", p=128)), writes=[wt])
            return wt

        def dbg(name, tiles, ap, shape, dt=F32):
            if name not in debug:
                if stop == name:
                    S_.stopped = True
                return
            d = nc.dram_tensor("dbg_" + name, list(shape), dt, kind="ExternalOutput").ap()
            dbg_d[name] = d
            S_.dma("sp", lambda e: e.dma_start(out=d, in_=ap), reads=[T()] + list(tiles))
            if stop == name:
                S_.stopped = True

        def chk(name):
            if stop == name:
                S_.stopped = True

        try:
            for (tl, src) in ((ident_bf, ident_d), (maskX, mask_d), (selones, sel_d), (pm_bf, pm_d), (ones_bf, ones_d)):
                S_.dma("pool", lambda e, tl=tl, src=src: e.dma_start(out=tl.t[:], in_=src), writes=[tl])
            S_.dma("sp", lambda e: e.dma_start(out=ident_f.t[:], in_=ident_d), writes=[ident_f])
            S_.dma("sp", lambda e: e.dma_start(out=small.t[:, 0:8], in_=ccol_d), writes=[small])
            S_.dma("sp", lambda e: e.dma_start(out=small.t[:, 8:16], in_=gmix_d), writes=[small])
            S_.dma("sp", lambda e: e.dma_start(out=small.t[:, 16:24], in_=gffn_d), writes=[small])
            S_.dma("sp", lambda e: e.dma_start(out=badac.t[:], in_=badac_d), writes=[badac])
            S_.op("act", lambda e: e.activation(out=cs_bf.t[:], in_=small.t[:, 0:8], func=AF.Silu), reads=[small], writes=[cs_bf])
            S_.op("dve", lambda e: e.tensor_copy(out=csb_bf.t[:], in_=cs_bf.t[:].unsqueeze(2).to_broadcast([128, KC, 128])),
                  reads=[cs_bf], writes=[csb_bf])

            def ada_cols(sec, dst0):
                pb = pall.next()
                for half in range(2):
                    wt = load_w(wada_d[:, sec * D + half * 512: sec * D + (half + 1) * 512], 512)
                    for j in range(4):
                        col = half * 4 + j
                        for kc in range(KC):
                            S_.op("pe", lambda e, wt=wt, j=j, kc=kc, col=col, pb=pb: e.matmul(
                                pb.t[:, col:col + 1], lhsT=wt.t[:, kc, j * 128:(j + 1) * 128], rhs=cs_bf.t[:, kc:kc + 1],
                                start=(kc == 0), stop=(kc == KC - 1)),
                                reads=[wt, cs_bf], writes=[pb], inc=(kc == KC - 1))
                S_.op("dve", lambda e, pb=pb: e.tensor_tensor(out=modT.t[:, dst0:dst0 + 8], in0=pb.t[:, 0:8],
                                                             in1=badac.t[:, sec * 8:(sec + 1) * 8], op=ALU.add),
                      reads=[pb, badac], writes=[modT])

            def ada_rows(sec, dstT):
                for half in range(2):
                    wt = load_w(wada_d[:, sec * D + half * 512: sec * D + (half + 1) * 512], 512)
                    pb = pall.next()
                    for kc in range(KC):
                        S_.op("pe", lambda e, wt=wt, kc=kc, pb=pb: e.matmul(
                            pb.t[:, :], lhsT=csb_bf.t[:, kc, :], rhs=wt.t[:, kc, :], start=(kc == 0), stop=(kc == KC - 1)),
                            reads=[wt, csb_bf], writes=[pb], inc=(kc == KC - 1))
                    S_.op("dve", lambda e, pb=pb, half=half: e.tensor_tensor(
                        out=dstT.t[:, half * 512:(half + 1) * 512], in0=pb.t[:, :], in1=dstT.t[:, half * 512:(half + 1) * 512],
                        op=ALU.add), reads=[pb, dstT], writes=[dstT])

            def ada_cols_half(sec, dst0, half, tile, ps=None):
                pb = (ps or pall).next()
                wt = load_w(wada_d[:, sec * D + half * 512: sec * D + (half + 1) * 512], 512, tile=tile)
                for j in range(4):
                    for kc in range(KC):
                        S_.op("pe", lambda e, wt=wt, j=j, kc=kc, pb=pb: e.matmul(
                            pb.t[:, j:j + 1], lhsT=wt.t[:, kc, j * 128:(j + 1) * 128], rhs=cs_bf.t[:, kc:kc + 1],
                            start=(kc == 0), stop=(kc == KC - 1)),
                            reads=[wt, cs_bf], writes=[pb], inc=(kc == KC - 1))
                c0 = dst0 + half * 4
                b0 = sec * 8 + half * 4
                S_.op("dve", lambda e, pb=pb: e.tensor_tensor(out=modT.t[:, c0:c0 + 4], in0=pb.t[:, 0:4],
                                                             in1=badac.t[:, b0:b0 + 4], op=ALU.add),
                      reads=[pb, badac], writes=[modT])

            def ada_rows_half(sec, dstT, half, tile, ps=None):
                wt = load_w(wada_d[:, sec * D + half * 512: sec * D + (half + 1) * 512], 512, tile=tile)
                pb = (ps or pall).next()
                for kc in range(KC):
                    S_.op("pe", lambda e, wt=wt, kc=kc, pb=pb: e.matmul(
                        pb.t[:, :], lhsT=csb_bf.t[:, kc, :], rhs=wt.t[:, kc, :], start=(kc == 0), stop=(kc == KC - 1)),
                        reads=[wt, csb_bf], writes=[pb], inc=(kc == KC - 1))
                S_.op("dve", lambda e, pb=pb, half=half: e.tensor_tensor(
                    out=dstT.t[:, half * 512:(half + 1) * 512], in0=pb.t[:, :], in1=dstT.t[:, half * 512:(half + 1) * 512],
                    op=ALU.add), reads=[pb, dstT], writes=[dstT])

            def make_gsc(which):
                sc0 = 8 if which == 0 else 24
                g0 = 8 if which == 0 else 16
                S_.op("dve", lambda e: e.scalar_tensor_tensor(
                    out=gsc.t[:, which * 8:(which + 1) * 8], in0=modT.t[:, sc0:sc0 + 8], scalar=1.0,
                    in1=small.t[:, g0:g0 + 8], op0=ALU.add, op1=ALU.mult), reads=[modT, small], writes=[gsc])

            def ada_first():
                ada_cols(0, 0)
                ada_cols(1, 8)
                make_gsc(0)

            def norm_to_T(src_tile_fn, nidx, dstbuf, dst_d, sh0, gs0, scope, hook=None, after_chunk=None):
                junk = T(sb("junk%d" % nidx, [128, D], BF16, scope))
                xs = [T(sb("xs%d_%d" % (nidx, i), [128, D], BF16, scope)) for i in range(8)]
                sq = T(sb("sq%d" % nidx, [128, 8], F32, scope))
                sq_c = [T() for _ in range(8)]
                ssq_c = [T() for _ in range(NT)]
                rstd_c = [T() for _ in range(NT)]

                def stats(tc):
                    for i in range(4):
                        tt = tc * 4 + i
                        col = nidx * NT + tt
                        b = (tc % 2) * 4 + i
                        xt, xap = src_tile_fn(tt)
                        S_.op("act", lambda e: e.activation(
                            out=junk.t[:], in_=xap, func=AF.Square, accum_out=ssq.t[:, col:col + 1]),
                            reads=[xt], writes=[junk, ssq_c[tt]])
                        S_.op("act", lambda e: e.activation(
                            out=sq.t[:, b:b + 1], in_=ssq.t[:, col:col + 1], func=AF.Sqrt, scale=1.0 / D, bias=EPS),
                            reads=[ssq_c[tt]], writes=[sq_c[b]])
                        S_.op("dve", lambda e: e.reciprocal(out=rstd.t[:, col:col + 1], in_=sq.t[:, b:b + 1]),
                              reads=[sq_c[b]], writes=[rstd_c[tt]])
                        S_.op("dve", lambda e: e.tensor_scalar_mul(
                            out=xs[b].t[:], in0=xap, scalar1=rstd.t[:, col:col + 1]),
                            reads=[xt, rstd_c[tt]], writes=[xs[b]])

                def tr_evac(tc):
                    for c in range(KC):
                        h = c % 2
                        pt = PTh[h]
                        for i in range(4):
                            b = (tc % 2) * 4 + i
                            S_.op("pe", lambda e: e.transpose(
                                PTv[h][:, i * 128:(i + 1) * 128], xs[b].t[:, c * 128:(c + 1) * 128], ident_bf.t[:]),
                                reads=[xs[b], ident_bf], writes=[pt], inc=(i == 3))
                        if h == 0:
                            S_.op("act", lambda e: e.activation(
                                out=dstbuf[:, c, tc * 512:(tc + 1) * 512], in_=PTv[h][:, 0:512], func=AF.Identity,
                                scale=gsc.t[:, gs0 + c:gs0 + c + 1], bias=modT.t[:, sh0 + c:sh0 + c + 1]),
                                reads=[pt, gsc, modT], writes=[dst_d[tc]])
                        else:
                            S_.op("dve", lambda e: e.tensor_scalar(
                                out=dstbuf[:, c, tc * 512:(tc + 1) * 512], in0=PTv[h][:, 0:512],
                                scalar1=gsc.t[:, gs0 + c:gs0 + c + 1], scalar2=modT.t[:, sh0 + c:sh0 + c + 1],
                                op0=ALU.mult, op1=ALU.add), reads=[pt, gsc, modT], writes=[dst_d[tc]])

                stats(0)
                stats(1)
                if hook is not None:
                    hook()
                for tc in range(NC4):
                    tr_evac(tc)
                    if tc + 2 < NC4:
                        stats(tc + 2)
                    if after_chunk is not None:
                        after_chunk(tc)

            sf = es.enter_context(ExitStack())
            Vall = T(sb("Vall", [128, NT, 512], BF16, sf))
            wv_box = []

            def ada_first_and_wv():
                ada_first()
                wv_box.append(load_w(wva_d, 512, tile=WT[2]))

            def v_chunk(tc):
                wv = wv_box[0]
                for tt in range(4 * tc, 4 * tc + 4):
                    pb = pall.next()
                    for kc in range(KC):
                        S_.op("pe", lambda e, kc=kc, tt=tt, pb=pb: e.matmul(
                            pb.t[:, :], lhsT=bufA[:, kc, tt * 128:(tt + 1) * 128], rhs=wv.t[:, kc, :],
                            start=(kc == 0), stop=(kc == KC - 1)), reads=[wv, bufA_d[tc]], writes=[pb], inc=(kc == KC - 1))
                    S_.op("act", lambda e, tt=tt, pb=pb: e.activation(out=Vall.t[:, tt, :], in_=pb.t[:, :], func=AF.Copy),
                          reads=[pb], writes=[Vall])

            with ExitStack() as sc1:
                xin = [T(sb("xin%d" % i, [128, D], F32, sc1)) for i in range(4)]
                xrot = Rot(xin)

                def x_from_hbm(tt):
                    xt = xrot.next()
                    S_.dma("sp", lambda e, xt=xt, tt=tt: e.dma_start(out=xt.t[:], in_=x_d[tt * 128:(tt + 1) * 128, :]), writes=[xt])
                    return xt, xt.t[:]
                norm_to_T(x_from_hbm, 0, bufA, bufA_d, 0, 0, sc1, hook=ada_first_and_wv, after_chunk=v_chunk)
            S_.barrier()
            S_.dma("sp", lambda e: e.dma_start(out=gfinBC.t[:], in_=gfin_d), writes=[gfinBC])
            S_.dma("sp", lambda e: e.dma_start(out=gaBCm.t[:], in_=bgam_d), writes=[gaBCm])
            S_.dma("sp", lambda e: e.dma_start(out=gaBCf.t[:], in_=bgaf_d), writes=[gaBCf])
            dbg("hT", bufA_d, bufA[:], [128, KC, S], BF16)

            deferred = [lambda t, p: ada_rows_half(2, gaBCm, 0, t, p), lambda t, p: ada_rows_half(2, gaBCm, 1, t, p),
                        lambda t, p: ada_cols_half(3, 16, 0, t, p), lambda t, p: ada_cols_half(3, 16, 1, t, p),
                        lambda t, p: ada_cols_half(4, 24, 0, t, p), lambda t, p: (ada_cols_half(4, 24, 1, t, p), make_gsc(1)),
                        lambda t, p: ada_rows_half(5, gaBCf, 0, t, p), lambda t, p: ada_rows_half(5, gaBCf, 1, t, p)]

            STB = Rot([PB[0], PB[1], PB[2], PB[7]])
            ACC = Rot([(PB[3], PB[4]), (PB[5], PB[6])])
            if True:
                yaT = sb("yaT", [128, 4, S], BF16, sf)
                yaT_d = [T() for _ in range(NC4)]
                wbrA = T(sb("wbrA", [128, 4, D], BF16, sf))
                kz = [T(sb("kzA%d" % i, [128, S], BF16, sf)) for i in range(2)]
                qz = [T(sb("qzA%d" % i, [128, S], BF16, sf)) for i in range(2)]
                PTb = Rot([T(sb("PTbA%d" % i, [128, 512], BF16, sf)) for i in range(3)])
                rec = Rot([T(sb("recA%d" % i, [128, 512], F32, sf)) for i in range(2)])
                Grow = T(sb("Grow", [8, S], F32, sf))
                spr = T(sb("spr", [8, S], F32, sf))
                onesr = T(sb("onesr", [8, 512], F32, sf))
                Fb8 = T(sb("Fb8", [128, S], BF16, sf))
                Gtok = T(sb("Gtok", [128, NT, 8], F32, sf))
                nbf = T(sb("nbf", [8, 2], F32, sf))
                etmp = T(sb("etmp", [8, 512], F32, sf))

                wf = WT[1]
                S_.dma("pool", lambda e: e.dma_start(out=wf.t[:, :, 0:8], in_=wfa_d.rearrange("p (kc n) -> p kc n", n=8)), writes=[wf])
                wq = load_w(wqa_d, 512, tile=WT[0])
                S_.op("pool", lambda e: e.memset(qz[0].t[:], 0.0), writes=[qz[0]])
                S_.op("pool", lambda e: e.memset(qz[1].t[:], 0.0), writes=[qz[1]])
                S_.op("pool", lambda e: e.memset(Fb8.t[:], 0.0), writes=[Fb8])
                S_.op("pool", lambda e: e.memset(kz[0].t[:], 0.0), writes=[kz[0]])
                S_.op("pool", lambda e: e.memset(kz[1].t[:], 0.0), writes=[kz[1]])
                S_.op("pool", lambda e: e.memset(kz[0].t[64:65, :], 1.0), writes=[kz[0]])
                S_.op("pool", lambda e: e.memset(kz[1].t[0:1, :], 1.0), writes=[kz[1]])
                S_.op("pool", lambda e: e.memset(onesr.t[:], 1.0), writes=[onesr])
                S_.dma("sp", lambda e: e.dma_start(out=nbf.t[:, 0:1], in_=bfg_d), writes=[nbf])
                S_.op("dve", lambda e: e.tensor_scalar_mul(out=nbf.t[:, 1:2], in0=nbf.t[:, 0:1], scalar1=-1.0),
                      reads=[nbf], writes=[nbf])

                for tc in range(NC4):
                    pb = pall.next()
                    for kc in range(KC):
                        S_.op("pe", lambda e, kc=kc, tc=tc, pb=pb: e.matmul(
                            pb.t[0:8, :], lhsT=wf.t[:, kc, 0:8], rhs=bufA[:, kc, tc * 512:(tc + 1) * 512],
                            start=(kc == 0), stop=(kc == KC - 1)), reads=[wf, bufA_d[tc]], writes=[pb], inc=(kc == KC - 1))
                    S_.op("act", lambda e, pb=pb: e.activation(out=etmp.t[:], in_=pb.t[0:8, :], func=AF.Exp, scale=-1.0,
                                                               bias=nbf.t[:, 1:2]), reads=[pb, nbf], writes=[etmp])
                    S_.op("act", lambda e, tc=tc: e.activation(out=spr.t[:, tc * 512:(tc + 1) * 512], in_=etmp.t[:], func=AF.Ln,
                                                               bias=1.0), reads=[etmp], writes=[spr])
                    if tc == 0:
                        S_.op("dve", lambda e: e.tensor_tensor_scan(out=Grow.t[:, 0:512], data0=onesr.t[:], data1=spr.t[:, 0:512],
                                                                    initial=0.0, op0=ALU.mult, op1=ALU.add),
                              reads=[onesr, spr], writes=[Grow])
                    else:
                        S_.op("dve", lambda e, tc=tc: e.tensor_tensor_scan(
                            out=Grow.t[:, tc * 512:(tc + 1) * 512], data0=onesr.t[:], data1=spr.t[:, tc * 512:(tc + 1) * 512],
                            initial=Grow.t[:, tc * 512 - 1:tc * 512], op0=ALU.mult, op1=ALU.add),
                            reads=[onesr, spr, Grow], writes=[Grow])
                S_.op("dve", lambda e: e.tensor_scalar_mul(out=Fb8.t[0:8, :], in0=Grow.t[:], scalar1=-8.0),
                      reads=[Grow], writes=[Fb8])
                pb = pall.next()
                for tt in range(NT):
                    S_.op("pe", lambda e, tt=tt, pb=pb: e.transpose(pb.t[:, tt * 8:(tt + 1) * 8], Grow.t[0:8, tt * 128:(tt + 1) * 128],
                                                                   ident_f.t[0:8, 0:8]),
                          reads=[Grow, ident_f], writes=[pb], inc=(tt == NT - 1))
                S_.op("dve", lambda e, pb=pb: e.tensor_copy(out=Gtok.t[:].rearrange("p a b -> p (a b)"), in_=pb.t[:, 0:128]),
                      reads=[pb], writes=[Gtok])
                dbg("Grow", [Grow], Grow.t[:], [8, S])

                wk = load_w(wka_d, 512, tile=WT[1])
                S_.dma("pool", lambda e: e.dma_start(out=wbrA.t[:], in_=wbra_d.rearrange("(pr p) n -> p pr n", p=128)), writes=[wbrA])

                for pr in range(4):
                    for tc in range(NC4):
                        pb = pall.next()
                        for kc in range(KC):
                            S_.op("pe", lambda e, kc=kc, tc=tc, pb=pb, pr=pr: e.matmul(
                                pb.t[:, :], lhsT=wq.t[:, kc, pr * 128:(pr + 1) * 128], rhs=bufA[:, kc, tc * 512:(tc + 1) * 512],
                                start=(kc == 0), stop=(kc == KC - 1)), reads=[wq, bufA_d[tc]], writes=[pb], inc=(kc == KC - 1))
                        for hp in range(2):
                            S_.op("act", lambda e, tc=tc, pb=pb, hp=hp: e.activation(
                                out=qz[hp].t[hp * 64:(hp + 1) * 64, tc * 512:(tc + 1) * 512], in_=pb.t[hp * 64:(hp + 1) * 64, :],
                                func=AF.Copy), reads=[pb], writes=[qz[hp]])
                        pb = pall.next()
                        for kc in range(KC):
                            S_.op("pe", lambda e, kc=kc, tc=tc, pb=pb, pr=pr: e.matmul(
                                pb.t[:, :], lhsT=wk.t[:, kc, pr * 128:(pr + 1) * 128], rhs=bufA[:, kc, tc * 512:(tc + 1) * 512],
                                start=(kc == 0), stop=(kc == KC - 1)), reads=[wk, bufA_d[tc]], writes=[pb], inc=(kc == KC - 1))
                        for hp in range(2):
                            S_.op("dve", lambda e, tc=tc, pb=pb, hp=hp: e.tensor_copy(
                                out=kz[hp].t[hp * 64:(hp + 1) * 64, tc * 512:(tc + 1) * 512], in_=pb.t[hp * 64:(hp + 1) * 64, :]),
                                reads=[pb], writes=[kz[hp]])
                        for hp in range(2):
                            h = pr * 2 + hp
                            ar = 64 if hp == 0 else 0
                            pb = pall.next()
                            S_.op("pe", lambda e, pb=pb, h=h, tc=tc: e.matmul(
                                pb.t[:, :], lhsT=selones.t[:, h * 128:(h + 1) * 128], rhs=Fb8.t[:, tc * 512:(tc + 1) * 512],
                                start=True, stop=True), reads=[selones, Fb8], writes=[pb])
                            S_.op("act", lambda e, pb=pb, hp=hp, ar=ar, tc=tc: e.activation(
                                out=qz[hp].t[ar:ar + 1, tc * 512:(tc + 1) * 512], in_=pb.t[ar:ar + 1, :], func=AF.Copy),
                                reads=[pb], writes=[qz[hp]])
                    if pr == 3:
                        wga_pre = [load_w(wga_d[:, 0:512], 512, tile=WT[0]), load_w(wga_d[:, 512:1024], 512, tile=WT[1])]
                    its = []
                    for hp in range(2):
                        for g in range(4):
                            for kb in range(4 * g + 4):
                                its.append(dict(hp=hp, g=g, kb=kb, nkb=4 * g + 4))

                    def qk(it):
                        hp, g, kb = it["hp"], it["g"], it["kb"]
                        if kb == 0:
                            it["acc"] = ACC.next()
                        else:
                            it["acc"] = it["prev"]["acc"]
                        c0 = max(0, kb - 4 * g) * 128
                        q0 = g * 512 + c0
                        st = STB.next()
                        it["st"], it["c0"] = st, c0
                        diag = kb >= 4 * g
                        S_.op("pe", lambda e: e.matmul(
                            st.t[:, c0:512], lhsT=kz[hp].t[:, kb * 128:(kb + 1) * 128], rhs=qz[hp].t[:, q0:(g + 1) * 512],
                            start=True, stop=(not diag)), reads=[kz[hp], qz[hp]], writes=[st], inc=(not diag))
                        if diag:
                            S_.op("pe", lambda e: e.matmul(
                                st.t[:, c0:c0 + 128], lhsT=ident_bf.t[:], rhs=maskX.t[:, 0:128], start=False, stop=True),
                                reads=[ident_bf, maskX], writes=[st], inc=True)

                    def ex_pv(it):
                        hp, g, kb, nkb = it["hp"], it["g"], it["kb"], it["nkb"]
                        h = pr * 2 + hp
                        st, c0 = it["st"], it["c0"]
                        num, den = it["acc"]
                        pt = PTb.next()
                        S_.op("act", lambda e: e.activation(
                            out=pt.t[:, c0:512], in_=st.t[:, c0:512], func=AF.Exp, scale=SCALE, bias=Gtok.t[:, kb, h:h + 1]),
                            reads=[st, Gtok], writes=[pt])
                        S_.op("pe", lambda e: e.matmul(
                            num.t[:, c0:512], lhsT=Vall.t[:, kb, pr * 128:(pr + 1) * 128], rhs=pt.t[:, c0:512],
                            start=(kb == 0), stop=(kb == nkb - 1), skip_group_check=True),
                            reads=[Vall, pt], writes=[num], inc=False)
                        S_.op("pe", lambda e: e.matmul(
                            den.t[:, c0:512], lhsT=ones_bf.t[:], rhs=pt.t[:, c0:512],
                            start=(kb == 0), stop=(kb == nkb - 1), skip_group_check=True),
                            reads=[ones_bf, pt], writes=[den], inc=True)
                        if kb == nkb - 1:
                            lanes = slice(hp * 64, (hp + 1) * 64)
                            rc = rec.next()
                            S_.op("dve", lambda e: e.reciprocal(out=rc.t[lanes, :], in_=den.t[lanes, :]), reads=[den], writes=[rc])
                            S_.op("dve", lambda e: e.tensor_tensor(
                                out=yaT[lanes, pr, g * 512:(g + 1) * 512], in0=num.t[lanes, :], in1=rc.t[lanes, :], op=ALU.mult),
                                reads=[num, rc], writes=[yaT_d[g]])

                    for i, it in enumerate(its):
                        it["prev"] = its[i - 1] if i > 0 else None
                    AH = 2
                    for i in range(AH):
                        qk(its[i])
                    for i, it in enumerate(its):
                        if i + AH < len(its):
                            qk(its[i + AH])
                        ex_pv(it)
                        if i == len(its) // 2 or i == len(its) - 1:
                            deferred.pop(0)(WT[2], STB)
                dbg("yaT", yaT_d, yaT[:], [128, 4, S], BF16)

                wbr = wbrA
                sg = Rot([T(sb("sgA%d" % i, [128, 512], F32, sf)) for i in range(2)])
                wqb_pre = load_w(wqb_d[:, 0:512], 512, tile=WT[2])
                for half in range(2):
                    wg = wga_pre[half]
                    if half == 1:
                        wkb_pre = load_w(wkb_d[:, 0:512], 512, tile=WT[0])
                    for j in range(4):
                        nch = half * 4 + j
                        for tc in range(NC4):
                            pg = pall.next()
                            for kc in range(KC):
                                S_.op("pe", lambda e, kc=kc, tc=tc, pg=pg, j=j, wg=wg: e.matmul(
                                    pg.t[:, :], lhsT=wg.t[:, kc, j * 128:(j + 1) * 128], rhs=bufA[:, kc, tc * 512:(tc + 1) * 512],
                                    start=(kc == 0), stop=(kc == KC - 1)), reads=[wg, bufA_d[tc]], writes=[pg], inc=(kc == KC - 1))
                            s_ = sg.next()
                            S_.op("act", lambda e, pg=pg, s_=s_: e.activation(out=s_.t[:], in_=pg.t[:, :], func=AF.Sigmoid),
                                  reads=[pg], writes=[s_])
                            pbr = pall.next()
                            for pr in range(4):
                                S_.op("pe", lambda e, pr=pr, tc=tc, pbr=pbr, nch=nch: e.matmul(
                                    pbr.t[:, :], lhsT=wbr.t[:, pr, nch * 128:(nch + 1) * 128], rhs=yaT[:, pr, tc * 512:(tc + 1) * 512],
                                    start=(pr == 0), stop=(pr == 3)), reads=[wbr, yaT_d[tc]], writes=[pbr], inc=(pr == 3))
                            S_.op("dve", lambda e, pbr=pbr, s_=s_, nch=nch, tc=tc: e.tensor_tensor(
                                out=bufB[:, nch, tc * 512:(tc + 1) * 512], in0=pbr.t[:, :], in1=s_.t[:], op=ALU.mult),
                                reads=[pbr, s_], writes=[bufB_d[tc]])
            sf.close()
            S_.barrier()
            dbg("mA", bufB_d, bufB[:], [128, KC, S], BF16)

            with ExitStack() as sd:
                ybT = sb("ybT", [128, 2, S], BF16, sd)
                ybT_d = [T() for _ in range(2)]
                wbrB = T(sb("wbrB", [128, 2, D], BF16, sd))
                S_.dma("pool", lambda e: e.dma_start(out=wbrB.t[:], in_=wbrb_d.rearrange("(pr p) n -> p pr n", p=128)), writes=[wbrB])
                sda = sd.enter_context(ExitStack())
                cosT = T(sb("cosT", [128, S], F32, sda))
                sinT = T(sb("sinT", [128, S], F32, sda))
                S_.dma("sp", lambda e: e.dma_start(out=cosT.t[:], in_=cos_d), writes=[cosT])
                S_.dma("sp", lambda e: e.dma_start(out=sinT.t[:], in_=sin_d), writes=[sinT])
                accN = T(sb("accN", [128, S], F32, sda))
                accD = T(sb("accD", [128, S], F32, sda))
                kp = T(sb("kpB", [128, S], BF16, sda))
                qz = [T(sb("qzB%d" % i, [128, S], BF16, sda)) for i in range(2)]
                Vp = T(sb("VpB", [128, NT, 128], BF16, sda))
                PTb = Rot([T(sb("PTbB%d" % i, [128, 512], BF16, sda)) for i in range(3)])
                qtmp = Rot([T(sb("qtmp%d" % i, [128, 512], BF16, sda)) for i in range(2)])
                t1r = Rot([T(sb("t1r%d" % i, [128, 512], F32, sda)) for i in range(2)])
                t2r = Rot([T(sb("t2r%d" % i, [128, 512], F32, sda)) for i in range(2)])
                S_.op("pool", lambda e: e.memset(qz[0].t[:], 0.0), writes=[qz[0]])
                S_.op("pool", lambda e: e.memset(qz[1].t[:], 0.0), writes=[qz[1]])
                wqb = [wqb_pre, None]
                wkb = [wkb_pre, None]
                wvb = [load_w(wvb_d[:, 0:512], 512, tile=WT[1]), None]
                WX = [T(sb("wx%d" % i, [128, KC, 256], BF16, sda)) for i in range(3)]
                for i, src in enumerate((wqb_d, wkb_d, wvb_d)):
                    S_.dma("pool", lambda e, i=i, src=src: e.dma_start(
                        out=WX[i].t[:], in_=src[:, 512:768].rearrange("(kc p) n -> p kc n", p=128)), writes=[WX[i]])

                def wsel(main, extra, pc):
                    if pc < 4:
                        return main, main.t, pc * 128
                    return extra, extra.t, (pc - 4) * 128

                chk("c1")
                def make_step(m, g, qz, kp):
                    d = DIL[g]
                    pc = 2 * g + m
                    nbk = 16 // d
                    wq_t, wq_ap, wq_c = wsel(wqb[0], WX[0], pc)
                    wk_t, wk_ap, wk_c = wsel(wkb[0], WX[1], pc)
                    wv_t, wv_ap, wv_c = wsel(wvb[0], WX[2], pc)
                    def f_proj():
                        pend = []
                        for which in range(2):
                            w_t, w_ap, w_c = (wq_t, wq_ap, wq_c) if which == 0 else (wk_t, wk_ap, wk_c)
                            for j in range(4):
                                pq = pall.next()
                                for kc in range(KC):
                                    S_.op("pe", lambda e, kc=kc, j=j, pq=pq, w_ap=w_ap, w_c=w_c, g=g: e.matmul(
                                        pq.t[:, :], lhsT=w_ap[:, kc, w_c:w_c + 128], rhs=bufA[:, kc, j * 512:(j + 1) * 512],
                                        start=(kc == 0), stop=(kc == KC - 1)), reads=[w_t, bufA_d[j]], writes=[pq], inc=(kc == KC - 1))
                                chk("c2")
                                qt = qtmp.next()
                                S_.op("act", lambda e, pq=pq, qt=qt: e.activation(out=qt.t[:], in_=pq.t[:, :], func=AF.Copy),
                                      reads=[pq], writes=[qt])
                                def post(pq=pq, qt=qt, j=j, which=which):
                                    psw = pall.next()
                                    S_.op("pe", lambda e, psw=psw, qt=qt: e.matmul(psw.t[:, :], lhsT=pm_bf.t[:], rhs=qt.t[:], start=True, stop=True),
                                          reads=[pm_bf, qt], writes=[psw])
                                    chk("c4")
                                    t1 = t1r.next()
                                    t2 = t2r.next()
                                    S_.op("dve", lambda e, pq=pq, t1=t1, g=g, j=j: e.tensor_tensor(
                                        out=t1.t[:], in0=pq.t[:, :], in1=cosT.t[:, j * 512:(j + 1) * 512], op=ALU.mult),
                                        reads=[pq, cosT], writes=[t1])
                                    chk("c4b")
                                    S_.op("dve", lambda e, psw=psw, t2=t2, g=g, j=j: e.tensor_tensor(
                                        out=t2.t[:], in0=psw.t[:, :], in1=sinT.t[:, j * 512:(j + 1) * 512], op=ALU.mult),
                                        reads=[psw, sinT], writes=[t2])
                                    chk("c5")
                                    if which == 0:
                                        for hp in range(2):
                                            S_.op("pool", lambda e, t1=t1, t2=t2, hp=hp, j=j: e.tensor_tensor(
                                                out=cls_out(qz[hp].t[hp * 64:(hp + 1) * 64, :], g, j), in0=nat_in(t1.t[hp * 64:(hp + 1) * 64, :], g),
                                                in1=nat_in(t2.t[hp * 64:(hp + 1) * 64, :], g), op=ALU.add), reads=[t1, t2], writes=[qz[hp]])
                                    else:
                                        S_.op("pool", lambda e, t1=t1, t2=t2, j=j: e.tensor_tensor(
                                            out=cls_out(kp.t[:, :], g, j), in0=nat_in(t1.t[:], g), in1=nat_in(t2.t[:], g), op=ALU.add),
                                            reads=[t1, t2], writes=[kp])
                                pend.append(post)
                                if len(pend) > 1:
                                    pend.pop(0)()
                        while pend:
                            pend.pop(0)()
                        chk("dq%d%d" % (m, g))
                    def f_v():
                        for cb4 in range(4):
                            pv = pall.next()
                            for i in range(4):
                                cb = cb4 * 4 + i
                                r, n_ = cb // nbk, cb % nbk
                                t0 = d * 128 * n_ + r
                                for kc in range(KC):
                                    S_.op("pe", lambda e, kc=kc, i=i, pv=pv, t0=t0, d=d, wv_ap=wv_ap, wv_c=wv_c: e.matmul(
                                        pv.t[:, i * 128:(i + 1) * 128], lhsT=bufA[:, kc, t0:t0 + 127 * d + 1:d], rhs=wv_ap[:, kc, wv_c:wv_c + 128],
                                        start=(kc == 0), stop=(kc == KC - 1)), reads=[wv_t] + bufA_d, writes=[pv],
                                        inc=(kc == KC - 1 and i == 3))
                            S_.op("act", lambda e, pv=pv, cb4=cb4: e.activation(
                                out=Vp.t[:, cb4 * 4:(cb4 + 1) * 4, :].rearrange("p a b -> p (a b)"), in_=pv.t[:, :], func=AF.Copy),
                                reads=[pv], writes=[Vp])
                        chk("dv%d%d" % (m, g))
                    def f_att():
                        units = []
                        for hp in range(2):
                            if g < 2:
                                for cb in range(16):
                                    hasnext = (cb % nbk) < nbk - 1
                                    units.append(dict(hp=hp, kbs=[cb], qcols=(cb * 128, (cb + (2 if hasnext else 1)) * 128),
                                                      mask0=0))
                            else:
                                for j in range(4):
                                    units.append(dict(hp=hp, kbs=[4 * j + i for i in range(4)], qcols=(j * 512, (j + 1) * 512),
                                                      mask0=256))
                        accs = {}

                        def acc_of(hp, j):
                            if (hp, j) not in accs:
                                accs[(hp, j)] = ACC.next()
                            return accs[(hp, j)]

                        def d_qk(u):
                            hp = u["hp"]
                            st = STB.next()
                            u["st"] = st
                            q0, q1 = u["qcols"]
                            n = q1 - q0
                            u["n"] = n
                            if g < 2:
                                cb = u["kbs"][0]
                                S_.op("pe", lambda e: e.matmul(
                                    st.t[:, 0:n], lhsT=kp.t[:, cb * 128:(cb + 1) * 128], rhs=qz[hp].t[:, q0:q1],
                                    start=True, stop=False), reads=[kp, qz[hp]], writes=[st], inc=False)
                            else:
                                for i, cb in enumerate(u["kbs"]):
                                    S_.op("pe", lambda e: e.matmul(
                                        st.t[:, i * 128:(i + 1) * 128], lhsT=kp.t[:, cb * 128:(cb + 1) * 128],
                                        rhs=qz[hp].t[:, cb * 128:(cb + 1) * 128], start=(i == 0), stop=False),
                                        reads=[kp, qz[hp]], writes=[st], inc=False)
                            m0 = u["mask0"]
                            S_.op("pe", lambda e: e.matmul(
                                st.t[:, 0:n], lhsT=ident_bf.t[:], rhs=maskX.t[:, m0:m0 + n], start=False, stop=True),
                                reads=[ident_bf, maskX], writes=[st], inc=True)

                        def d_pv(u):
                            hp, st, n = u["hp"], u["st"], u["n"]
                            lanes = slice(hp * 64, (hp + 1) * 64)
                            pt = PTb.next()
                            S_.op("act", lambda e: e.activation(out=pt.t[:, 0:n], in_=st.t[:, 0:n], func=AF.Exp, scale=SCALE),
                                  reads=[st], writes=[pt])
                            contribs = []
                            if g < 2:
                                cb = u["kbs"][0]
                                contribs.append((cb, 0, cb, (cb % nbk) == 0))
                                if n == 256:
                                    contribs.append((cb, 128, cb + 1, True))
                            else:
                                for i, cb in enumerate(u["kbs"]):
                                    contribs.append((cb, i * 128, cb, True))
                            for (kb_, pc, qb_, first) in contribs:
                                num, den = acc_of(hp, qb_ // 4)
                                cols = slice((qb_ % 4) * 128, (qb_ % 4 + 1) * 128)
                                last = (kb_ == qb_)
                                S_.op("pe", lambda e: e.matmul(
                                    num.t[:, cols], lhsT=Vp.t[:, kb_, :], rhs=pt.t[:, pc:pc + 128], start=first, stop=last),
                                    reads=[Vp, pt], writes=[num], inc=False)
                                S_.op("pe", lambda e: e.matmul(
                                    den.t[:, cols], lhsT=ones_bf.t[:], rhs=pt.t[:, pc:pc + 128], start=first, stop=last),
                                    reads=[ones_bf, pt], writes=[den], inc=True)
                                if last and qb_ % 4 == 3:
                                    j = qb_ // 4
                                    for (acc, src) in ((accN, num), (accD, den)):
                                        if g == 0:
                                            S_.op("dve", lambda e: e.tensor_copy(
                                                out=tok_view(acc.t[lanes, :], g, j), in_=chunk_view(src.t[lanes, :], g)),
                                                reads=[src], writes=[acc])
                                        else:
                                            S_.op("dve", lambda e: e.tensor_tensor(
                                                out=tok_view(acc.t[lanes, :], g, j), in0=chunk_view(src.t[lanes, :], g),
                                                in1=tok_view(acc.t[lanes, :], g, j), op=ALU.add), reads=[src, acc], writes=[acc])

                        AHEAD = 2
                        for i in range(min(AHEAD, len(units))):
                            d_qk(units[i])
                        for i, u in enumerate(units):
                            if i + AHEAD < len(units):
                                d_qk(units[i + AHEAD])
                            d_pv(u)
                        chk("da%d%d" % (m, g))
                    return f_proj, f_v, f_att

                def finalize(m):
                    S_.op("dve", lambda e: e.reciprocal(out=accD.t[:], in_=accD.t[:]), reads=[accD], writes=[accD])
                    S_.op("dve", lambda e, m=m: e.tensor_tensor(out=ybT[:, m, :], in0=accN.t[:], in1=accD.t[:], op=ALU.mult),
                          reads=[accN, accD], writes=[ybT_d[m]])

                qzs = [qz, [T(sb("qzC%d" % i, [128, S], BF16, sda)) for i in range(2)]]
                kps = [kp, T(sb("kpC", [128, S], BF16, sda))]
                S_.op("pool", lambda e: e.memset(qzs[1][0].t[:], 0.0), writes=[qzs[1][0]])
                S_.op("pool", lambda e: e.memset(qzs[1][1].t[:], 0.0), writes=[qzs[1][1]])
                steps = [make_step(m_, g_, qzs[(m_ * 3 + g_) % 2], kps[(m_ * 3 + g_) % 2]) for m_ in range(2) for g_ in range(3)]
                steps[0][0]()
                steps[0][1]()
                for s_i in range(6):
                    if s_i + 1 < 6:
                        steps[s_i + 1][0]()
                    if s_i == 4:
                        wgb_pre = [load_w(wgb_d[:, 0:512], 512, tile=wqb[0]), load_w(wgb_d[:, 512:1024], 512, tile=wkb[0])]
                    if s_i == 5:
                        wo_pre0 = load_w(wout_d[:, 0:512], 512, tile=wvb[0])
                    steps[s_i][2]()
                    if s_i % 3 == 2:
                        finalize(s_i // 3)
                    if s_i + 1 < 6:
                        steps[s_i + 1][1]()
                sda.close()
                S_.barrier()
                dbg("ybT", ybT_d, ybT[:], [128, 2, S], BF16)

                wbr = wbrB
                sg = Rot([T(sb("sgB%d" % i, [128, 512], F32, sd)) for i in range(2)])
                tm = Rot([T(sb("tmB%d" % i, [128, 512], F32, sd)) for i in range(2)])
                S_.op("dve", lambda e: e.tensor_tensor(
                    out=wo_pre0.t[:], in0=wo_pre0.t[:],
                    in1=gaBCm.t[:, 0:512].unsqueeze(1).to_broadcast([128, KC, 512]), op=ALU.mult),
                    reads=[wo_pre0, gaBCm], writes=[wo_pre0])
                for half in range(2):
                    wg = wgb_pre[half]
                    if half == 1:
                        wo_pre1 = load_w(wout_d[:, 512:1024], 512, tile=wgb_pre[0])
                    for j in range(4):
                        nch = half * 4 + j
                        for tc in range(NC4):
                            pg = pall.next()
                            for kc in range(KC):
                                S_.op("pe", lambda e, kc=kc, tc=tc, pg=pg, j=j, wg=wg: e.matmul(
                                    pg.t[:, :], lhsT=wg.t[:, kc, j * 128:(j + 1) * 128], rhs=bufA[:, kc, tc * 512:(tc + 1) * 512],
                                    start=(kc == 0), stop=(kc == KC - 1)), reads=[wg, bufA_d[tc]], writes=[pg], inc=(kc == KC - 1))
                            s_ = sg.next()
                            S_.op("act", lambda e, pg=pg, s_=s_: e.activation(out=s_.t[:], in_=pg.t[:, :], func=AF.Sigmoid),
                                  reads=[pg], writes=[s_])
                            pbr = pall.next()
                            for pr in range(2):
                                S_.op("pe", lambda e, pr=pr, tc=tc, pbr=pbr, nch=nch: e.matmul(
                                    pbr.t[:, :], lhsT=wbr.t[:, pr, nch * 128:(nch + 1) * 128], rhs=ybT[:, pr, tc * 512:(tc + 1) * 512],
                                    start=(pr == 0), stop=(pr == 1)), reads=[wbr] + ybT_d, writes=[pbr], inc=(pr == 1))
                            t_ = tm.next()
                            S_.op("dve", lambda e, pbr=pbr, s_=s_, t_=t_: e.tensor_tensor(
                                out=t_.t[:], in0=pbr.t[:, :], in1=s_.t[:], op=ALU.mult), reads=[pbr, s_], writes=[t_])
                            S_.op("pool", lambda e, t_=t_, nch=nch, tc=tc: e.tensor_tensor(
                                out=bufB[:, nch, tc * 512:(tc + 1) * 512], in0=bufB[:, nch, tc * 512:(tc + 1) * 512], in1=t_.t[:], op=ALU.add),
                                reads=[t_, bufB_d[tc]], writes=[bufB_d[tc]])
                S_.op("dve", lambda e: e.tensor_tensor(
                    out=wo_pre1.t[:], in0=wo_pre1.t[:],
                    in1=gaBCm.t[:, 512:1024].unsqueeze(1).to_broadcast([128, KC, 512]), op=ALU.mult),
                    reads=[wo_pre1, gaBCm], writes=[wo_pre1])
            S_.barrier()
            dbg("merged", bufB_d, bufB[:], [128, KC, S], BF16)

            with ExitStack() as s2:
                x1 = sb("x1", [128, NT, D], F32, s2)
                x1_d = [T() for _ in range(NT)]
                WT4 = T(sb("wt4", [128, KC, 512], BF16, s2))
                for tt in range(NT):
                    S_.dma("sp", lambda e, tt=tt: e.dma_start(out=x1[:, tt, :], in_=x_d[tt * 128:(tt + 1) * 128, :]), writes=[x1_d[tt]])
                so = s2.enter_context(ExitStack())
                tmo = Rot([T(sb("tmo%d" % i, [128, 512], F32, so)) for i in range(3)])
                wo = [wo_pre0, wo_pre1]
                others = [t for t in WT if t is not wo_pre0 and t is not wo_pre1]
                pre_w = [load_w(wfg_d[:, 0:512], 512, tile=others[0]), load_w(wfu_d[:, 0:512], 512, tile=WT4)]
                wrot = Rot([wo_pre0, wo_pre1, others[0], WT4])

                def outproj_tile(tt):
                    for ch in range(2):
                        po = pall.next()
                        for kc in range(KC):
                            S_.op("pe", lambda e: e.matmul(
                                po.t[:, :], lhsT=bufB[:, kc, tt * 128:(tt + 1) * 128], rhs=wo[ch].t[:, kc, :],
                                start=(kc == 0), stop=(kc == KC - 1)), reads=[wo[ch], bufB_d[tt // 4]], writes=[po], inc=(kc == KC - 1))
                        S_.op("dve", lambda e: e.tensor_tensor(
                            out=x1[:, tt, ch * 512:(ch + 1) * 512], in0=po.t[:, :], in1=x1[:, tt, ch * 512:(ch + 1) * 512], op=ALU.add),
                            reads=[po, x1_d[tt]], writes=[x1_d[tt]])
                    return x1_d[tt], x1[:, tt, :]

                with ExitStack() as sn:
                    norm_to_T(outproj_tile, 1, bufA, bufA_d, 16, 8, sn)
                so.close()
                S_.barrier()
                dbg("x1", x1_d, x1[:], [128, NT, D])
                dbg("h2T", bufA_d, bufA[:], [128, KC, S], BF16)
                sf2 = s2.enter_context(ExitStack())
                tmo = Rot([T(sb("tmf%d" % i, [128, 512], F32, sf2)) for i in range(3)])

                wdt = [T(sb("wd%d" % i, [128, KC, D], BF16, sf2)) for i in range(1)]
                sa = Rot([T(sb("sa%d" % i, [128, 512], F32, sf2)) for i in range(2)])
                groups = [(0, 8), (8, 8), (16, 6)]
                sqf = T(sb("sqF", [128, NT], F32, sf2))

                def final_tile(tt):
                    col = 2 * NT + tt
                    jk = sa.next()
                    S_.op("act", lambda e: e.activation(out=jk.t[:].bitcast(BF16), in_=x1[:, tt, :], func=AF.Square,
                                                        accum_out=ssq.t[:, col:col + 1]), reads=[x1_d[tt]], writes=[jk, ssq])
                    S_.op("act", lambda e: e.activation(out=sqf.t[:, tt:tt + 1], in_=ssq.t[:, col:col + 1], func=AF.Sqrt,
                                                        scale=1.0 / D, bias=EPS), reads=[ssq], writes=[sqf])
                    S_.op("dve", lambda e: e.reciprocal(out=rstd.t[:, col:col + 1], in_=sqf.t[:, tt:tt + 1]), reads=[sqf], writes=[rstd])
                    for ch in range(2):
                        y = tmo.next()
                        S_.op("act", lambda e: e.activation(out=y.t[:], in_=x1[:, tt, ch * 512:(ch + 1) * 512], func=AF.Identity,
                                                            scale=rstd.t[:, col:col + 1]), reads=[x1_d[tt], rstd], writes=[y])
                        S_.op("dve", lambda e: e.tensor_tensor(out=y.t[:], in0=y.t[:], in1=gfinBC.t[:, ch * 512:(ch + 1) * 512],
                                                               op=ALU.mult), reads=[y, gfinBC], writes=[y])
                        S_.dma("sp", lambda e: e.dma_start(out=out_d[tt * 128:(tt + 1) * 128, ch * 512:(ch + 1) * 512], in_=y.t[:]),
                               reads=[y])
                for (f0, nf) in groups:
                    wd = wdt[0]
                    for q4 in range(0, nf, 4):
                        nq = min(4, nf - q4)
                        c0 = (f0 + q4) * 128
                        if pre_w:
                            wg, wu = pre_w
                            pre_w = None
                        else:
                            wg = load_w(wfg_d[:, c0:c0 + nq * 128], nq * 128)
                            wu = load_w(wfu_d[:, c0:c0 + nq * 128], nq * 128)
                        if q4 == 0:
                            S_.dma("pool", lambda e, f0=f0, nf=nf, wd=wd: e.dma_start(
                                out=wd.t[:, 0:nf, :], in_=wfd_d[f0 * 128:(f0 + nf) * 128, :].rearrange("(kc p) n -> p kc n", p=128)),
                                writes=[wd])
                            S_.op("pool", lambda e, nf=nf, wd=wd: e.tensor_tensor(
                                out=wd.t[:, 0:nf, :], in0=wd.t[:, 0:nf, :],
                                in1=gaBCf.t[:, :].unsqueeze(1).to_broadcast([128, nf, D]), op=ALU.mult),
                                reads=[wd, gaBCf], writes=[wd])
                        for jj in range(nq):
                            fl = q4 + jj
                            for tc in range(NC4):
                                pa = pall.next()
                                for kc in range(KC):
                                    S_.op("pe", lambda e, kc=kc, tc=tc, pa=pa, jj=jj, wg=wg: e.matmul(
                                        pa.t[:, :], lhsT=wg.t[:, kc, jj * 128:(jj + 1) * 128], rhs=bufA[:, kc, tc * 512:(tc + 1) * 512],
                                        start=(kc == 0), stop=(kc == KC - 1)), reads=[wg, bufA_d[tc]], writes=[pa], inc=(kc == KC - 1))
                                pu = pall.next()
                                for kc in range(KC):
                                    S_.op("pe", lambda e, kc=kc, tc=tc, pu=pu, jj=jj, wu=wu: e.matmul(
                                        pu.t[:, :], lhsT=wu.t[:, kc, jj * 128:(jj + 1) * 128], rhs=bufA[:, kc, tc * 512:(tc + 1) * 512],
                                        start=(kc == 0), stop=(kc == KC - 1)), reads=[wu, bufA_d[tc]], writes=[pu], inc=(kc == KC - 1))
                                s_ = sa.next()
                                S_.op("act", lambda e, pa=pa, s_=s_: e.activation(out=s_.t[:], in_=pa.t[:, :], func=AF.Silu),
                                      reads=[pa], writes=[s_])
                                S_.op("dve", lambda e, pu=pu, s_=s_, fl=fl, tc=tc: e.tensor_tensor(
                                    out=bufB[:, fl, tc * 512:(tc + 1) * 512], in0=pu.t[:, :], in1=s_.t[:], op=ALU.mult),
                                    reads=[pu, s_], writes=[bufB_d[tc]])
                    for tt in range(NT):
                        for ch in range(2):
                            po = pall.next()
                            for kc in range(nf):
                                S_.op("pe", lambda e, kc=kc, tt=tt, ch=ch, po=po, nf=nf, wd=wd: e.matmul(
                                    po.t[:, :], lhsT=bufB[:, kc, tt * 128:(tt + 1) * 128], rhs=wd.t[:, kc, ch * 512:(ch + 1) * 512],
                                    start=(kc == 0), stop=(kc == nf - 1)), reads=[wd, bufB_d[tt // 4]], writes=[po], inc=(kc == nf - 1))
                            S_.op("dve", lambda e, po=po, tt=tt, ch=ch: e.tensor_tensor(
                                out=x1[:, tt, ch * 512:(ch + 1) * 512], in0=po.t[:, :], in1=x1[:, tt, ch * 512:(ch + 1) * 512], op=ALU.add),
                                reads=[po, x1_d[tt]], writes=[x1_d[tt]])
                        if f0 + nf == NFF:
                            if tt >= 2:
                                final_tile(tt - 2)
                            if tt == NT - 1:
                                final_tile(NT - 2)
                                final_tile(NT - 1)

                dbg("x2", x1_d, x1[:], [128, NT, D])
                sf2.close()
        except _Stop:
            pass
        S_.finish()
        S_.emit()
    return nc, dbg_d


def _consts():
    ident = np.eye(128, dtype=np.float32)
    k = np.arange(128)[:, None]
    q = np.arange(128)[None, :]
    anti = np.where(k >= q, 0.0, NEG).astype(np.float32)
    caus = np.where(k <= q, 0.0, NEG).astype(np.float32)
    mask = np.concatenate([caus, anti, caus, caus, caus, caus], axis=1)
    sel = np.zeros((128, 8 * 128), np.float32)
    for h in range(8):
        sel[h, h * 128:(h + 1) * 128] = 1.0
    pm = np.zeros((128, 128), np.float32)
    cosf = np.ones((128, S), np.float32)
    sinf = np.zeros((128, S), np.float32)
    pos = np.arange(S, dtype=np.float32)
    inv_freq = (np.float32(500000.0) ** (-(np.arange(0, 16, 2, dtype=np.float32)) / np.float32(16))).astype(np.float32)
    ang = (pos[:, None] * inv_freq[None, :]).astype(np.float32)
    cs = np.cos(ang).astype(np.float32).T
    sn = np.sin(ang).astype(np.float32).T
    for hp in range(2):
        for dd in range(16):
            p = hp * 64 + dd
            i = dd % 8
            partner = p + 8 if dd < 8 else p - 8
            pm[partner, p] = 1.0
            cosf[p] = cs[i]
            sinf[p] = -sn[i] if dd < 8 else sn[i]
    ones = np.ones((128, 128), np.float32)
    return dict(k_ident=ident, k_mask=mask, k_sel=sel, k_pm=pm, k_cos=cosf, k_sin=sinf, k_ones=ones)


def _col(v, n):
    return np.ascontiguousarray(np.asarray(v, np.float32).reshape(n, 128).T)


def _prep_inputs(x, c, w_ada, b_ada, g_mix, w_in, b_fgate, w_br_a, w_br_b, w_out,
                 g_ffn, w_ffn_gate, w_ffn_up, w_ffn_down, g_final):
    f = lambda a: np.ascontiguousarray(np.asarray(a, dtype=np.float32))
    x, c = f(x), f(c)
    w_in0 = f(w_in)[0]
    cuts = np.cumsum([512, 512, 512, 8, 768, 768, 768, 1024, 1024])[:-1]
    qa, ka, va, fa, qb, kb, vb, ga, gb = [np.ascontiguousarray(p) for p in np.split(w_in0, cuts, axis=1)]
    b_ada0 = f(b_ada)[0]
    shared = dict(
        w_ada=f(w_ada)[0], b_ada_col=_col(b_ada0, 48),
        b_gam_bc=np.ascontiguousarray(np.broadcast_to(b_ada0[2 * D:3 * D], (128, D))),
        b_gaf_bc=np.ascontiguousarray(np.broadcast_to(b_ada0[5 * D:6 * D], (128, D))),
        g_mix_col=_col(f(g_mix)[0], KC), g_ffn_col=_col(f(g_ffn)[0], KC),
        g_fin_bc=np.ascontiguousarray(np.broadcast_to(f(g_final), (128, D))),
        b_fg_col=np.ascontiguousarray(f(b_fgate)[0].reshape(8, 1)),
        w_qa=qa, w_ka=ka, w_va=va, w_fa=np.ascontiguousarray(fa.reshape(KC, 128, 8).transpose(1, 0, 2).reshape(128, KC * 8)), w_qb=qb, w_kb=kb, w_vb=vb, w_ga=ga, w_gb=gb,
        w_br_a=f(w_br_a)[0], w_br_b=f(w_br_b)[0], w_out=f(w_out)[0],
        w_ffn_gate=f(w_ffn_gate)[0], w_ffn_up=f(w_ffn_up)[0], w_ffn_down=f(w_ffn_down)[0],
    )
    shared.update(_consts())
    in_maps = []
    for b in range(8):
        m = dict(shared)
        m["x"] = np.ascontiguousarray(x[b])
        m["c_col"] = _col(c[b], KC)
        in_maps.append(m)
    return in_maps


_NC_CACHE = {}


def kernel(**inputs):
    in_maps = _prep_inputs(**inputs)
    if "nc" not in _NC_CACHE:
        _NC_CACHE["nc"] = build_program()[0]
    nc = _NC_CACHE["nc"]
    res = run_bass_kernel_spmd(nc, in_maps, core_ids=list(range(8)))
    out = np.stack([np.asarray(r["out"], dtype=np.float32).reshape(S, D) for r in res.results], axis=0)
    return out
```

```python
import numpy as np
from contextlib import ExitStack
import concourse.bass as bass
import concourse.mybir as mybir
from concourse.bass_utils import run_bass_kernel_spmd

F32 = mybir.dt.float32
BF16 = mybir.dt.bfloat16
AF = mybir.ActivationFunctionType
ALU = mybir.AluOpType

D = 1024
S = 2048
NT = 16
NC4 = 4
KC = 8
DFF = 2816
NFF = 22
EPS = 1e-6
NEG = -30000.0
SCALE = 0.125
DIL = (1, 4, 16)
ENG = ("pe", "act", "dve", "pool", "sp")


class T:
    __slots__ = ("t", "w", "r", "ds", "excl")

    def __init__(self, t=None, excl=False):
        self.t = t
        self.excl = excl
        self.w = None
        self.r = {}
        self.ds = None


class Rec:
    def __init__(self):
        self.call = None

    def __getattr__(self, name):
        def f(*a, **k):
            self.call = (name, a, k)
        return f


def _record(fn):
    r = Rec()
    fn(r)
    assert r.call is not None
    return r.call


class Sched:
    def __init__(self, nc, es, n_dma=24):
        self.nc = nc
        self.sem = {e: es.enter_context(nc.semaphore("s_" + e)) for e in ENG}
        self.cnt = {e: 0 for e in ENG}
        self.es = es
        self.dsem = []
        self.dcnt = []
        self.dq = []
        self.inflight = {"pool": [], "sp": []}
        self.max_inflight = {"pool": 2, "sp": 4}
        self.streams = {e: [] for e in ENG}
        self.waited = {e: {} for e in ENG}
        self.stopped = False

    def _need(self, eng, deps):
        for key, val in deps:
            if key == eng and eng == "pe":
                continue
            if self.waited[eng].get(key, 0) >= val:
                continue
            self.waited[eng][key] = val
            self.streams[eng].append(("w", key, val))

    @staticmethod
    def _deps(reads, writes, eng=None):
        deps = []
        for t in reads:
            if t.w is not None:
                deps.append(t.w)
            if t.excl:
                deps.extend((k, v) for k, v in t.r.items() if k != eng)
        for t in writes:
            if t.w is not None:
                deps.append(t.w)
            deps.extend(t.r.items())
        return deps

    def op(self, eng, fn, reads=(), writes=(), inc=True):
        if self.stopped:
            return
        self._need(eng, self._deps(reads, writes, eng))
        val = self.cnt[eng] + 1
        if inc:
            self.cnt[eng] = val
        self.streams[eng].append(("o", _record(fn), inc))
        for t in reads:
            if t.r.get(eng, 0) < val:
                t.r[eng] = val
        for t in writes:
            t.w = (eng, val)
            t.r = {}

    def dma(self, q, fn, reads=(), writes=()):
        if self.stopped:
            return
        own = writes[0] if len(writes) else reads[0]
        if own.ds is None:
            own.ds = len(self.dsem)
            self.dsem.append(self.es.enter_context(self.nc.semaphore("d%d" % own.ds)))
            self.dcnt.append(0)
            self.dq.append(q)
        i = own.ds
        assert self.dq[i] == q
        key = ("d", i)
        deps = self._deps(reads, writes)
        if self.dcnt[i] > 0:
            deps.append((key, self.dcnt[i]))
        fl = self.inflight[q]
        while len(fl) >= self.max_inflight[q]:
            deps.append(fl.pop(0))
        self._need(q, deps)
        self.dcnt[i] += 16
        val = self.dcnt[i]
        fl.append((key, val))
        self.streams[q].append(("d", _record(fn), i))
        for t in reads:
            t.r[key] = val
        for t in writes:
            t.w = (key, val)
            t.r = {}

    def barrier(self):
        if self.stopped:
            return
        edeps = [(e, self.cnt[e]) for e in ENG if self.cnt[e] > 0]
        for q in ("sp", "pool"):
            ddeps = [(("d", i), v) for i, v in enumerate(self.dcnt) if v > 0 and self.dq[i] == q]
            self._need(q, edeps + ddeps)
            self.cnt[q] += 1
            self.streams[q].append(("o", ("nop", (), {}), True))
        edeps = [(e, self.cnt[e]) for e in ENG if self.cnt[e] > 0]
        for e in ENG:
            self._need(e, edeps)

    def finish(self):
        ddeps = [(("d", i), v) for i, v in enumerate(self.dcnt) if v > 0 and self.dq[i] == "pool"]
        if ddeps:
            self._need("pool", ddeps)
            self.cnt["pool"] += 1
            self.streams["pool"].append(("o", ("nop", (), {}), True))
        for i, v in enumerate(self.dcnt):
            if v > 0 and self.dq[i] == "sp":
                self._need("sp", [(("d", i), v)])
        for e in ENG:
            self._need("sp", [(e, self.cnt[e])] if self.cnt[e] > 0 else [])

    def emit(self):
        nc = self.nc
        needed = {e: set() for e in ENG}
        for e in ENG:
            for item in self.streams[e]:
                if item[0] == "w" and isinstance(item[1], str):
                    needed[item[1]].add(item[2])
        rank = {e: {v: i + 1 for i, v in enumerate(sorted(needed[e]))} for e in ENG}
        with nc.Block() as block:
            def mk(e):
                def body(eng):
                    prov = 0
                    for item in self.streams[e]:
                        if item[0] == "w":
                            key = item[1]
                            if isinstance(key, str):
                                eng.wait_ge(self.sem[key], rank[key][item[2]])
                            else:
                                eng.wait_ge(self.dsem[key[1]], item[2])
                        elif item[0] == "o":
                            c = item[1]
                            ins = getattr(eng, c[0])(*c[1], **c[2])
                            if item[2]:
                                prov += 1
                                if prov in needed[e]:
                                    ins.then_inc(self.sem[e], 1)
                        else:
                            c = item[1]
                            getattr(eng, c[0])(*c[1], **c[2]).then_inc(self.dsem[item[2]], 16)
                    assert prov == self.cnt[e], (e, prov, self.cnt[e])
                return body
            block.tensor(mk("pe"))
            block.scalar(mk("act"))
            block.vector(mk("dve"))
            block.gpsimd(mk("pool"))
            block.sync(mk("sp"))


class Rot:
    def __init__(self, items):
        self.items = items
        self.i = 0

    def next(self):
        x = self.items[self.i]
        self.i = (self.i + 1) % len(self.items)
        return x


def tok_view(ap2d, g, j):
    if g == 0:
        return ap2d[:, j * 512:(j + 1) * 512]
    if g == 1:
        return ap2d.rearrange("p (i r) -> p r i", r=4)[:, j, :]
    return ap2d.rearrange("p (i r) -> p r i", r=16)[:, 4 * j:4 * j + 4, :]


def cls_out(ap2d, g, tc):
    dd = DIL[g]
    if g == 0:
        return ap2d[:, tc * 512:(tc + 1) * 512]
    n = 512 // dd
    return ap2d.rearrange("p (r i) -> p r i", r=dd)[:, :, n * tc:n * (tc + 1)]


def nat_in(ap2d, g):
    if g == 0:
        return ap2d
    return ap2d.rearrange("p (i r) -> p r i", r=DIL[g])


def chunk_view(ap2d, g):
    if g == 2:
        return ap2d.rearrange("p (a b) -> p a b", a=4)
    return ap2d


class _Stop(Exception):
    pass


def build_program(debug=(), stop=None):
    nc = bass.Bass("TRN2", target_bir_lowering=False)

    def din(name, shape):
        return nc.dram_tensor(name, list(shape), F32, kind="ExternalInput").ap()

    x_d = din("x", [S, D])
    ccol_d = din("c_col", [128, KC])
    wada_d = din("w_ada", [D, 6 * D])
    badac_d = din("b_ada_col", [128, 48])
    bgam_d = din("b_gam_bc", [128, D])
    bgaf_d = din("b_gaf_bc", [128, D])
    gmix_d = din("g_mix_col", [128, KC])
    gffn_d = din("g_ffn_col", [128, KC])
    gfin_d = din("g_fin_bc", [128, D])
    bfg_d = din("b_fg_col", [8, 1])
    wqa_d = din("w_qa", [D, 512])
    wka_d = din("w_ka", [D, 512])
    wva_d = din("w_va", [D, 512])
    wfa_d = din("w_fa", [128, KC * 8])
    wqb_d = din("w_qb", [D, 768])
    wkb_d = din("w_kb", [D, 768])
    wvb_d = din("w_vb", [D, 768])
    wga_d = din("w_ga", [D, D])
    wgb_d = din("w_gb", [D, D])
    wbra_d = din("w_br_a", [512, D])
    wbrb_d = din("w_br_b", [256, D])
    wout_d = din("w_out", [D, D])
    wfg_d = din("w_ffn_gate", [D, DFF])
    wfu_d = din("w_ffn_up", [D, DFF])
    wfd_d = din("w_ffn_down", [DFF, D])
    ident_d = din("k_ident", [128, 128])
    mask_d = din("k_mask", [128, 768])
    sel_d = din("k_sel", [128, 8 * 128])
    pm_d = din("k_pm", [128, 128])
    cos_d = din("k_cos", [128, S])
    sin_d = din("k_sin", [128, S])
    ones_d = din("k_ones", [128, 128])
    out_d = nc.dram_tensor("out", [S, D], F32, kind="ExternalOutput").ap()
    dbg_d = {}

    with ExitStack() as es:
        S_ = Sched(nc, es)

        def sb(name, shape, dt, scope=es):
            return scope.enter_context(nc.sbuf_tensor(name, list(shape), dt))

        PB = [T(es.enter_context(nc.psum_tensor("pb%d" % i, [128, 512], F32)), excl=True) for i in range(8)]
        PTh = [PB[6], PB[7]]
        PTv = [PB[6].t[:, :].bitcast(BF16), PB[7].t[:, :].bitcast(BF16)]
        pall = Rot(PB)

        bufA = sb("bufA", [128, KC, S], BF16)
        bufB = sb("bufB", [128, KC, S], BF16)
        bufA_d = [T() for _ in range(NC4)]
        bufB_d = [T() for _ in range(NC4)]
        WT = [T(sb("wt%d" % i, [128, KC, 512], BF16)) for i in range(3)]
        wrot = Rot(WT)
        ident_bf = T(sb("ident_bf", [128, 128], BF16))
        ident_f = T(sb("ident_f", [128, 128], F32))
        maskX = T(sb("maskX", [128, 768], BF16))
        selones = T(sb("selones", [128, 8 * 128], BF16))
        pm_bf = T(sb("pm_bf", [128, 128], BF16))
        ones_bf = T(sb("ones_bf", [128, 128], BF16))
        gaBCm = T(sb("gaBCm", [128, D], F32))
        gaBCf = T(sb("gaBCf", [128, D], F32))
        gfinBC = T(sb("gfinBC", [128, D], F32))
        modT = T(sb("modT", [128, 32], F32))
        gsc = T(sb("gsc", [128, 16], F32))
        small = T(sb("small", [128, 64], F32))
        badac = T(sb("badac", [128, 48], F32))
        cs_bf = T(sb("cs_bf", [128, KC], BF16))
        csb_bf = T(sb("csb_bf", [128, KC, 128], BF16))
        ssq = T(sb("ssq", [128, 3 * NT], F32))
        rstd = T(sb("rstd", [128, 3 * NT], F32))

        def load_w(src_ap, ncols, kc=KC, tile=None):
            wt = wrot.next() if tile is None else tile
            S_.dma("pool", lambda e, wt=wt: e.dma_start(
                out=wt.t[:, 0:kc, 0:ncols], in_=src_ap.rearrange("(kc p) n -> p kc n", p=128)), writes=[wt])
            return wt

        def dbg(name, tiles, ap, shape, dt=F32):
            if name not in debug:
                if stop == name:
                    S_.stopped = True
                return
            d = nc.dram_tensor("dbg_" + name, list(shape), dt, kind="ExternalOutput").ap()
            dbg_d[name] = d
            S_.dma("sp", lambda e: e.dma_start(out=d, in_=ap), reads=[T()] + list(tiles))
            if stop == name:
                S_.stopped = True

        def chk(name):
            if stop == name:
                S_.stopped = True

        try:
            for (tl, src) in ((ident_bf, ident_d), (maskX, mask_d), (selones, sel_d), (pm_bf, pm_d), (ones_bf, ones_d)):
                S_.dma("pool", lambda e, tl=tl, src=src: e.dma_start(out=tl.t[:], in_=src), writes=[tl])
            S_.dma("sp", lambda e: e.dma_start(out=ident_f.t[:], in_=ident_d), writes=[ident_f])
            S_.dma("sp", lambda e: e.dma_start(out=small.t[:, 0:8], in_=ccol_d), writes=[small])
            S_.dma("sp", lambda e: e.dma_start(out=small.t[:, 8:16], in_=gmix_d), writes=[small])
            S_.dma("sp", lambda e: e.dma_start(out=small.t[:, 16:24], in_=gffn_d), writes=[small])
            S_.dma("sp", lambda e: e.dma_start(out=badac.t[:], in_=badac_d), writes=[badac])
            S_.op("act", lambda e: e.activation(out=cs_bf.t[:], in_=small.t[:, 0:8], func=AF.Silu), reads=[small], writes=[cs_bf])
            S_.op("dve", lambda e: e.tensor_copy(out=csb_bf.t[:], in_=cs_bf.t[:].unsqueeze(2).to_broadcast([128, KC, 128])),
                  reads=[cs_bf], writes=[csb_bf])

            def ada_cols(sec, dst0):
                pb = pall.next()
                for half in range(2):
                    wt = load_w(wada_d[:, sec * D + half * 512: sec * D + (half + 1) * 512], 512)
                    for j in range(4):
                        col = half * 4 + j
                        for kc in range(KC):
                            S_.op("pe", lambda e, wt=wt, j=j, kc=kc, col=col, pb=pb: e.matmul(
                                pb.t[:, col:col + 1], lhsT=wt.t[:, kc, j * 128:(j + 1) * 128], rhs=cs_bf.t[:, kc:kc + 1],
                                start=(kc == 0), stop=(kc == KC - 1)),
                                reads=[wt, cs_bf], writes=[pb], inc=(kc == KC - 1))
                S_.op("dve", lambda e, pb=pb: e.tensor_tensor(out=modT.t[:, dst0:dst0 + 8], in0=pb.t[:, 0:8],
                                                             in1=badac.t[:, sec * 8:(sec + 1) * 8], op=ALU.add),
                      reads=[pb, badac], writes=[modT])

            def ada_rows(sec, dstT):
                for half in range(2):
                    wt = load_w(wada_d[:, sec * D + half * 512: sec * D + (half + 1) * 512], 512)
                    pb = pall.next()
                    for kc in range(KC):
                        S_.op("pe", lambda e, wt=wt, kc=kc, pb=pb: e.matmul(
                            pb.t[:, :], lhsT=csb_bf.t[:, kc, :], rhs=wt.t[:, kc, :], start=(kc == 0), stop=(kc == KC - 1)),
                            reads=[wt, csb_bf], writes=[pb], inc=(kc == KC - 1))
                    S_.op("dve", lambda e, pb=pb, half=half: e.tensor_tensor(
                        out=dstT.t[:, half * 512:(half + 1) * 512], in0=pb.t[:, :], in1=dstT.t[:, half * 512:(half + 1) * 512],
                        op=ALU.add), reads=[pb, dstT], writes=[dstT])

            def ada_cols_half(sec, dst0, half, tile, ps=None):
                pb = (ps or pall).next()
                wt = load_w(wada_d[:, sec * D + half * 512: sec * D + (half + 1) * 512], 512, tile=tile)
                for j in range(4):
                    for kc in range(KC):
                        S_.op("pe", lambda e, wt=wt, j=j, kc=kc, pb=pb: e.matmul(
                            pb.t[:, j:j + 1], lhsT=wt.t[:, kc, j * 128:(j + 1) * 128], rhs=cs_bf.t[:, kc:kc + 1],
                            start=(kc == 0), stop=(kc == KC - 1)),
                            reads=[wt, cs_bf], writes=[pb], inc=(kc == KC - 1))
                c0 = dst0 + half * 4
                b0 = sec * 8 + half * 4
                S_.op("dve", lambda e, pb=pb: e.tensor_tensor(out=modT.t[:, c0:c0 + 4], in0=pb.t[:, 0:4],
                                                             in1=badac.t[:, b0:b0 + 4], op=ALU.add),
                      reads=[pb, badac], writes=[modT])

            def ada_rows_half(sec, dstT, half, tile, ps=None):
                wt = load_w(wada_d[:, sec * D + half * 512: sec * D + (half + 1) * 512], 512, tile=tile)
                pb = (ps or pall).next()
                for kc in range(KC):
                    S_.op("pe", lambda e, wt=wt, kc=kc, pb=pb: e.matmul(
                        pb.t[:, :], lhsT=csb_bf.t[:, kc, :], rhs=wt.t[:, kc, :], start=(kc == 0), stop=(kc == KC - 1)),
                        reads=[wt, csb_bf], writes=[pb], inc=(kc == KC - 1))
                S_.op("dve", lambda e, pb=pb, half=half: e.tensor_tensor(
                    out=dstT.t[:, half * 512:(half + 1) * 512], in0=pb.t[:, :], in1=dstT.t[:, half * 512:(half + 1) * 512],
                    op=ALU.add), reads=[pb, dstT], writes=[dstT])

            def make_gsc(which):
                sc0 = 8 if which == 0 else 24
                g0 = 8 if which == 0 else 16
                S_.op("dve", lambda e: e.scalar_tensor_tensor(
                    out=gsc.t[:, which * 8:(which + 1) * 8], in0=modT.t[:, sc0:sc0 + 8], scalar=1.0,
                    in1=small.t[:, g0:g0 + 8], op0=ALU.add, op1=ALU.mult), reads=[modT, small], writes=[gsc])

            def ada_first():
                ada_cols(0, 0)
                ada_cols(1, 8)
                make_gsc(0)

            def norm_to_T(src_tile_fn, nidx, dstbuf, dst_d, sh0, gs0, scope, hook=None, after_chunk=None):
                junk = T(sb("junk%d" % nidx, [128, D], BF16, scope))
                xs = [T(sb("xs%d_%d" % (nidx, i), [128, D], BF16, scope)) for i in range(8)]
                sq = T(sb("sq%d" % nidx, [128, 8], F32, scope))
                sq_c = [T() for _ in range(8)]
                ssq_c = [T() for _ in range(NT)]
                rstd_c = [T() for _ in range(NT)]

                def stats(tc):
                    for i in range(4):
                        tt = tc * 4 + i
                        col = nidx * NT + tt
                        b = (tc % 2) * 4 + i
                        xt, xap = src_tile_fn(tt)
                        S_.op("act", lambda e: e.activation(
                            out=junk.t[:], in_=xap, func=AF.Square, accum_out=ssq.t[:, col:col + 1]),
                            reads=[xt], writes=[junk, ssq_c[tt]])
                        S_.op("act", lambda e: e.activation(
                            out=sq.t[:, b:b + 1], in_=ssq.t[:, col:col + 1], func=AF.Sqrt, scale=1.0 / D, bias=EPS),
                            reads=[ssq_c[tt]], writes=[sq_c[b]])
                        S_.op("dve", lambda e: e.reciprocal(out=rstd.t[:, col:col + 1], in_=sq.t[:, b:b + 1]),
                              reads=[sq_c[b]], writes=[rstd_c[tt]])
                        S_.op("dve", lambda e: e.tensor_scalar_mul(
                            out=xs[b].t[:], in0=xap, scalar1=rstd.t[:, col:col + 1]),
                            reads=[xt, rstd_c[tt]], writes=[xs[b]])

                def tr_evac(tc):
                    for c in range(KC):
                        h = c % 2
                        pt = PTh[h]
                        for i in range(4):
                            b = (tc % 2) * 4 + i
                            S_.op("pe", lambda e: e.transpose(
                                PTv[h][:, i * 128:(i + 1) * 128], xs[b].t[:, c * 128:(c + 1) * 128], ident_bf.t[:]),
                                reads=[xs[b], ident_bf], writes=[pt], inc=(i == 3))
                        if h == 0:
                            S_.op("act", lambda e: e.activation(
                                out=dstbuf[:, c, tc * 512:(tc + 1) * 512], in_=PTv[h][:, 0:512], func=AF.Identity,
                                scale=gsc.t[:, gs0 + c:gs0 + c + 1], bias=modT.t[:, sh0 + c:sh0 + c + 1]),
                                reads=[pt, gsc, modT], writes=[dst_d[tc]])
                        else:
                            S_.op("dve", lambda e: e.tensor_scalar(
                                out=dstbuf[:, c, tc * 512:(tc + 1) * 512], in0=PTv[h][:, 0:512],
                                scalar1=gsc.t[:, gs0 + c:gs0 + c + 1], scalar2=modT.t[:, sh0 + c:sh0 + c + 1],
                                op0=ALU.mult, op1=ALU.add), reads=[pt, gsc, modT], writes=[dst_d[tc]])

                stats(0)
                stats(1)
                if hook is not None:
                    hook()
                for tc in range(NC4):
                    tr_evac(tc)
                    if tc + 2 < NC4:
                        stats(tc + 2)
                    if after_chunk is not None:
                        after_chunk(tc)

            sf = es.enter_context(ExitStack())
            Vall = T(sb("Vall", [128, NT, 512], BF16, sf))
            kz = [T(sb("kzA%d" % i, [128, S], BF16, sf)) for i in range(2)]
            qz = [T(sb("qzA%d" % i, [128, S], BF16, sf)) for i in range(2)]
            onesr = T(sb("onesr", [8, 512], F32, sf))
            Fb8 = T(sb("Fb8", [128, S], BF16, sf))
            wv_box = []

            def ada_first_and_wv():
                ada_first()
                wv_box.append(load_w(wva_d, 512, tile=WT[2]))
                S_.op("pool", lambda e: e.memset(qz[0].t[:], 0.0), writes=[qz[0]])
                S_.op("pool", lambda e: e.memset(qz[1].t[:], 0.0), writes=[qz[1]])
                S_.op("pool", lambda e: e.memset(Fb8.t[:], 0.0), writes=[Fb8])
                S_.op("pool", lambda e: e.memset(kz[0].t[:], 0.0), writes=[kz[0]])
                S_.op("pool", lambda e: e.memset(kz[1].t[:], 0.0), writes=[kz[1]])
                S_.op("pool", lambda e: e.memset(kz[0].t[64:65, :], 1.0), writes=[kz[0]])
                S_.op("pool", lambda e: e.memset(kz[1].t[0:1, :], 1.0), writes=[kz[1]])
                S_.op("pool", lambda e: e.memset(onesr.t[:], 1.0), writes=[onesr])

            def v_chunk(tc):
                wv = wv_box[0]
                for tt in range(4 * tc, 4 * tc + 4):
                    pb = pall.next()
                    for kc in range(KC):
                        S_.op("pe", lambda e, kc=kc, tt=tt, pb=pb: e.matmul(
                            pb.t[:, :], lhsT=bufA[:, kc, tt * 128:(tt + 1) * 128], rhs=wv.t[:, kc, :],
                            start=(kc == 0), stop=(kc == KC - 1)), reads=[wv, bufA_d[tc]], writes=[pb], inc=(kc == KC - 1))
                    S_.op("act", lambda e, tt=tt, pb=pb: e.activation(out=Vall.t[:, tt, :], in_=pb.t[:, :], func=AF.Copy),
                          reads=[pb], writes=[Vall])

            with ExitStack() as sc1:
                xin = [T(sb("xin%d" % i, [128, D], F32, sc1)) for i in range(4)]
                xrot = Rot(xin)

                def x_from_hbm(tt):
                    xt = xrot.next()
                    S_.dma("sp", lambda e, xt=xt, tt=tt: e.dma_start(out=xt.t[:], in_=x_d[tt * 128:(tt + 1) * 128, :]), writes=[xt])
                    return xt, xt.t[:]
                norm_to_T(x_from_hbm, 0, bufA, bufA_d, 0, 0, sc1, hook=ada_first_and_wv, after_chunk=v_chunk)
            S_.barrier()
            S_.dma("sp", lambda e: e.dma_start(out=gfinBC.t[:], in_=gfin_d), writes=[gfinBC])
            S_.dma("sp", lambda e: e.dma_start(out=gaBCm.t[:], in_=bgam_d), writes=[gaBCm])
            S_.dma("sp", lambda e: e.dma_start(out=gaBCf.t[:], in_=bgaf_d), writes=[gaBCf])
            dbg("hT", bufA_d, bufA[:], [128, KC, S], BF16)

            deferred = [lambda t, p: ada_rows_half(2, gaBCm, 0, t, p), lambda t, p: ada_rows_half(2, gaBCm, 1, t, p),
                        lambda t, p: ada_cols_half(3, 16, 0, t, p), lambda t, p: ada_cols_half(3, 16, 1, t, p),
                        lambda t, p: ada_cols_half(4, 24, 0, t, p), lambda t, p: (ada_cols_half(4, 24, 1, t, p), make_gsc(1)),
                        lambda t, p: ada_rows_half(5, gaBCf, 0, t, p), lambda t, p: ada_rows_half(5, gaBCf, 1, t, p)]

            STB = Rot([PB[0], PB[1], PB[2], PB[7]])
            ACC = Rot([(PB[3], PB[4]), (PB[5], PB[6])])
            if True:
                yaT = sb("yaT", [128, 4, S], BF16, sf)
                yaT_d = [T() for _ in range(NC4)]
                wbrA = T(sb("wbrA", [128, 4, D], BF16, sf))
                PTb = Rot([T(sb("PTbA%d" % i, [128, 512], BF16, sf)) for i in range(3)])
                rec = Rot([T(sb("recA%d" % i, [128, 512], F32, sf)) for i in range(2)])
                Grow = T(sb("Grow", [8, S], F32, sf))
                spr = T(sb("spr", [8, S], F32, sf))
                Gtok = T(sb("Gtok", [128, NT, 8], F32, sf))
                nbf = T(sb("nbf", [8, 2], F32, sf))
                etmp = T(sb("etmp", [8, 512], F32, sf))

                wf = WT[1]
                S_.dma("pool", lambda e: e.dma_start(out=wf.t[:, :, 0:8], in_=wfa_d.rearrange("p (kc n) -> p kc n", n=8)), writes=[wf])
                wq = load_w(wqa_d, 512, tile=WT[0])
                S_.dma("sp", lambda e: e.dma_start(out=nbf.t[:, 0:1], in_=bfg_d), writes=[nbf])
                S_.op("dve", lambda e: e.tensor_scalar_mul(out=nbf.t[:, 1:2], in0=nbf.t[:, 0:1], scalar1=-1.0),
                      reads=[nbf], writes=[nbf])

                for tc in range(NC4):
                    pb = pall.next()
                    for kc in range(KC):
                        S_.op("pe", lambda e, kc=kc, tc=tc, pb=pb: e.matmul(
                            pb.t[0:8, :], lhsT=wf.t[:, kc, 0:8], rhs=bufA[:, kc, tc * 512:(tc + 1) * 512],
                            start=(kc == 0), stop=(kc == KC - 1)), reads=[wf, bufA_d[tc]], writes=[pb], inc=(kc == KC - 1))
                    S_.op("act", lambda e, pb=pb: e.activation(out=etmp.t[:], in_=pb.t[0:8, :], func=AF.Exp, scale=-1.0,
                                                               bias=nbf.t[:, 1:2]), reads=[pb, nbf], writes=[etmp])
                    S_.op("act", lambda e, tc=tc: e.activation(out=spr.t[:, tc * 512:(tc + 1) * 512], in_=etmp.t[:], func=AF.Ln,
                                                               bias=1.0), reads=[etmp], writes=[spr])
                    if tc == 0:
                        S_.op("dve", lambda e: e.tensor_tensor_scan(out=Grow.t[:, 0:512], data0=onesr.t[:], data1=spr.t[:, 0:512],
                                                                    initial=0.0, op0=ALU.mult, op1=ALU.add),
                              reads=[onesr, spr], writes=[Grow])
                    else:
                        S_.op("dve", lambda e, tc=tc: e.tensor_tensor_scan(
                            out=Grow.t[:, tc * 512:(tc + 1) * 512], data0=onesr.t[:], data1=spr.t[:, tc * 512:(tc + 1) * 512],
                            initial=Grow.t[:, tc * 512 - 1:tc * 512], op0=ALU.mult, op1=ALU.add),
                            reads=[onesr, spr, Grow], writes=[Grow])
                S_.op("dve", lambda e: e.tensor_scalar_mul(out=Fb8.t[0:8, :], in0=Grow.t[:], scalar1=-8.0),
                      reads=[Grow], writes=[Fb8])
                pb = pall.next()
                for tt in range(NT):
                    S_.op("pe", lambda e, tt=tt, pb=pb: e.transpose(pb.t[:, tt * 8:(tt + 1) * 8], Grow.t[0:8, tt * 128:(tt + 1) * 128],
                                                                   ident_f.t[0:8, 0:8]),
                          reads=[Grow, ident_f], writes=[pb], inc=(tt == NT - 1))
                S_.op("dve", lambda e, pb=pb: e.tensor_copy(out=Gtok.t[:].rearrange("p a b -> p (a b)"), in_=pb.t[:, 0:128]),
                      reads=[pb], writes=[Gtok])
                dbg("Grow", [Grow], Grow.t[:], [8, S])

                wk = load_w(wka_d, 512, tile=WT[1])
                S_.dma("pool", lambda e: e.dma_start(out=wbrA.t[:], in_=wbra_d.rearrange("(pr p) n -> p pr n", p=128)), writes=[wbrA])

                for pr in range(4):
                    for tc in range(NC4):
                        pb = pall.next()
                        for kc in range(KC):
                            S_.op("pe", lambda e, kc=kc, tc=tc, pb=pb, pr=pr: e.matmul(
                                pb.t[:, :], lhsT=wq.t[:, kc, pr * 128:(pr + 1) * 128], rhs=bufA[:, kc, tc * 512:(tc + 1) * 512],
                                start=(kc == 0), stop=(kc == KC - 1)), reads=[wq, bufA_d[tc]], writes=[pb], inc=(kc == KC - 1))
                        for hp in range(2):
                            S_.op("act", lambda e, tc=tc, pb=pb, hp=hp: e.activation(
                                out=qz[hp].t[hp * 64:(hp + 1) * 64, tc * 512:(tc + 1) * 512], in_=pb.t[hp * 64:(hp + 1) * 64, :],
                                func=AF.Copy), reads=[pb], writes=[qz[hp]])
                        pb = pall.next()
                        for kc in range(KC):
                            S_.op("pe", lambda e, kc=kc, tc=tc, pb=pb, pr=pr: e.matmul(
                                pb.t[:, :], lhsT=wk.t[:, kc, pr * 128:(pr + 1) * 128], rhs=bufA[:, kc, tc * 512:(tc + 1) * 512],
                                start=(kc == 0), stop=(kc == KC - 1)), reads=[wk, bufA_d[tc]], writes=[pb], inc=(kc == KC - 1))
                        for hp in range(2):
                            S_.op("dve", lambda e, tc=tc, pb=pb, hp=hp: e.tensor_copy(
                                out=kz[hp].t[hp * 64:(hp + 1) * 64, tc * 512:(tc + 1) * 512], in_=pb.t[hp * 64:(hp + 1) * 64, :]),
                                reads=[pb], writes=[kz[hp]])
                        for hp in range(2):
                            h = pr * 2 + hp
                            ar = 64 if hp == 0 else 0
                            pb = pall.next()
                            S_.op("pe", lambda e, pb=pb, h=h, tc=tc: e.matmul(
                                pb.t[:, :], lhsT=selones.t[:, h * 128:(h + 1) * 128], rhs=Fb8.t[:, tc * 512:(tc + 1) * 512],
                                start=True, stop=True), reads=[selones, Fb8], writes=[pb])
                            S_.op("act", lambda e, pb=pb, hp=hp, ar=ar, tc=tc: e.activation(
                                out=qz[hp].t[ar:ar + 1, tc * 512:(tc + 1) * 512], in_=pb.t[ar:ar + 1, :], func=AF.Copy),
                                reads=[pb], writes=[qz[hp]])
                    if pr == 3:
                        wga_pre = [load_w(wga_d[:, 0:512], 512, tile=WT[0]), load_w(wga_d[:, 512:1024], 512, tile=WT[1])]
                    its = []
                    for hp in range(2):
                        for g in range(4):
                            for kb in range(4 * g + 4):
                                its.append(dict(hp=hp, g=g, kb=kb, nkb=4 * g + 4))

                    def qk(it):
                        hp, g, kb = it["hp"], it["g"], it["kb"]
                        if kb == 0:
                            it["acc"] = ACC.next()
                        else:
                            it["acc"] = it["prev"]["acc"]
                        c0 = max(0, kb - 4 * g) * 128
                        q0 = g * 512 + c0
                        st = STB.next()
                        it["st"], it["c0"] = st, c0
                        diag = kb >= 4 * g
                        S_.op("pe", lambda e: e.matmul(
                            st.t[:, c0:512], lhsT=kz[hp].t[:, kb * 128:(kb + 1) * 128], rhs=qz[hp].t[:, q0:(g + 1) * 512],
                            start=True, stop=(not diag)), reads=[kz[hp], qz[hp]], writes=[st], inc=(not diag))
                        if diag:
                            S_.op("pe", lambda e: e.matmul(
                                st.t[:, c0:c0 + 128], lhsT=ident_bf.t[:], rhs=maskX.t[:, 0:128], start=False, stop=True),
                                reads=[ident_bf, maskX], writes=[st], inc=True)

                    def ex_pv(it):
                        hp, g, kb, nkb = it["hp"], it["g"], it["kb"], it["nkb"]
                        h = pr * 2 + hp
                        st, c0 = it["st"], it["c0"]
                        num, den = it["acc"]
                        pt = PTb.next()
                        S_.op("act", lambda e: e.activation(
                            out=pt.t[:, c0:512], in_=st.t[:, c0:512], func=AF.Exp, scale=SCALE, bias=Gtok.t[:, kb, h:h + 1]),
                            reads=[st, Gtok], writes=[pt])
                        S_.op("pe", lambda e: e.matmul(
                            num.t[:, c0:512], lhsT=Vall.t[:, kb, pr * 128:(pr + 1) * 128], rhs=pt.t[:, c0:512],
                            start=(kb == 0), stop=(kb == nkb - 1), skip_group_check=True),
                            reads=[Vall, pt], writes=[num], inc=False)
                        S_.op("pe", lambda e: e.matmul(
                            den.t[:, c0:512], lhsT=ones_bf.t[:], rhs=pt.t[:, c0:512],
                            start=(kb == 0), stop=(kb == nkb - 1), skip_group_check=True),
                            reads=[ones_bf, pt], writes=[den], inc=True)
                        if kb == nkb - 1:
                            lanes = slice(hp * 64, (hp + 1) * 64)
                            rc = rec.next()
                            S_.op("dve", lambda e: e.reciprocal(out=rc.t[lanes, :], in_=den.t[lanes, :]), reads=[den], writes=[rc])
                            S_.op("dve", lambda e: e.tensor_tensor(
                                out=yaT[lanes, pr, g * 512:(g + 1) * 512], in0=num.t[lanes, :], in1=rc.t[lanes, :], op=ALU.mult),
                                reads=[num, rc], writes=[yaT_d[g]])

                    for i, it in enumerate(its):
                        it["prev"] = its[i - 1] if i > 0 else None
                    AH = 2
                    for i in range(AH):
                        qk(its[i])
                    for i, it in enumerate(its):
                        if i + AH < len(its):
                            qk(its[i + AH])
                        ex_pv(it)
                        if i == len(its) // 2 or i == len(its) - 1:
                            deferred.pop(0)(WT[2], STB)
                dbg("yaT", yaT_d, yaT[:], [128, 4, S], BF16)

                wbr = wbrA
                sg = Rot([T(sb("sgA%d" % i, [128, 512], F32, sf)) for i in range(2)])
                wqb_pre = load_w(wqb_d[:, 0:512], 512, tile=WT[2])
                for half in range(2):
                    wg = wga_pre[half]
                    if half == 1:
                        wkb_pre = load_w(wkb_d[:, 0:512], 512, tile=WT[0])
                    for j in range(4):
                        nch = half * 4 + j
                        for tc in range(NC4):
                            pg = pall.next()
                            for kc in range(KC):
                                S_.op("pe", lambda e, kc=kc, tc=tc, pg=pg, j=j, wg=wg: e.matmul(
                                    pg.t[:, :], lhsT=wg.t[:, kc, j * 128:(j + 1) * 128], rhs=bufA[:, kc, tc * 512:(tc + 1) * 512],
                                    start=(kc == 0), stop=(kc == KC - 1)), reads=[wg, bufA_d[tc]], writes=[pg], inc=(kc == KC - 1))
                            s_ = sg.next()
                            S_.op("act", lambda e, pg=pg, s_=s_: e.activation(out=s_.t[:], in_=pg.t[:, :], func=AF.Sigmoid),
                                  reads=[pg], writes=[s_])
                            pbr = pall.next()
                            for pr in range(4):
                                S_.op("pe", lambda e, pr=pr, tc=tc, pbr=pbr, nch=nch: e.matmul(
                                    pbr.t[:, :], lhsT=wbr.t[:, pr, nch * 128:(nch + 1) * 128], rhs=yaT[:, pr, tc * 512:(tc + 1) * 512],
                                    start=(pr == 0), stop=(pr == 3)), reads=[wbr, yaT_d[tc]], writes=[pbr], inc=(pr == 3))
                            S_.op("dve", lambda e, pbr=pbr, s_=s_, nch=nch, tc=tc: e.tensor_tensor(
                                out=bufB[:, nch, tc * 512:(tc + 1) * 512], in0=pbr.t[:, :], in1=s_.t[:], op=ALU.mult),
                                reads=[pbr, s_], writes=[bufB_d[tc]])
            sf.close()
            S_.barrier()
            dbg("mA", bufB_d, bufB[:], [128, KC, S], BF16)

            with ExitStack() as sd:
                ybT = sb("ybT", [128, 2, S], BF16, sd)
                ybT_d = [T() for _ in range(2)]
                wbrB = T(sb("wbrB", [128, 2, D], BF16, sd))
                S_.dma("pool", lambda e: e.dma_start(out=wbrB.t[:], in_=wbrb_d.rearrange("(pr p) n -> p pr n", p=128)), writes=[wbrB])
                sda = sd.enter_context(ExitStack())
                cosT = T(sb("cosT", [128, S], F32, sda))
                sinT = T(sb("sinT", [128, S], F32, sda))
                S_.dma("sp", lambda e: e.dma_start(out=cosT.t[:], in_=cos_d), writes=[cosT])
                S_.dma("sp", lambda e: e.dma_start(out=sinT.t[:], in_=sin_d), writes=[sinT])
                accN = T(sb("accN", [128, S], F32, sda))
                accD = T(sb("accD", [128, S], F32, sda))
                kp = T(sb("kpB", [128, S], BF16, sda))
                qz = [T(sb("qzB%d" % i, [128, S], BF16, sda)) for i in range(2)]
                Vp = T(sb("VpB", [128, NT, 128], BF16, sda))
                PTb = Rot([T(sb("PTbB%d" % i, [128, 512], BF16, sda)) for i in range(3)])
                qtmp = Rot([T(sb("qtmp%d" % i, [128, 512], BF16, sda)) for i in range(2)])
                t1r = Rot([T(sb("t1r%d" % i, [128, 512], F32, sda)) for i in range(2)])
                t2r = Rot([T(sb("t2r%d" % i, [128, 512], F32, sda)) for i in range(2)])
                S_.op("pool", lambda e: e.memset(qz[0].t[:], 0.0), writes=[qz[0]])
                S_.op("pool", lambda e: e.memset(qz[1].t[:], 0.0), writes=[qz[1]])
                wqb = [wqb_pre, None]
                wkb = [wkb_pre, None]
                wvb = [load_w(wvb_d[:, 0:512], 512, tile=WT[1]), None]
                WX = [T(sb("wx%d" % i, [128, KC, 256], BF16, sda)) for i in range(3)]
                for i, src in enumerate((wqb_d, wkb_d, wvb_d)):
                    S_.dma("pool", lambda e, i=i, src=src: e.dma_start(
                        out=WX[i].t[:], in_=src[:, 512:768].rearrange("(kc p) n -> p kc n", p=128)), writes=[WX[i]])

                def wsel(main, extra, pc):
                    if pc < 4:
                        return main, main.t, pc * 128
                    return extra, extra.t, (pc - 4) * 128

                chk("c1")
                def make_step(m, g, qz, kp):
                    d = DIL[g]
                    pc = 2 * g + m
                    nbk = 16 // d
                    wq_t, wq_ap, wq_c = wsel(wqb[0], WX[0], pc)
                    wk_t, wk_ap, wk_c = wsel(wkb[0], WX[1], pc)
                    wv_t, wv_ap, wv_c = wsel(wvb[0], WX[2], pc)
                    def f_proj():
                        pend = []
                        for which in range(2):
                            w_t, w_ap, w_c = (wq_t, wq_ap, wq_c) if which == 0 else (wk_t, wk_ap, wk_c)
                            for j in range(4):
                                pq = pall.next()
                                for kc in range(KC):
                                    S_.op("pe", lambda e, kc=kc, j=j, pq=pq, w_ap=w_ap, w_c=w_c, g=g: e.matmul(
                                        pq.t[:, :], lhsT=w_ap[:, kc, w_c:w_c + 128], rhs=bufA[:, kc, j * 512:(j + 1) * 512],
                                        start=(kc == 0), stop=(kc == KC - 1)), reads=[w_t, bufA_d[j]], writes=[pq], inc=(kc == KC - 1))
                                chk("c2")
                                qt = qtmp.next()
                                S_.op("act", lambda e, pq=pq, qt=qt: e.activation(out=qt.t[:], in_=pq.t[:, :], func=AF.Copy),
                                      reads=[pq], writes=[qt])
                                def post(pq=pq, qt=qt, j=j, which=which):
                                    psw = pall.next()
                                    S_.op("pe", lambda e, psw=psw, qt=qt: e.matmul(psw.t[:, :], lhsT=pm_bf.t[:], rhs=qt.t[:], start=True, stop=True),
                                          reads=[pm_bf, qt], writes=[psw])
                                    chk("c4")
                                    t1 = t1r.next()
                                    t2 = t2r.next()
                                    S_.op("dve", lambda e, pq=pq, t1=t1, g=g, j=j: e.tensor_tensor(
                                        out=t1.t[:], in0=pq.t[:, :], in1=cosT.t[:, j * 512:(j + 1) * 512], op=ALU.mult),
                                        reads=[pq, cosT], writes=[t1])
                                    chk("c4b")
                                    S_.op("dve", lambda e, psw=psw, t2=t2, g=g, j=j: e.tensor_tensor(
                                        out=t2.t[:], in0=psw.t[:, :], in1=sinT.t[:, j * 512:(j + 1) * 512], op=ALU.mult),
                                        reads=[psw, sinT], writes=[t2])
                                    chk("c5")
                                    if which == 0:
                                        for hp in range(2):
                                            S_.op("pool", lambda e, t1=t1, t2=t2, hp=hp, j=j: e.tensor_tensor(
                                                out=cls_out(qz[hp].t[hp * 64:(hp + 1) * 64, :], g, j), in0=nat_in(t1.t[hp * 64:(hp + 1) * 64, :], g),
                                                in1=nat_in(t2.t[hp * 64:(hp + 1) * 64, :], g), op=ALU.add), reads=[t1, t2], writes=[qz[hp]])
                                    else:
                                        S_.op("pool", lambda e, t1=t1, t2=t2, j=j: e.tensor_tensor(
                                            out=cls_out(kp.t[:, :], g, j), in0=nat_in(t1.t[:], g), in1=nat_in(t2.t[:], g), op=ALU.add),
                                            reads=[t1, t2], writes=[kp])
                                pend.append(post)
                                if len(pend) > 1:
                                    pend.pop(0)()
                        while pend:
                            pend.pop(0)()
                        chk("dq%d%d" % (m, g))
                    def f_v():
                        for cb4 in range(4):
                            pv = pall.next()
                            for i in range(4):
                                cb = cb4 * 4 + i
                                r, n_ = cb // nbk, cb % nbk
                                t0 = d * 128 * n_ + r
                                for kc in range(KC):
                                    S_.op("pe", lambda e, kc=kc, i=i, pv=pv, t0=t0, d=d, wv_ap=wv_ap, wv_c=wv_c: e.matmul(
                                        pv.t[:, i * 128:(i + 1) * 128], lhsT=bufA[:, kc, t0:t0 + 127 * d + 1:d], rhs=wv_ap[:, kc, wv_c:wv_c + 128],
                                        start=(kc == 0), stop=(kc == KC - 1)), reads=[wv_t] + bufA_d, writes=[pv],
                                        inc=(kc == KC - 1 and i == 3))
                            S_.op("act", lambda e, pv=pv, cb4=cb4: e.activation(
                                out=Vp.t[:, cb4 * 4:(cb4 + 1) * 4, :].rearrange("p a b -> p (a b)"), in_=pv.t[:, :], func=AF.Copy),
                                reads=[pv], writes=[Vp])
                        chk("dv%d%d" % (m, g))
                    def f_att():
                        units = []
                        for hp in range(2):
                            if g < 2:
                                for cb in range(16):
                                    hasnext = (cb % nbk) < nbk - 1
                                    units.append(dict(hp=hp, kbs=[cb], qcols=(cb * 128, (cb + (2 if hasnext else 1)) * 128),
                                                      mask0=0))
                            else:
                                for j in range(4):
                                    units.append(dict(hp=hp, kbs=[4 * j + i for i in range(4)], qcols=(j * 512, (j + 1) * 512),
                                                      mask0=256))
                        accs = {}

                        def acc_of(hp, j):
                            if (hp, j) not in accs:
                                accs[(hp, j)] = ACC.next()
                            return accs[(hp, j)]

                        def d_qk(u):
                            hp = u["hp"]
                            st = STB.next()
                            u["st"] = st
                            q0, q1 = u["qcols"]
                            n = q1 - q0
                            u["n"] = n
                            if g < 2:
                                cb = u["kbs"][0]
                                S_.op("pe", lambda e: e.matmul(
                                    st.t[:, 0:n], lhsT=kp.t[:, cb * 128:(cb + 1) * 128], rhs=qz[hp].t[:, q0:q1],
                                    start=True, stop=False), reads=[kp, qz[hp]], writes=[st], inc=False)
                            else:
                                for i, cb in enumerate(u["kbs"]):
                                    S_.op("pe", lambda e: e.matmul(
                                        st.t[:, i * 128:(i + 1) * 128], lhsT=kp.t[:, cb * 128:(cb + 1) * 128],
                                        rhs=qz[hp].t[:, cb * 128:(cb + 1) * 128], start=(i == 0), stop=False),
                                        reads=[kp, qz[hp]], writes=[st], inc=False)
                            m0 = u["mask0"]
                            S_.op("pe", lambda e: e.matmul(
                                st.t[:, 0:n], lhsT=ident_bf.t[:], rhs=maskX.t[:, m0:m0 + n], start=False, stop=True),
                                reads=[ident_bf, maskX], writes=[st], inc=True)

                        def d_pv(u):
                            hp, st, n = u["hp"], u["st"], u["n"]
                            lanes = slice(hp * 64, (hp + 1) * 64)
                            pt = PTb.next()
                            S_.op("act", lambda e: e.activation(out=pt.t[:, 0:n], in_=st.t[:, 0:n], func=AF.Exp, scale=SCALE),
                                  reads=[st], writes=[pt])
                            contribs = []
                            if g < 2:
                                cb = u["kbs"][0]
                                contribs.append((cb, 0, cb, (cb % nbk) == 0))
                                if n == 256:
                                    contribs.append((cb, 128, cb + 1, True))
                            else:
                                for i, cb in enumerate(u["kbs"]):
                                    contribs.append((cb, i * 128, cb, True))
                            for (kb_, pc, qb_, first) in contribs:
                                num, den = acc_of(hp, qb_ // 4)
                                cols = slice((qb_ % 4) * 128, (qb_ % 4 + 1) * 128)
                                last = (kb_ == qb_)
                                S_.op("pe", lambda e: e.matmul(
                                    num.t[:, cols], lhsT=Vp.t[:, kb_, :], rhs=pt.t[:, pc:pc + 128], start=first, stop=last),
                                    reads=[Vp, pt], writes=[num], inc=False)
                                S_.op("pe", lambda e: e.matmul(
                                    den.t[:, cols], lhsT=ones_bf.t[:], rhs=pt.t[:, pc:pc + 128], start=first, stop=last),
                                    reads=[ones_bf, pt], writes=[den], inc=True)
                                if last and qb_ % 4 == 3:
                                    j = qb_ // 4
                                    for (acc, src) in ((accN, num), (accD, den)):
                                        if g == 0:
                                            S_.op("dve", lambda e: e.tensor_copy(
                                                out=tok_view(acc.t[lanes, :], g, j), in_=chunk_view(src.t[lanes, :], g)),
                                                reads=[src], writes=[acc])
                                        else:
                                            S_.op("dve", lambda e: e.tensor_tensor(
                                                out=tok_view(acc.t[lanes, :], g, j), in0=chunk_view(src.t[lanes, :], g),
                                                in1=tok_view(acc.t[lanes, :], g, j), op=ALU.add), reads=[src, acc], writes=[acc])

                        AHEAD = 2
                        for i in range(min(AHEAD, len(units))):
                            d_qk(units[i])
                        for i, u in enumerate(units):
                            if i + AHEAD < len(units):
                                d_qk(units[i + AHEAD])
                            d_pv(u)
                        chk("da%d%d" % (m, g))
                    return f_proj, f_v, f_att

                def finalize(m):
                    S_.op("dve", lambda e: e.reciprocal(out=accD.t[:], in_=accD.t[:]), reads=[accD], writes=[accD])
                    S_.op("dve", lambda e, m=m: e.tensor_tensor(out=ybT[:, m, :], in0=accN.t[:], in1=accD.t[:], op=ALU.mult),
                          reads=[accN, accD], writes=[ybT_d[m]])

                qzs = [qz, [T(sb("qzC%d" % i, [128, S], BF16, sda)) for i in range(2)]]
                kps = [kp, T(sb("kpC", [128, S], BF16, sda))]
                S_.op("pool", lambda e: e.memset(qzs[1][0].t[:], 0.0), writes=[qzs[1][0]])
                S_.op("pool", lambda e: e.memset(qzs[1][1].t[:], 0.0), writes=[qzs[1][1]])
                steps = [make_step(m_, g_, qzs[(m_ * 3 + g_) % 2], kps[(m_ * 3 + g_) % 2]) for m_ in range(2) for g_ in range(3)]
                steps[0][0]()
                steps[0][1]()
                for s_i in range(6):
                    if s_i + 1 < 6:
                        steps[s_i + 1][0]()
                    if s_i == 4:
                        wgb_pre = [load_w(wgb_d[:, 0:512], 512, tile=wqb[0]), load_w(wgb_d[:, 512:1024], 512, tile=wkb[0])]
                    if s_i == 5:
                        wo_pre0 = load_w(wout_d[:, 0:512], 512, tile=wvb[0])
                    steps[s_i][2]()
                    if s_i % 3 == 2:
                        finalize(s_i // 3)
                    if s_i + 1 < 6:
                        steps[s_i + 1][1]()
                sda.close()
                S_.barrier()
                dbg("ybT", ybT_d, ybT[:], [128, 2, S], BF16)

                wbr = wbrB
                sg = Rot([T(sb("sgB%d" % i, [128, 512], F32, sd)) for i in range(2)])
                tm = Rot([T(sb("tmB%d" % i, [128, 512], F32, sd)) for i in range(2)])
                S_.op("dve", lambda e: e.tensor_tensor(
                    out=wo_pre0.t[:], in0=wo_pre0.t[:],
                    in1=gaBCm.t[:, 0:512].unsqueeze(1).to_broadcast([128, KC, 512]), op=ALU.mult),
                    reads=[wo_pre0, gaBCm], writes=[wo_pre0])
                for half in range(2):
                    wg = wgb_pre[half]
                    if half == 1:
                        wo_pre1 = load_w(wout_d[:, 512:1024], 512, tile=wgb_pre[0])
                    for j in range(4):
                        nch = half * 4 + j
                        for tc in range(NC4):
                            pg = pall.next()
                            for kc in range(KC):
                                S_.op("pe", lambda e, kc=kc, tc=tc, pg=pg, j=j, wg=wg: e.matmul(
                                    pg.t[:, :], lhsT=wg.t[:, kc, j * 128:(j + 1) * 128], rhs=bufA[:, kc, tc * 512:(tc + 1) * 512],
                                    start=(kc == 0), stop=(kc == KC - 1)), reads=[wg, bufA_d[tc]], writes=[pg], inc=(kc == KC - 1))
                            s_ = sg.next()
                            S_.op("act", lambda e, pg=pg, s_=s_: e.activation(out=s_.t[:], in_=pg.t[:, :], func=AF.Sigmoid),
                                  reads=[pg], writes=[s_])
                            pbr = pall.next()
                            for pr in range(2):
                                S_.op("pe", lambda e, pr=pr, tc=tc, pbr=pbr, nch=nch: e.matmul(
                                    pbr.t[:, :], lhsT=wbr.t[:, pr, nch * 128:(nch + 1) * 128], rhs=ybT[:, pr, tc * 512:(tc + 1) * 512],
                                    start=(pr == 0), stop=(pr == 1)), reads=[wbr] + ybT_d, writes=[pbr], inc=(pr == 1))
                            t_ = tm.next()
                            S_.op("dve", lambda e, pbr=pbr, s_=s_, t_=t_: e.tensor_tensor(
                                out=t_.t[:], in0=pbr.t[:, :], in1=s_.t[:], op=ALU.mult), reads=[pbr, s_], writes=[t_])
                            S_.op("pool", lambda e, t_=t_, nch=nch, tc=tc: e.tensor_tensor(
                                out=bufB[:, nch, tc * 512:(tc + 1) * 512], in0=bufB[:, nch, tc * 512:(tc + 1) * 512], in1=t_.t[:], op=ALU.add),
                                reads=[t_, bufB_d[tc]], writes=[bufB_d[tc]])
                S_.op("dve", lambda e: e.tensor_tensor(
                    out=wo_pre1.t[:], in0=wo_pre1.t[:],
                    in1=gaBCm.t[:, 512:1024].unsqueeze(1).to_broadcast([128, KC, 512]), op=ALU.mult),
                    reads=[wo_pre1, gaBCm], writes=[wo_pre1])
            S_.barrier()
            dbg("merged", bufB_d, bufB[:], [128, KC, S], BF16)

            with ExitStack() as s2:
                x1 = sb("x1", [128, NT, D], F32, s2)
                x1_d = [T() for _ in range(NT)]
                WT4 = T(sb("wt4", [128, KC, 512], BF16, s2))
                for tt in range(NT):
                    S_.dma("sp", lambda e, tt=tt: e.dma_start(out=x1[:, tt, :], in_=x_d[tt * 128:(tt + 1) * 128, :]), writes=[x1_d[tt]])
                so = s2.enter_context(ExitStack())
                tmo = Rot([T(sb("tmo%d" % i, [128, 512], F32, so)) for i in range(3)])
                wo = [wo_pre0, wo_pre1]
                others = [t for t in WT if t is not wo_pre0 and t is not wo_pre1]
                pre_w = [load_w(wfg_d[:, 0:512], 512, tile=others[0]), load_w(wfu_d[:, 0:512], 512, tile=WT4)]
                wrot = Rot([wo_pre0, wo_pre1, others[0], WT4])

                def outproj_tile(tt):
                    for ch in range(2):
                        po = pall.next()
                        for kc in range(KC):
                            S_.op("pe", lambda e: e.matmul(
                                po.t[:, :], lhsT=bufB[:, kc, tt * 128:(tt + 1) * 128], rhs=wo[ch].t[:, kc, :],
                                start=(kc == 0), stop=(kc == KC - 1)), reads=[wo[ch], bufB_d[tt // 4]], writes=[po], inc=(kc == KC - 1))
                        S_.op("dve", lambda e: e.tensor_tensor(
                            out=x1[:, tt, ch * 512:(ch + 1) * 512], in0=po.t[:, :], in1=x1[:, tt, ch * 512:(ch + 1) * 512], op=ALU.add),
                            reads=[po, x1_d[tt]], writes=[x1_d[tt]])
                    return x1_d[tt], x1[:, tt, :]

                with ExitStack() as sn:
                    norm_to_T(outproj_tile, 1, bufA, bufA_d, 16, 8, sn)
                so.close()
                S_.barrier()
                dbg("x1", x1_d, x1[:], [128, NT, D])
                dbg("h2T", bufA_d, bufA[:], [128, KC, S], BF16)
                sf2 = s2.enter_context(ExitStack())
                tmo = Rot([T(sb("tmf%d" % i, [128, 512], F32, sf2)) for i in range(3)])

                wdt = [T(sb("wd%d" % i, [128, KC, D], BF16, sf2)) for i in range(1)]
                sa = Rot([T(sb("sa%d" % i, [128, 512], F32, sf2)) for i in range(2)])
                groups = [(0, 8), (8, 8), (16, 6)]
                sqf = T(sb("sqF", [128, NT], F32, sf2))

                def final_tile(tt):
                    col = 2 * NT + tt
                    jk = sa.next()
                    S_.op("act", lambda e: e.activation(out=jk.t[:].bitcast(BF16), in_=x1[:, tt, :], func=AF.Square,
                                                        accum_out=ssq.t[:, col:col + 1]), reads=[x1_d[tt]], writes=[jk, ssq])
                    S_.op("act", lambda e: e.activation(out=sqf.t[:, tt:tt + 1], in_=ssq.t[:, col:col + 1], func=AF.Sqrt,
                                                        scale=1.0 / D, bias=EPS), reads=[ssq], writes=[sqf])
                    S_.op("dve", lambda e: e.reciprocal(out=rstd.t[:, col:col + 1], in_=sqf.t[:, tt:tt + 1]), reads=[sqf], writes=[rstd])
                    for ch in range(2):
                        y = tmo.next()
                        S_.op("act", lambda e: e.activation(out=y.t[:], in_=x1[:, tt, ch * 512:(ch + 1) * 512], func=AF.Identity,
                                                            scale=rstd.t[:, col:col + 1]), reads=[x1_d[tt], rstd], writes=[y])
                        S_.op("dve", lambda e: e.tensor_tensor(out=y.t[:], in0=y.t[:], in1=gfinBC.t[:, ch * 512:(ch + 1) * 512],
                                                               op=ALU.mult), reads=[y, gfinBC], writes=[y])
                        S_.dma("sp", lambda e: e.dma_start(out=out_d[tt * 128:(tt + 1) * 128, ch * 512:(ch + 1) * 512], in_=y.t[:]),
                               reads=[y])
                for (f0, nf) in groups:
                    wd = wdt[0]
                    for q4 in range(0, nf, 4):
                        nq = min(4, nf - q4)
                        c0 = (f0 + q4) * 128
                        if pre_w:
                            wg, wu = pre_w
                            pre_w = None
                        else:
                            wg = load_w(wfg_d[:, c0:c0 + nq * 128], nq * 128)
                            wu = load_w(wfu_d[:, c0:c0 + nq * 128], nq * 128)
                        if q4 == 0:
                            S_.dma("pool", lambda e, f0=f0, nf=nf, wd=wd: e.dma_start(
                                out=wd.t[:, 0:nf, :], in_=wfd_d[f0 * 128:(f0 + nf) * 128, :].rearrange("(kc p) n -> p kc n", p=128)),
                                writes=[wd])
                            S_.op("pool", lambda e, nf=nf, wd=wd: e.tensor_tensor(
                                out=wd.t[:, 0:nf, :], in0=wd.t[:, 0:nf, :],
                                in1=gaBCf.t[:, :].unsqueeze(1).to_broadcast([128, nf, D]), op=ALU.mult),
                                reads=[wd, gaBCf], writes=[wd])
                        for jj in range(nq):
                            fl = q4 + jj
                            for tc in range(NC4):
                                pa = pall.next()
                                for kc in range(KC):
                                    S_.op("pe", lambda e, kc=kc, tc=tc, pa=pa, jj=jj, wg=wg: e.matmul(
                                        pa.t[:, :], lhsT=wg.t[:, kc, jj * 128:(jj + 1) * 128], rhs=bufA[:, kc, tc * 512:(tc + 1) * 512],
                                        start=(kc == 0), stop=(kc == KC - 1)), reads=[wg, bufA_d[tc]], writes=[pa], inc=(kc == KC - 1))
                                pu = pall.next()
                                for kc in range(KC):
                                    S_.op("pe", lambda e, kc=kc, tc=tc, pu=pu, jj=jj, wu=wu: e.matmul(
                                        pu.t[:, :], lhsT=wu.t[:, kc, jj * 128:(jj + 1) * 128], rhs=bufA[:, kc, tc * 512:(tc + 1) * 512],
                                        start=(kc == 0), stop=(kc == KC - 1)), reads=[wu, bufA_d[tc]], writes=[pu], inc=(kc == KC - 1))
                                s_ = sa.next()
                                S_.op("act", lambda e, pa=pa, s_=s_: e.activation(out=s_.t[:], in_=pa.t[:, :], func=AF.Silu),
                                      reads=[pa], writes=[s_])
                                S_.op("dve", lambda e, pu=pu, s_=s_, fl=fl, tc=tc: e.tensor_tensor(
                                    out=bufB[:, fl, tc * 512:(tc + 1) * 512], in0=pu.t[:, :], in1=s_.t[:], op=ALU.mult),
                                    reads=[pu, s_], writes=[bufB_d[tc]])
                    for tt in range(NT):
                        for ch in range(2):
                            po = pall.next()
                            for kc in range(nf):
                                S_.op("pe", lambda e, kc=kc, tt=tt, ch=ch, po=po, nf=nf, wd=wd: e.matmul(
                                    po.t[:, :], lhsT=bufB[:, kc, tt * 128:(tt + 1) * 128], rhs=wd.t[:, kc, ch * 512:(ch + 1) * 512],
                                    start=(kc == 0), stop=(kc == nf - 1)), reads=[wd, bufB_d[tt // 4]], writes=[po], inc=(kc == nf - 1))
                            S_.op("dve", lambda e, po=po, tt=tt, ch=ch: e.tensor_tensor(
                                out=x1[:, tt, ch * 512:(ch + 1) * 512], in0=po.t[:, :], in1=x1[:, tt, ch * 512:(ch + 1) * 512], op=ALU.add),
                                reads=[po, x1_d[tt]], writes=[x1_d[tt]])
                        if f0 + nf == NFF:
                            if tt >= 2:
                                final_tile(tt - 2)
                            if tt == NT - 1:
                                final_tile(NT - 2)
                                final_tile(NT - 1)

                dbg("x2", x1_d, x1[:], [128, NT, D])
                sf2.close()
        except _Stop:
            pass
        S_.finish()
        S_.emit()
    return nc, dbg_d


def _consts():
    ident = np.eye(128, dtype=np.float32)
    k = np.arange(128)[:, None]
    q = np.arange(128)[None, :]
    anti = np.where(k >= q, 0.0, NEG).astype(np.float32)
    caus = np.where(k <= q, 0.0, NEG).astype(np.float32)
    mask = np.concatenate([caus, anti, caus, caus, caus, caus], axis=1)
    sel = np.zeros((128, 8 * 128), np.float32)
    for h in range(8):
        sel[h, h * 128:(h + 1) * 128] = 1.0
    pm = np.zeros((128, 128), np.float32)
    cosf = np.ones((128, S), np.float32)
    sinf = np.zeros((128, S), np.float32)
    pos = np.arange(S, dtype=np.float32)
    inv_freq = (np.float32(500000.0) ** (-(np.arange(0, 16, 2, dtype=np.float32)) / np.float32(16))).astype(np.float32)
    ang = (pos[:, None] * inv_freq[None, :]).astype(np.float32)
    cs = np.cos(ang).astype(np.float32).T
    sn = np.sin(ang).astype(np.float32).T
    for hp in range(2):
        for dd in range(16):
            p = hp * 64 + dd
            i = dd % 8
            partner = p + 8 if dd < 8 else p - 8
            pm[partner, p] = 1.0
            cosf[p] = cs[i]
            sinf[p] = -sn[i] if dd < 8 else sn[i]
    ones = np.ones((128, 128), np.float32)
    return dict(k_ident=ident, k_mask=mask, k_sel=sel, k_pm=pm, k_cos=cosf, k_sin=sinf, k_ones=ones)


def _col(v, n):
    return np.ascontiguousarray(np.asarray(v, np.float32).reshape(n, 128).T)


def _prep_inputs(x, c, w_ada, b_ada, g_mix, w_in, b_fgate, w_br_a, w_br_b, w_out,
                 g_ffn, w_ffn_gate, w_ffn_up, w_ffn_down, g_final):
    f = lambda a: np.ascontiguousarray(np.asarray(a, dtype=np.float32))
    x, c = f(x), f(c)
    w_in0 = f(w_in)[0]
    cuts = np.cumsum([512, 512, 512, 8, 768, 768, 768, 1024, 1024])[:-1]
    qa, ka, va, fa, qb, kb, vb, ga, gb = [np.ascontiguousarray(p) for p in np.split(w_in0, cuts, axis=1)]
    b_ada0 = f(b_ada)[0]
    shared = dict(
        w_ada=f(w_ada)[0], b_ada_col=_col(b_ada0, 48),
        b_gam_bc=np.ascontiguousarray(np.broadcast_to(b_ada0[2 * D:3 * D], (128, D))),
        b_gaf_bc=np.ascontiguousarray(np.broadcast_to(b_ada0[5 * D:6 * D], (128, D))),
        g_mix_col=_col(f(g_mix)[0], KC), g_ffn_col=_col(f(g_ffn)[0], KC),
        g_fin_bc=np.ascontiguousarray(np.broadcast_to(f(g_final), (128, D))),
        b_fg_col=np.ascontiguousarray(f(b_fgate)[0].reshape(8, 1)),
        w_qa=qa, w_ka=ka, w_va=va, w_fa=np.ascontiguousarray(fa.reshape(KC, 128, 8).transpose(1, 0, 2).reshape(128, KC * 8)), w_qb=qb, w_kb=kb, w_vb=vb, w_ga=ga, w_gb=gb,
        w_br_a=f(w_br_a)[0], w_br_b=f(w_br_b)[0], w_out=f(w_out)[0],
        w_ffn_gate=f(w_ffn_gate)[0], w_ffn_up=f(w_ffn_up)[0], w_ffn_down=f(w_ffn_down)[0],
    )
    shared.update(_consts())
    in_maps = []
    for b in range(8):
        m = dict(shared)
        m["x"] = np.ascontiguousarray(x[b])
        m["c_col"] = _col(c[b], KC)
        in_maps.append(m)
    return in_maps


_NC_CACHE = {}


def kernel(**inputs):
    in_maps = _prep_inputs(**inputs)
    if "nc" not in _NC_CACHE:
        _NC_CACHE["nc"] = build_program()[0]
    nc = _NC_CACHE["nc"]
    res = run_bass_kernel_spmd(nc, in_maps, core_ids=list(range(8)))
    out = np.stack([np.asarray(r["out"], dtype=np.float32).reshape(S, D) for r in res.results], axis=0)
    return out
```

```python
import numpy as np
from contextlib import ExitStack
import concourse.bass as bass
import concourse.mybir as mybir
from concourse.bass_utils import run_bass_kernel_spmd

F32 = mybir.dt.float32
BF16 = mybir.dt.bfloat16
AF = mybir.ActivationFunctionType
ALU = mybir.AluOpType

D = 1024
S = 2048
NT = 16
NC4 = 4
KC = 8
DFF = 2816
NFF = 22
EPS = 1e-6
NEG = -30000.0
SCALE = 0.125
DIL = (1, 4, 16)
ENG = ("pe", "act", "dve", "pool", "sp")


class T:
    __slots__ = ("t", "w", "r", "ds", "excl")

    def __init__(self, t=None, excl=False):
        self.t = t
        self.excl = excl
        self.w = None
        self.r = {}
        self.ds = None


class Rec:
    def __init__(self):
        self.call = None

    def __getattr__(self, name):
        def f(*a, **k):
            self.call = (name, a, k)
        return f


def _record(fn):
    r = Rec()
    fn(r)
    assert r.call is not None
    return r.call


class Sched:
    def __init__(self, nc, es, n_dma=24):
        self.nc = nc
        self.sem = {e: es.enter_context(nc.semaphore("s_" + e)) for e in ENG}
        self.cnt = {e: 0 for e in ENG}
        self.es = es
        self.dsem = []
        self.dcnt = []
        self.dq = []
        self.inflight = {"pool": [], "sp": []}
        self.max_inflight = {"pool": 2, "sp": 4}
        self.streams = {e: [] for e in ENG}
        self.waited = {e: {} for e in ENG}
        self.stopped = False

    def _need(self, eng, deps):
        for key, val in deps:
            if key == eng and eng == "pe":
                continue
            if self.waited[eng].get(key, 0) >= val:
                continue
            self.waited[eng][key] = val
            self.streams[eng].append(("w", key, val))

    @staticmethod
    def _deps(reads, writes, eng=None):
        deps = []
        for t in reads:
            if t.w is not None:
                deps.append(t.w)
            if t.excl:
                deps.extend((k, v) for k, v in t.r.items() if k != eng)
        for t in writes:
            if t.w is not None:
                deps.append(t.w)
            deps.extend(t.r.items())
        return deps

    def op(self, eng, fn, reads=(), writes=(), inc=True):
        if self.stopped:
            return
        self._need(eng, self._deps(reads, writes, eng))
        val = self.cnt[eng] + 1
        if inc:
            self.cnt[eng] = val
        self.streams[eng].append(("o", _record(fn), inc))
        for t in reads:
            if t.r.get(eng, 0) < val:
                t.r[eng] = val
        for t in writes:
            t.w = (eng, val)
            t.r = {}

    def dma(self, q, fn, reads=(), writes=()):
        if self.stopped:
            return
        own = writes[0] if len(writes) else reads[0]
        if own.ds is None:
            own.ds = len(self.dsem)
            self.dsem.append(self.es.enter_context(self.nc.semaphore("d%d" % own.ds)))
            self.dcnt.append(0)
            self.dq.append(q)
        i = own.ds
        assert self.dq[i] == q
        key = ("d", i)
        deps = self._deps(reads, writes)
        if self.dcnt[i] > 0:
            deps.append((key, self.dcnt[i]))
        fl = self.inflight[q]
        while len(fl) >= self.max_inflight[q]:
            deps.append(fl.pop(0))
        self._need(q, deps)
        self.dcnt[i] += 16
        val = self.dcnt[i]
        fl.append((key, val))
        self.streams[q].append(("d", _record(fn), i))
        for t in reads:
            t.r[key] = val
        for t in writes:
            t.w = (key, val)
            t.r = {}

    def barrier(self):
        if self.stopped:
            return
        edeps = [(e, self.cnt[e]) for e in ENG if self.cnt[e] > 0]
        for q in ("sp", "pool"):
            ddeps = [(("d", i), v) for i, v in enumerate(self.dcnt) if v > 0 and self.dq[i] == q]
            self._need(q, edeps + ddeps)
            self.cnt[q] += 1
            self.streams[q].append(("o", ("nop", (), {}), True))
        edeps = [(e, self.cnt[e]) for e in ENG if self.cnt[e] > 0]
        for e in ENG:
            self._need(e, edeps)

    def finish(self):
        ddeps = [(("d", i), v) for i, v in enumerate(self.dcnt) if v > 0 and self.dq[i] == "pool"]
        if ddeps:
            self._need("pool", ddeps)
            self.cnt["pool"] += 1
            self.streams["pool"].append(("o", ("nop", (), {}), True))
        for i, v in enumerate(self.dcnt):
            if v > 0 and self.dq[i] == "sp":
                self._need("sp", [(("d", i), v)])
        for e in ENG:
            self._need("sp", [(e, self.cnt[e])] if self.cnt[e] > 0 else [])

    def emit(self):
        nc = self.nc
        needed = {e: set() for e in ENG}
        for e in ENG:
            for item in self.streams[e]:
                if item[0] == "w" and isinstance(item[1], str):
                    needed[item[1]].add(item[2])
        rank = {e: {v: i + 1 for i, v in enumerate(sorted(needed[e]))} for e in ENG}
        with nc.Block() as block:
            def mk(e):
                def body(eng):
                    prov = 0
                    for item in self.streams[e]:
                        if item[0] == "w":
                            key = item[1]
                            if isinstance(key, str):
                                eng.wait_ge(self.sem[key], rank[key][item[2]])
                            else:
                                eng.wait_ge(self.dsem[key[1]], item[2])
                        elif item[0] == "o":
                            c = item[1]
                            ins = getattr(eng, c[0])(*c[1], **c[2])
                            if item[2]:
                                prov += 1
                                if prov in needed[e]:
                                    ins.then_inc(self.sem[e], 1)
                        else:
                            c = item[1]
                            getattr(eng, c[0])(*c[1], **c[2]).then_inc(self.dsem[item[2]], 16)
                    assert prov == self.cnt[e], (e, prov, self.cnt[e])
                return body
            block.tensor(mk("pe"))
            block.scalar(mk("act"))
            block.vector(mk("dve"))
            block.gpsimd(mk("pool"))
            block.sync(mk("sp"))


class Rot:
    def __init__(self, items):
        self.items = items
        self.i = 0

    def next(self):
        x = self.items[self.i]
        self.i = (self.i + 1) % len(self.items)
        return x


def tok_view(ap2d, g, j):
    if g == 0:
        return ap2d[:, j * 512:(j + 1) * 512]
    if g == 1:
        return ap2d.rearrange("p (i r) -> p r i", r=4)[:, j, :]
    return ap2d.rearrange("p (i r) -> p r i", r=16)[:, 4 * j:4 * j + 4, :]


def cls_out(ap2d, g, tc):
    dd = DIL[g]
    if g == 0:
        return ap2d[:, tc * 512:(tc + 1) * 512]
    n = 512 // dd
    return ap2d.rearrange("p (r i) -> p r i", r=dd)[:, :, n * tc:n * (tc + 1)]


def nat_in(ap2d, g):
    if g == 0:
        return ap2d
    return ap2d.rearrange("p (i r) -> p r i", r=DIL[g])


def chunk_view(ap2d, g):
    if g == 2:
        return ap2d.rearrange("p (a b) -> p a b", a=4)
    return ap2d


class _Stop(Exception):
    pass


def build_program(debug=(), stop=None):
    nc = bass.Bass("TRN2", target_bir_lowering=False)

    def din(name, shape):
        return nc.dram_tensor(name, list(shape), F32, kind="ExternalInput").ap()

    x_d = din("x", [S, D])
    ccol_d = din("c_col", [128, KC])
    wada_d = din("w_ada", [D, 6 * D])
    badac_d = din("b_ada_col", [128, 48])
    bgam_d = din("b_gam_bc", [128, D])
    bgaf_d = din("b_gaf_bc", [128, D])
    gmix_d = din("g_mix_col", [128, KC])
    gffn_d = din("g_ffn_col", [128, KC])
    gfin_d = din("g_fin_bc", [128, D])
    bfg_d = din("b_fg_col", [8, 1])
    wqa_d = din("w_qa", [D, 512])
    wka_d = din("w_ka", [D, 512])
    wva_d = din("w_va", [D, 512])
    wfa_d = din("w_fa", [128, KC * 8])
    wqb_d = din("w_qb", [D, 768])
    wkb_d = din("w_kb", [D, 768])
    wvb_d = din("w_vb", [D, 768])
    wga_d = din("w_ga", [D, D])
    wgb_d = din("w_gb", [D, D])
    wbra_d = din("w_br_a", [512, D])
    wbrb_d = din("w_br_b", [256, D])
    wout_d = din("w_out", [D, D])
    wfg_d = din("w_ffn_gate", [D, DFF])
    wfu_d = din("w_ffn_up", [D, DFF])
    wfd_d = din("w_ffn_down", [DFF, D])
    ident_d = din("k_ident", [128, 128])
    mask_d = din("k_mask", [128, 768])
    sel_d = din("k_sel", [128, 8 * 128])
    pm_d = din("k_pm", [128, 128])
    cos_d = din("k_cos", [128, S])
    sin_d = din("k_sin", [128, S])
    ones_d = din("k_ones", [128, 128])
    out_d = nc.dram_tensor("out", [S, D], F32, kind="ExternalOutput").ap()
    dbg_d = {}

    with ExitStack() as es:
        S_ = Sched(nc, es)

        def sb(name, shape, dt, scope=es):
            return scope.enter_context(nc.sbuf_tensor(name, list(shape), dt))

        PB = [T(es.enter_context(nc.psum_tensor("pb%d" % i, [128, 512], F32)), excl=True) for i in range(8)]
        PTh = [PB[6], PB[7]]
        PTv = [PB[6].t[:, :].bitcast(BF16), PB[7].t[:, :].bitcast(BF16)]
        pall = Rot(PB)

        bufA = sb("bufA", [128, KC, S], BF16)
        bufB = sb("bufB", [128, KC, S], BF16)
        bufA_d = [T() for _ in range(NC4)]
        bufB_d = [T() for _ in range(NC4)]
        WT = [T(sb("wt%d" % i, [128, KC, 512], BF16)) for i in range(3)]
        wrot = Rot(WT)
        ident_bf = T(sb("ident_bf", [128, 128], BF16))
        ident_f = T(sb("ident_f", [128, 128], F32))
        maskX = T(sb("maskX", [128, 768], BF16))
        selones = T(sb("selones", [128, 8 * 128], BF16))
        pm_bf = T(sb("pm_bf", [128, 128], BF16))
        ones_bf = T(sb("ones_bf", [128, 128], BF16))
        gaBCm = T(sb("gaBCm", [128, D], F32))
        gaBCf = T(sb("gaBCf", [128, D], F32))
        gfinBC = T(sb("gfinBC", [128, D], F32))
        modT = T(sb("modT", [128, 32], F32))
        gsc = T(sb("gsc", [128, 16], F32))
        small = T(sb("small", [128, 64], F32))
        badac = T(sb("badac", [128, 48], F32))
        cs_bf = T(sb("cs_bf", [128, KC], BF16))
        csb_bf = T(sb("csb_bf", [128, KC, 128], BF16))
        ssq = T(sb("ssq", [128, 3 * NT], F32))
        rstd = T(sb("rstd", [128, 3 * NT], F32))

        def load_w(src_ap, ncols, kc=KC, tile=None):
            wt = wrot.next() if tile is None else tile
            S_.dma("pool", lambda e, wt=wt: e.dma_start(
                out=wt.t[:, 0:kc, 0:ncols], in_=src_ap.rearrange("(kc p) n -> p kc n", p=128)), writes=[wt])
            return wt

        def dbg(name, tiles, ap, shape, dt=F32):
            if name not in debug:
                if stop == name:
                    S_.stopped = True
                return
            d = nc.dram_tensor("dbg_" + name, list(shape), dt, kind="ExternalOutput").ap()
            dbg_d[name] = d
            S_.dma("sp", lambda e: e.dma_start(out=d, in_=ap), reads=[T()] + list(tiles))
            if stop == name:
                S_.stopped = True

        def chk(name):
            if stop == name:
                S_.stopped = True

        try:
            for (tl, src) in ((ident_bf, ident_d), (maskX, mask_d), (selones, sel_d), (pm_bf, pm_d), (ones_bf, ones_d)):
                S_.dma("pool", lambda e, tl=tl, src=src: e.dma_start(out=tl.t[:], in_=src), writes=[tl])
            S_.dma("sp", lambda e: e.dma_start(out=ident_f.t[:], in_=ident_d), writes=[ident_f])
            S_.dma("sp", lambda e: e.dma_start(out=small.t[:, 0:8], in_=ccol_d), writes=[small])
            S_.dma("sp", lambda e: e.dma_start(out=small.t[:, 8:16], in_=gmix_d), writes=[small])
            S_.dma("sp", lambda e: e.dma_start(out=small.t[:, 16:24], in_=gffn_d), writes=[small])
            S_.dma("sp", lambda e: e.dma_start(out=badac.t[:], in_=badac_d), writes=[badac])
            S_.op("act", lambda e: e.activation(out=cs_bf.t[:], in_=small.t[:, 0:8], func=AF.Silu), reads=[small], writes=[cs_bf])
            S_.op("dve", lambda e: e.tensor_copy(out=csb_bf.t[:], in_=cs_bf.t[:].unsqueeze(2).to_broadcast([128, KC, 128])),
                  reads=[cs_bf], writes=[csb_bf])

            def ada_cols(sec, dst0):
                pb = pall.next()
                for half in range(2):
                    wt = load_w(wada_d[:, sec * D + half * 512: sec * D + (half + 1) * 512], 512)
                    for j in range(4):
                        col = half * 4 + j
                        for kc in range(KC):
                            S_.op("pe", lambda e, wt=wt, j=j, kc=kc, col=col, pb=pb: e.matmul(
                                pb.t[:, col:col + 1], lhsT=wt.t[:, kc, j * 128:(j + 1) * 128], rhs=cs_bf.t[:, kc:kc + 1],
                                start=(kc == 0), stop=(kc == KC - 1)),
                                reads=[wt, cs_bf], writes=[pb], inc=(kc == KC - 1))
                S_.op("dve", lambda e, pb=pb: e.tensor_tensor(out=modT.t[:, dst0:dst0 + 8], in0=pb.t[:, 0:8],
                                                             in1=badac.t[:, sec * 8:(sec + 1) * 8], op=ALU.add),
                      reads=[pb, badac], writes=[modT])

            def ada_rows(sec, dstT):
                for half in range(2):
                    wt = load_w(wada_d[:, sec * D + half * 512: sec * D + (half + 1) * 512], 512)
                    pb = pall.next()
                    for kc in range(KC):
                        S_.op("pe", lambda e, wt=wt, kc=kc, pb=pb: e.matmul(
                            pb.t[:, :], lhsT=csb_bf.t[:, kc, :], rhs=wt.t[:, kc, :], start=(kc == 0), stop=(kc == KC - 1)),
                            reads=[wt, csb_bf], writes=[pb], inc=(kc == KC - 1))
                    S_.op("dve", lambda e, pb=pb, half=half: e.tensor_tensor(
                        out=dstT.t[:, half * 512:(half + 1) * 512], in0=pb.t[:, :], in1=dstT.t[:, half * 512:(half + 1) * 512],
                        op=ALU.add), reads=[pb, dstT], writes=[dstT])

            def ada_cols_half(sec, dst0, half, tile, ps=None):
                pb = (ps or pall).next()
                wt = load_w(wada_d[:, sec * D + half * 512: sec * D + (half + 1) * 512], 512, tile=tile)
                for j in range(4):
                    for kc in range(KC):
                        S_.op("pe", lambda e, wt=wt, j=j, kc=kc, pb=pb: e.matmul(
                            pb.t[:, j:j + 1], lhsT=wt.t[:, kc, j * 128:(j + 1) * 128], rhs=cs_bf.t[:, kc:kc + 1],
                            start=(kc == 0), stop=(kc == KC - 1)),
                            reads=[wt, cs_bf], writes=[pb], inc=(kc == KC - 1))
                c0 = dst0 + half * 4
                b0 = sec * 8 + half * 4
                S_.op("dve", lambda e, pb=pb: e.tensor_tensor(out=modT.t[:, c0:c0 + 4], in0=pb.t[:, 0:4],
                                                             in1=badac.t[:, b0:b0 + 4], op=ALU.add),
                      reads=[pb, badac], writes=[modT])

            def ada_rows_half(sec, dstT, half, tile, ps=None):
                wt = load_w(wada_d[:, sec * D + half * 512: sec * D + (half + 1) * 512], 512, tile=tile)
                pb = (ps or pall).next()
                for kc in range(KC):
                    S_.op("pe", lambda e, wt=wt, kc=kc, pb=pb: e.matmul(
                        pb.t[:, :], lhsT=csb_bf.t[:, kc, :], rhs=wt.t[:, kc, :], start=(kc == 0), stop=(kc == KC - 1)),
                        reads=[wt, csb_bf], writes=[pb], inc=(kc == KC - 1))
                S_.op("dve", lambda e, pb=pb, half=half: e.tensor_tensor(
                    out=dstT.t[:, half * 512:(half + 1) * 512], in0=pb.t[:, :], in1=dstT.t[:, half * 512:(half + 1) * 512],
                    op=ALU.add), reads=[pb, dstT], writes=[dstT])

            def make_gsc(which):
                sc0 = 8 if which == 0 else 24
                g0 = 8 if which == 0 else 16
                S_.op("dve", lambda e: e.scalar_tensor_tensor(
                    out=gsc.t[:, which * 8:(which + 1) * 8], in0=modT.t[:, sc0:sc0 + 8], scalar=1.0,
                    in1=small.t[:, g0:g0 + 8], op0=ALU.add, op1=ALU.mult), reads=[modT, small], writes=[gsc])

            def ada_first():
                ada_cols(0, 0)
                ada_cols(1, 8)
                make_gsc(0)

            def norm_to_T(src_tile_fn, nidx, dstbuf, dst_d, sh0, gs0, scope, hook=None, after_chunk=None):
                junk = T(sb("junk%d" % nidx, [128, D], BF16, scope))
                xs = [T(sb("xs%d_%d" % (nidx, i), [128, D], BF16, scope)) for i in range(8)]
                sq = T(sb("sq%d" % nidx, [128, 8], F32, scope))
                sq_c = [T() for _ in range(8)]
                ssq_c = [T() for _ in range(NT)]
                rstd_c = [T() for _ in range(NT)]

                def stats(tc):
                    for i in range(4):
                        tt = tc * 4 + i
                        col = nidx * NT + tt
                        b = (tc % 2) * 4 + i
                        xt, xap = src_tile_fn(tt)
                        S_.op("act", lambda e: e.activation(
                            out=junk.t[:], in_=xap, func=AF.Square, accum_out=ssq.t[:, col:col + 1]),
                            reads=[xt], writes=[junk, ssq_c[tt]])
                        S_.op("act", lambda e: e.activation(
                            out=sq.t[:, b:b + 1], in_=ssq.t[:, col:col + 1], func=AF.Sqrt, scale=1.0 / D, bias=EPS),
                            reads=[ssq_c[tt]], writes=[sq_c[b]])
                        S_.op("dve", lambda e: e.reciprocal(out=rstd.t[:, col:col + 1], in_=sq.t[:, b:b + 1]),
                              reads=[sq_c[b]], writes=[rstd_c[tt]])
                        S_.op("dve", lambda e: e.tensor_scalar_mul(
                            out=xs[b].t[:], in0=xap, scalar1=rstd.t[:, col:col + 1]),
                            reads=[xt, rstd_c[tt]], writes=[xs[b]])

                def tr_evac(tc):
                    for c in range(KC):
                        h = c % 2
                        pt = PTh[h]
                        for i in range(4):
                            b = (tc % 2) * 4 + i
                            S_.op("pe", lambda e: e.transpose(
                                PTv[h][:, i * 128:(i + 1) * 128], xs[b].t[:, c * 128:(c + 1) * 128], ident_bf.t[:]),
                                reads=[xs[b], ident_bf], writes=[pt], inc=(i == 3))
                        if h == 0:
                            S_.op("act", lambda e: e.activation(
                                out=dstbuf[:, c, tc * 512:(tc + 1) * 512], in_=PTv[h][:, 0:512], func=AF.Identity,
                                scale=gsc.t[:, gs0 + c:gs0 + c + 1], bias=modT.t[:, sh0 + c:sh0 + c + 1]),
                                reads=[pt, gsc, modT], writes=[dst_d[tc]])
                        else:
                            S_.op("dve", lambda e: e.tensor_scalar(
                                out=dstbuf[:, c, tc * 512:(tc + 1) * 512], in0=PTv[h][:, 0:512],
                                scalar1=gsc.t[:, gs0 + c:gs0 + c + 1], scalar2=modT.t[:, sh0 + c:sh0 + c + 1],
                                op0=ALU.mult, op1=ALU.add), reads=[pt, gsc, modT], writes=[dst_d[tc]])

                stats(0)
                stats(1)
                if hook is not None:
                    hook()
                for tc in range(NC4):
                    tr_evac(tc)
                    if tc + 2 < NC4:
                        stats(tc + 2)
                    if after_chunk is not None:
                        after_chunk(tc)

            sf = es.enter_context(ExitStack())
            Vall = T(sb("Vall", [128, NT, 512], BF16, sf))
            kz = [T(sb("kzA%d" % i, [128, S], BF16, sf)) for i in range(2)]
            qz = [T(sb("qzA%d" % i, [128, S], BF16, sf)) for i in range(2)]
            onesr = T(sb("onesr", [8, 512], F32, sf))
            Fb8 = T(sb("Fb8", [128, S], BF16, sf))
            wv_box = []

            def ada_first_and_wv():
                ada_first()
                wv_box.append(load_w(wva_d, 512, tile=WT[2]))
                S_.op("pool", lambda e: e.memset(qz[0].t[:], 0.0), writes=[qz[0]])
                S_.op("pool", lambda e: e.memset(qz[1].t[:], 0.0), writes=[qz[1]])
                S_.op("pool", lambda e: e.memset(Fb8.t[:], 0.0), writes=[Fb8])
                S_.op("pool", lambda e: e.memset(kz[0].t[:], 0.0), writes=[kz[0]])
                S_.op("pool", lambda e: e.memset(kz[1].t[:], 0.0), writes=[kz[1]])
                S_.op("pool", lambda e: e.memset(kz[0].t[64:65, :], 1.0), writes=[kz[0]])
                S_.op("pool", lambda e: e.memset(kz[1].t[0:1, :], 1.0), writes=[kz[1]])
                S_.op("pool", lambda e: e.memset(onesr.t[:], 1.0), writes=[onesr])

            def v_chunk(tc):
                wv = wv_box[0]
                for tt in range(4 * tc, 4 * tc + 4):
                    pb = pall.next()
                    for kc in range(KC):
                        S_.op("pe", lambda e, kc=kc, tt=tt, pb=pb: e.matmul(
                            pb.t[:, :], lhsT=bufA[:, kc, tt * 128:(tt + 1) * 128], rhs=wv.t[:, kc, :],
                            start=(kc == 0), stop=(kc == KC - 1)), reads=[wv, bufA_d[tc]], writes=[pb], inc=(kc == KC - 1))
                    S_.op("act", lambda e, tt=tt, pb=pb: e.activation(out=Vall.t[:, tt, :], in_=pb.t[:, :], func=AF.Copy),
                          reads=[pb], writes=[Vall])

            with ExitStack() as sc1:
                xin = [T(sb("xin%d" % i, [128, D], F32, sc1)) for i in range(4)]
                xrot = Rot(xin)

                def x_from_hbm(tt):
                    xt = xrot.next()
                    S_.dma("sp", lambda e, xt=xt, tt=tt: e.dma_start(out=xt.t[:], in_=x_d[tt * 128:(tt + 1) * 128, :]), writes=[xt])
                    return xt, xt.t[:]
                norm_to_T(x_from_hbm, 0, bufA, bufA_d, 0, 0, sc1, hook=ada_first_and_wv, after_chunk=v_chunk)
            S_.barrier()
            S_.dma("sp", lambda e: e.dma_start(out=gfinBC.t[:], in_=gfin_d), writes=[gfinBC])
            S_.dma("sp", lambda e: e.dma_start(out=gaBCm.t[:], in_=bgam_d), writes=[gaBCm])
            S_.dma("sp", lambda e: e.dma_start(out=gaBCf.t[:], in_=bgaf_d), writes=[gaBCf])
            dbg("hT", bufA_d, bufA[:], [128, KC, S], BF16)

            deferred = [lambda t, p: ada_rows_half(2, gaBCm, 0, t, p), lambda t, p: ada_rows_half(2, gaBCm, 1, t, p),
                        lambda t, p: ada_cols_half(3, 16, 0, t, p), lambda t, p: ada_cols_half(3, 16, 1, t, p),
                        lambda t, p: ada_cols_half(4, 24, 0, t, p), lambda t, p: (ada_cols_half(4, 24, 1, t, p), make_gsc(1)),
                        lambda t, p: ada_rows_half(5, gaBCf, 0, t, p), lambda t, p: ada_rows_half(5, gaBCf, 1, t, p)]

            STB = Rot([PB[0], PB[1], PB[2], PB[7]])
            ACC = Rot([(PB[3], PB[4]), (PB[5], PB[6])])
            if True:
                yaT = sb("yaT", [128, 4, S], BF16, sf)
                yaT_d = [T() for _ in range(NC4)]
                wbrA = T(sb("wbrA", [128, 4, D], BF16, sf))
                PTb = Rot([T(sb("PTbA%d" % i, [128, 512], BF16, sf)) for i in range(3)])
                rec = Rot([T(sb("recA%d" % i, [128, 512], F32, sf)) for i in range(2)])
                Grow = T(sb("Grow", [8, S], F32, sf))
                spr = T(sb("spr", [8, S], F32, sf))
                Gtok = T(sb("Gtok", [128, NT, 8], F32, sf))
                nbf = T(sb("nbf", [8, 2], F32, sf))
                etmp = T(sb("etmp", [8, 512], F32, sf))

                wf = WT[1]
                S_.dma("pool", lambda e: e.dma_start(out=wf.t[:, :, 0:8], in_=wfa_d.rearrange("p (kc n) -> p kc n", n=8)), writes=[wf])
                wq = load_w(wqa_d, 512, tile=WT[0])
                S_.dma("sp", lambda e: e.dma_start(out=nbf.t[:, 0:1], in_=bfg_d), writes=[nbf])
                S_.op("dve", lambda e: e.tensor_scalar_mul(out=nbf.t[:, 1:2], in0=nbf.t[:, 0:1], scalar1=-1.0),
                      reads=[nbf], writes=[nbf])

                for tc in range(NC4):
                    pb = pall.next()
                    for kc in range(KC):
                        S_.op("pe", lambda e, kc=kc, tc=tc, pb=pb: e.matmul(
                            pb.t[0:8, :], lhsT=wf.t[:, kc, 0:8], rhs=bufA[:, kc, tc * 512:(tc + 1) * 512],
                            start=(kc == 0), stop=(kc == KC - 1)), reads=[wf, bufA_d[tc]], writes=[pb], inc=(kc == KC - 1))
                    S_.op("act", lambda e, pb=pb: e.activation(out=etmp.t[:], in_=pb.t[0:8, :], func=AF.Exp, scale=-1.0,
                                                               bias=nbf.t[:, 1:2]), reads=[pb, nbf], writes=[etmp])
                    S_.op("act", lambda e, tc=tc: e.activation(out=spr.t[:, tc * 512:(tc + 1) * 512], in_=etmp.t[:], func=AF.Ln,
                                                               bias=1.0), reads=[etmp], writes=[spr])
                    if tc == 0:
                        S_.op("dve", lambda e: e.tensor_tensor_scan(out=Grow.t[:, 0:512], data0=onesr.t[:], data1=spr.t[:, 0:512],
                                                                    initial=0.0, op0=ALU.mult, op1=ALU.add),
                              reads=[onesr, spr], writes=[Grow])
                    else:
                        S_.op("dve", lambda e, tc=tc: e.tensor_tensor_scan(
                            out=Grow.t[:, tc * 512:(tc + 1) * 512], data0=onesr.t[:], data1=spr.t[:, tc * 512:(tc + 1) * 512],
                            initial=Grow.t[:, tc * 512 - 1:tc * 512], op0=ALU.mult, op1=ALU.add),
                            reads=[onesr, spr, Grow], writes=[Grow])
                S_.op("dve", lambda e: e.tensor_scalar_mul(out=Fb8.t[0:8, :], in0=Grow.t[:], scalar1=-8.0),
                      reads=[Grow], writes=[Fb8])
                pb = pall.next()
                for tt in range(NT):
                    S_.op("pe", lambda e, tt=tt, pb=pb: e.transpose(pb.t[:, tt * 8:(tt + 1) * 8], Grow.t[0:8, tt * 128:(tt + 1) * 128],
                                                                   ident_f.t[0:8, 0:8]),
                          reads=[Grow, ident_f], writes=[pb], inc=(tt == NT - 1))
                S_.op("dve", lambda e, pb=pb: e.tensor_copy(out=Gtok.t[:].rearrange("p a b -> p (a b)"), in_=pb.t[:, 0:128]),
                      reads=[pb], writes=[Gtok])
                dbg("Grow", [Grow], Grow.t[:], [8, S])

                wk = load_w(wka_d, 512, tile=WT[1])
                S_.dma("pool", lambda e: e.dma_start(out=wbrA.t[:], in_=wbra_d.rearrange("(pr p) n -> p pr n", p=128)), writes=[wbrA])

                for pr in range(4):
                    for tc in range(NC4):
                        pb = pall.next()
                        for kc in range(KC):
                            S_.op("pe", lambda e, kc=kc, tc=tc, pb=pb, pr=pr: e.matmul(
                                pb.t[:, :], lhsT=wq.t[:, kc, pr * 128:(pr + 1) * 128], rhs=bufA[:, kc, tc * 512:(tc + 1) * 512],
                                start=(kc == 0), stop=(kc == KC - 1)), reads=[wq, bufA_d[tc]], writes=[pb], inc=(kc == KC - 1))
                        for hp in range(2):
                            S_.op("act", lambda e, tc=tc, pb=pb, hp=hp: e.activation(
                                out=qz[hp].t[hp * 64:(hp + 1) * 64, tc * 512:(tc + 1) * 512], in_=pb.t[hp * 64:(hp + 1) * 64, :],
                                func=AF.Copy), reads=[pb], writes=[qz[hp]])
                        pb = pall.next()
                        for kc in range(KC):
                            S_.op("pe", lambda e, kc=kc, tc=tc, pb=pb, pr=pr: e.matmul(
                                pb.t[:, :], lhsT=wk.t[:, kc, pr * 128:(pr + 1) * 128], rhs=bufA[:, kc, tc * 512:(tc + 1) * 512],
                                start=(kc == 0), stop=(kc == KC - 1)), reads=[wk, bufA_d[tc]], writes=[pb], inc=(kc == KC - 1))
                        for hp in range(2):
                            S_.op("dve", lambda e, tc=tc, pb=pb, hp=hp: e.tensor_copy(
                                out=kz[hp].t[hp * 64:(hp + 1) * 64, tc * 512:(tc + 1) * 512], in_=pb.t[hp * 64:(hp + 1) * 64, :]),
                                reads=[pb], writes=[kz[hp]])
                        for hp in range(2):
                            h = pr * 2 + hp
                            ar = 64 if hp == 0 else 0
                            pb = pall.next()
                            S_.op("pe", lambda e, pb=pb, h=h, tc=tc: e.matmul(
                                pb.t[:, :], lhsT=selones.t[:, h * 128:(h + 1) * 128], rhs=Fb8.t[:, tc * 512:(tc + 1) * 512],
                                start=True, stop=True), reads=[selones, Fb8], writes=[pb])
                            S_.op("act", lambda e, pb=pb, hp=hp, ar=ar, tc=tc: e.activation(
                                out=qz[hp].t[ar:ar + 1, tc * 512:(tc + 1) * 512], in_=pb.t[ar:ar + 1, :], func=AF.Copy),
                                reads=[pb], writes=[qz[hp]])
                    if pr == 3:
                        wga_pre = [load_w(wga_d[:, 0:512], 512, tile=WT[0]), load_w(wga_d[:, 512:1024], 512, tile=WT[1])]
                    its = []
                    for hp in range(2):
                        for g in range(4):
                            for kb in range(4 * g + 4):
                                its.append(dict(hp=hp, g=g, kb=kb, nkb=4 * g + 4))

                    def qk(it):
                        hp, g, kb = it["hp"], it["g"], it["kb"]
                        if kb == 0:
                            it["acc"] = ACC.next()
                        else:
                            it["acc"] = it["prev"]["acc"]
                        c0 = max(0, kb - 4 * g) * 128
                        q0 = g * 512 + c0
                        st = STB.next()
                        it["st"], it["c0"] = st, c0
                        diag = kb >= 4 * g
                        S_.op("pe", lambda e: e.matmul(
                            st.t[:, c0:512], lhsT=kz[hp].t[:, kb * 128:(kb + 1) * 128], rhs=qz[hp].t[:, q0:(g + 1) * 512],
                            start=True, stop=(not diag)), reads=[kz[hp], qz[hp]], writes=[st], inc=(not diag))
                        if diag:
                            S_.op("pe", lambda e: e.matmul(
                                st.t[:, c0:c0 + 128], lhsT=ident_bf.t[:], rhs=maskX.t[:, 0:128], start=False, stop=True),
                                reads=[ident_bf, maskX], writes=[st], inc=True)

                    def ex_pv(it):
                        hp, g, kb, nkb = it["hp"], it["g"], it["kb"], it["nkb"]
                        h = pr * 2 + hp
                        st, c0 = it["st"], it["c0"]
                        num, den = it["acc"]
                        pt = PTb.next()
                        S_.op("act", lambda e: e.activation(
                            out=pt.t[:, c0:512], in_=st.t[:, c0:512], func=AF.Exp, scale=SCALE, bias=Gtok.t[:, kb, h:h + 1]),
                            reads=[st, Gtok], writes=[pt])
                        S_.op("pe", lambda e: e.matmul(
                            num.t[:, c0:512], lhsT=Vall.t[:, kb, pr * 128:(pr + 1) * 128], rhs=pt.t[:, c0:512],
                            start=(kb == 0), stop=(kb == nkb - 1), skip_group_check=True),
                            reads=[Vall, pt], writes=[num], inc=False)
                        S_.op("pe", lambda e: e.matmul(
                            den.t[:, c0:512], lhsT=ones_bf.t[:], rhs=pt.t[:, c0:512],
                            start=(kb == 0), stop=(kb == nkb - 1), skip_group_check=True),
                            reads=[ones_bf, pt], writes=[den], inc=True)
                        if kb == nkb - 1:
                            lanes = slice(hp * 64, (hp + 1) * 64)
                            rc = rec.next()
                            S_.op("dve", lambda e: e.reciprocal(out=rc.t[lanes, :], in_=den.t[lanes, :]), reads=[den], writes=[rc])
                            S_.op("dve", lambda e: e.tensor_tensor(
                                out=yaT[lanes, pr, g * 512:(g + 1) * 512], in0=num.t[lanes, :], in1=rc.t[lanes, :], op=ALU.mult),
                                reads=[num, rc], writes=[yaT_d[g]])

                    for i, it in enumerate(its):
                        it["prev"] = its[i - 1] if i > 0 else None
                    AH = 2
                    for i in range(AH):
                        qk(its[i])
                    for i, it in enumerate(its):
                        if i + AH < len(its):
                            qk(its[i + AH])
                        ex_pv(it)
                        if i == len(its) // 2 or i == len(its) - 1:
                            deferred.pop(0)(WT[2], STB)
                dbg("yaT", yaT_d, yaT[:], [128, 4, S], BF16)

                wbr = wbrA
                sg = Rot([T(sb("sgA%d" % i, [128, 512], F32, sf)) for i in range(2)])
                wqb_pre = load_w(wqb_d[:, 0:512], 512, tile=WT[2])
                for half in range(2):
                    wg = wga_pre[half]
                    if half == 1:
                        wkb_pre = load_w(wkb_d[:, 0:512], 512, tile=WT[0])
                    for j in range(4):
                        nch = half * 4 + j
                        for tc in range(NC4):
                            pg = pall.next()
                            for kc in range(KC):
                                S_.op("pe", lambda e, kc=kc, tc=tc, pg=pg, j=j, wg=wg: e.matmul(
                                    pg.t[:, :], lhsT=wg.t[:, kc, j * 128:(j + 1) * 128], rhs=bufA[:, kc, tc * 512:(tc + 1) * 512],
                                    start=(kc == 0), stop=(kc == KC - 1)), reads=[wg, bufA_d[tc]], writes=[pg], inc=(kc == KC - 1))
                            s_ = sg.next()
                            S_.op("act", lambda e, pg=pg, s_=s_: e.activation(out=s_.t[:], in_=pg.t[:, :], func=AF.Sigmoid),
                                  reads=[pg], writes=[s_])
                            pbr = pall.next()
                            for pr in range(4):
                                S_.op("pe", lambda e, pr=pr, tc=tc, pbr=pbr, nch=nch: e.matmul(
                                    pbr.t[:, :], lhsT=wbr.t[:, pr, nch * 128:(nch + 1) * 128], rhs=yaT[:, pr, tc * 512:(tc + 1) * 512],
                                    start=(pr == 0), stop=(pr == 3)), reads=[wbr, yaT_d[tc]], writes=[pbr], inc=(pr == 3))
                            S_.op("dve", lambda e, pbr=pbr, s_=s_, nch=nch, tc=tc: e.tensor_tensor(
                                out=bufB[:, nch, tc * 512:(tc + 1) * 512], in0=pbr.t[:, :], in1=s_.t[:], op=ALU.mult),
                                reads=[pbr, s_], writes=[bufB_d[tc]])
            sf.close()
            S_.barrier()
            dbg("mA", bufB_d, bufB[:], [128, KC, S], BF16)

            with ExitStack() as sd:
                ybT = sb("ybT", [128, 2, S], BF16, sd)
                ybT_d = [T() for _ in range(2)]
                wbrB = T(sb("wbrB", [128, 2, D], BF16, sd))
                S_.dma("pool", lambda e: e.dma_start(out=wbrB.t[:], in_=wbrb_d.rearrange("(pr p) n -> p pr n", p=128)), writes=[wbrB])
                sda = sd.enter_context(ExitStack())
                cosT = T(sb("cosT", [128, S], F32, sda))
                sinT = T(sb("sinT", [128, S], F32, sda))
                S_.dma("sp", lambda e: e.dma_start(out=cosT.t[:], in_=cos_d), writes=[cosT])
                S_.dma("sp", lambda e: e.dma_start(out=sinT.t[:], in_=sin_d), writes=[sinT])
                accN = T(sb("accN", [128, S], F32, sda))
                accD = T(sb("accD", [128, S], F32, sda))
                kp = T(sb("kpB", [128, S], BF16, sda))
                qz = [T(sb("qzB%d" % i, [128, S], BF16, sda)) for i in range(2)]
                Vp = T(sb("VpB", [128, NT, 128], BF16, sda))
                PTb = Rot([T(sb("PTbB%d" % i, [128, 512], BF16, sda)) for i in range(3)])
                qtmp = Rot([T(sb("qtmp%d" % i, [128, 512], BF16, sda)) for i in range(2)])
                t1r = Rot([T(sb("t1r%d" % i, [128, 512], F32, sda)) for i in range(2)])
                t2r = Rot([T(sb("t2r%d" % i, [128, 512], F32, sda)) for i in range(2)])
                S_.op("pool", lambda e: e.memset(qz[0].t[:], 0.0), writes=[qz[0]])
                S_.op("pool", lambda e: e.memset(qz[1].t[:], 0.0), writes=[qz[1]])
                wqb = [wqb_pre, None]
                wkb = [wkb_pre, None]
                wvb = [load_w(wvb_d[:, 0:512], 512, tile=WT[1]), None]
                WX = [T(sb("wx%d" % i, [128, KC, 256], BF16, sda)) for i in range(3)]
                for i, src in enumerate((wqb_d, wkb_d, wvb_d)):
                    S_.dma("pool", lambda e, i=i, src=src: e.dma_start(
                        out=WX[i].t[:], in_=src[:, 512:768].rearrange("(kc p) n -> p kc n", p=128)), writes=[WX[i]])

                def wsel(main, extra, pc):
                    if pc < 4:
                        return main, main.t, pc * 128
                    return extra, extra.t, (pc - 4) * 128

                chk("c1")
                def make_step(m, g, qz, kp):
                    d = DIL[g]
                    pc = 2 * g + m
                    nbk = 16 // d
                    wq_t, wq_ap, wq_c = wsel(wqb[0], WX[0], pc)
                    wk_t, wk_ap, wk_c = wsel(wkb[0], WX[1], pc)
                    wv_t, wv_ap, wv_c = wsel(wvb[0], WX[2], pc)
                    def f_proj():
                        pend = []
                        for which in range(2):
                            w_t, w_ap, w_c = (wq_t, wq_ap, wq_c) if which == 0 else (wk_t, wk_ap, wk_c)
                            for j in range(4):
                                pq = pall.next()
                                for kc in range(KC):
                                    S_.op("pe", lambda e, kc=kc, j=j, pq=pq, w_ap=w_ap, w_c=w_c, g=g: e.matmul(
                                        pq.t[:, :], lhsT=w_ap[:, kc, w_c:w_c + 128], rhs=bufA[:, kc, j * 512:(j + 1) * 512],
                                        start=(kc == 0), stop=(kc == KC - 1)), reads=[w_t, bufA_d[j]], writes=[pq], inc=(kc == KC - 1))
                                chk("c2")
                                qt = qtmp.next()
                                S_.op("act", lambda e, pq=pq, qt=qt: e.activation(out=qt.t[:], in_=pq.t[:, :], func=AF.Copy),
                                      reads=[pq], writes=[qt])
                                def post(pq=pq, qt=qt, j=j, which=which):
                                    psw = pall.next()
                                    S_.op("pe", lambda e, psw=psw, qt=qt: e.matmul(psw.t[:, :], lhsT=pm_bf.t[:], rhs=qt.t[:], start=True, stop=True),
                                          reads=[pm_bf, qt], writes=[psw])
                                    chk("c4")
                                    t1 = t1r.next()
                                    t2 = t2r.next()
                                    S_.op("dve", lambda e, pq=pq, t1=t1, g=g, j=j: e.tensor_tensor(
                                        out=t1.t[:], in0=pq.t[:, :], in1=cosT.t[:, j * 512:(j + 1) * 512], op=ALU.mult),
                                        reads=[pq, cosT], writes=[t1])
                                    chk("c4b")
                                    S_.op("dve", lambda e, psw=psw, t2=t2, g=g, j=j: e.tensor_tensor(
                                        out=t2.t[:], in0=psw.t[:, :], in1=sinT.t[:, j * 512:(j + 1) * 512], op=ALU.mult),
                                        reads=[psw, sinT], writes=[t2])
                                    chk("c5")
                                    if which == 0:
                                        for hp in range(2):
                                            S_.op("pool", lambda e, t1=t1, t2=t2, hp=hp, j=j: e.tensor_tensor(
                                                out=cls_out(qz[hp].t[hp * 64:(hp + 1) * 64, :], g, j), in0=nat_in(t1.t[hp * 64:(hp + 1) * 64, :], g),
                                                in1=nat_in(t2.t[hp * 64:(hp + 1) * 64, :], g), op=ALU.add), reads=[t1, t2], writes=[qz[hp]])
                                    else:
                                        S_.op("pool", lambda e, t1=t1, t2=t2, j=j: e.tensor_tensor(
                                            out=cls_out(kp.t[:, :], g, j), in0=nat_in(t1.t[:], g), in1=nat_in(t2.t[:], g), op=ALU.add),
                                            reads=[t1, t2], writes=[kp])
                                pend.append(post)
                                if len(pend) > 1:
                                    pend.pop(0)()
                        while pend:
                            pend.pop(0)()
                        chk("dq%d%d" % (m, g))
                    def f_v():
                        for cb4 in range(4):
                            pv = pall.next()
                            for i in range(4):
                                cb = cb4 * 4 + i
                                r, n_ = cb // nbk, cb % nbk
                                t0 = d * 128 * n_ + r
                                for kc in range(KC):
                                    S_.op("pe", lambda e, kc=kc, i=i, pv=pv, t0=t0, d=d, wv_ap=wv_ap, wv_c=wv_c: e.matmul(
                                        pv.t[:, i * 128:(i + 1) * 128], lhsT=bufA[:, kc, t0:t0 + 127 * d + 1:d], rhs=wv_ap[:, kc, wv_c:wv_c + 128],
                                        start=(kc == 0), stop=(kc == KC - 1)), reads=[wv_t] + bufA_d, writes=[pv],
                                        inc=(kc == KC - 1 and i == 3))
                            S_.op("act", lambda e, pv=pv, cb4=cb4: e.activation(
                                out=Vp.t[:, cb4 * 4:(cb4 + 1) * 4, :].rearrange("p a b -> p (a b)"), in_=pv.t[:, :], func=AF.Copy),
                                reads=[pv], writes=[Vp])
                        chk("dv%d%d" % (m, g))
                    def f_att():
                        units = []
                        for hp in range(2):
                            if g < 2:
                                for cb in range(16):
                                    hasnext = (cb % nbk) < nbk - 1
                                    units.append(dict(hp=hp, kbs=[cb], qcols=(cb * 128, (cb + (2 if hasnext else 1)) * 128),
                                                      mask0=0))
                            else:
                                for j in range(4):
                                    units.append(dict(hp=hp, kbs=[4 * j + i for i in range(4)], qcols=(j * 512, (j + 1) * 512),
                                                      mask0=256))
                        accs = {}

                        def acc_of(hp, j):
                            if (hp, j) not in accs:
                                accs[(hp, j)] = ACC.next()
                            return accs[(hp, j)]

                        def d_qk(u):
                            hp = u["hp"]
                            st = STB.next()
                            u["st"] = st
                            q0, q1 = u["qcols"]
                            n = q1 - q0
                            u["n"] = n
                            if g < 2:
                                cb = u["kbs"][0]
                                S_.op("pe", lambda e: e.matmul(
                                    st.t[:, 0:n], lhsT=kp.t[:, cb * 128:(cb + 1) * 128], rhs=qz[hp].t[:, q0:q1],
                                    start=True, stop=False), reads=[kp, qz[hp]], writes=[st], inc=False)
                            else:
                                for i, cb in enumerate(u["kbs"]):
                                    S_.op("pe", lambda e: e.matmul(
                                        st.t[:, i * 128:(i + 1) * 128], lhsT=kp.t[:, cb * 128:(cb + 1) * 128],
                                        rhs=qz[hp].t[:, cb * 128:(cb + 1) * 128], start=(i == 0), stop=False),
                                        reads=[kp, qz[hp]], writes=[st], inc=False)
                            m0 = u["mask0"]
                            S_.op("pe", lambda e: e.matmul(
                                st.t[:, 0:n], lhsT=ident_bf.t[:], rhs=maskX.t[:, m0:m0 + n], start=False, stop=True),
                                reads=[ident_bf, maskX], writes=[st], inc=True)

                        def d_pv(u):
                            hp, st, n = u["hp"], u["st"], u["n"]
                            lanes = slice(hp * 64, (hp + 1) * 64)
                            pt = PTb.next()
                            S_.op("act", lambda e: e.activation(out=pt.t[:, 0:n], in_=st.t[:, 0:n], func=AF.Exp, scale=SCALE),
                                  reads=[st], writes=[pt])
                            contribs = []
                            if g < 2:
                                cb = u["kbs"][0]
                                contribs.append((cb, 0, cb, (cb % nbk) == 0))
                                if n == 256:
                                    contribs.append((cb, 128, cb + 1, True))
                            else:
                                for i, cb in enumerate(u["kbs"]):
                                    contribs.append((cb, i * 128, cb, True))
                            for (kb_, pc, qb_, first) in contribs:
                                num, den = acc_of(hp, qb_ // 4)
                                cols = slice((qb_ % 4) * 128, (qb_ % 4 + 1) * 128)
                                last = (kb_ == qb_)
                                S_.op("pe", lambda e: e.matmul(
                                    num.t[:, cols], lhsT=Vp.t[:, kb_, :], rhs=pt.t[:, pc:pc + 128], start=first, stop=last),
                                    reads=[Vp, pt], writes=[num], inc=False)
                                S_.op("pe", lambda e: e.matmul(
                                    den.t[:, cols], lhsT=ones_bf.t[:], rhs=pt.t[:, pc:pc + 128], start=first, stop=last),
                                    reads=[ones_bf, pt], writes=[den], inc=True)
                                if last and qb_ % 4 == 3:
                                    j = qb_ // 4
                                    for (acc, src) in ((accN, num), (accD, den)):
                                        if g == 0:
                                            S_.op("dve", lambda e: e.tensor_copy(
                                                out=tok_view(acc.t[lanes, :], g, j), in_=chunk_view(src.t[lanes, :], g)),
                                                reads=[src], writes=[acc])
                                        else:
                                            S_.op("dve", lambda e: e.tensor_tensor(
                                                out=tok_view(acc.t[lanes, :], g, j), in0=chunk_view(src.t[lanes, :], g),
                                                in1=tok_view(acc.t[lanes, :], g, j), op=ALU.add), reads=[src, acc], writes=[acc])

                        AHEAD = 2
                        for i in range(min(AHEAD, len(units))):
                            d_qk(units[i])
                        for i, u in enumerate(units):
                            if i + AHEAD < len(units):
                                d_qk(units[i + AHEAD])
                            d_pv(u)
                        chk("da%d%d" % (m, g))
                    return f_proj, f_v, f_att

                def finalize(m):
                    S_.op("act", lambda e: e.activation(out=accD.t[:], in_=accD.t[:], func=AF.Ln), reads=[accD], writes=[accD])
                    S_.op("act", lambda e: e.activation(out=accD.t[:], in_=accD.t[:], func=AF.Exp, scale=-1.0), reads=[accD], writes=[accD])
                    S_.op("dve", lambda e, m=m: e.tensor_tensor(out=ybT[:, m, :], in0=accN.t[:], in1=accD.t[:], op=ALU.mult),
                          reads=[accN, accD], writes=[ybT_d[m]])

                qzs = [qz, [T(sb("qzC%d" % i, [128, S], BF16, sda)) for i in range(2)]]
                kps = [kp, T(sb("kpC", [128, S], BF16, sda))]
                S_.op("pool", lambda e: e.memset(qzs[1][0].t[:], 0.0), writes=[qzs[1][0]])
                S_.op("pool", lambda e: e.memset(qzs[1][1].t[:], 0.0), writes=[qzs[1][1]])
                steps = [make_step(m_, g_, qzs[(m_ * 3 + g_) % 2], kps[(m_ * 3 + g_) % 2]) for m_ in range(2) for g_ in range(3)]
                steps[0][0]()
                steps[0][1]()
                for s_i in range(6):
                    if s_i + 1 < 6:
                        steps[s_i + 1][0]()
                    if s_i == 4:
                        wgb_pre = [load_w(wgb_d[:, 0:512], 512, tile=wqb[0]), load_w(wgb_d[:, 512:1024], 512, tile=wkb[0])]
                    if s_i == 5:
                        wo_pre0 = load_w(wout_d[:, 0:512], 512, tile=wvb[0])
                    steps[s_i][2]()
                    if s_i % 3 == 2:
                        finalize(s_i // 3)
                    if s_i + 1 < 6:
                        steps[s_i + 1][1]()
                sda.close()
                S_.barrier()
                dbg("ybT", ybT_d, ybT[:], [128, 2, S], BF16)

                wbr = wbrB
                sg = Rot([T(sb("sgB%d" % i, [128, 512], F32, sd)) for i in range(2)])
                tm = Rot([T(sb("tmB%d" % i, [128, 512], F32, sd)) for i in range(2)])
                S_.op("dve", lambda e: e.tensor_tensor(
                    out=wo_pre0.t[:], in0=wo_pre0.t[:],
                    in1=gaBCm.t[:, 0:512].unsqueeze(1).to_broadcast([128, KC, 512]), op=ALU.mult),
                    reads=[wo_pre0, gaBCm], writes=[wo_pre0])
                for half in range(2):
                    wg = wgb_pre[half]
                    if half == 1:
                        wo_pre1 = load_w(wout_d[:, 512:1024], 512, tile=wgb_pre[0])
                    for j in range(4):
                        nch = half * 4 + j
                        for tc in range(NC4):
                            pg = pall.next()
                            for kc in range(KC):
                                S_.op("pe", lambda e, kc=kc, tc=tc, pg=pg, j=j, wg=wg: e.matmul(
                                    pg.t[:, :], lhsT=wg.t[:, kc, j * 128:(j + 1) * 128], rhs=bufA[:, kc, tc * 512:(tc + 1) * 512],
                                    start=(kc == 0), stop=(kc == KC - 1)), reads=[wg, bufA_d[tc]], writes=[pg], inc=(kc == KC - 1))
                            s_ = sg.next()
                            S_.op("act", lambda e, pg=pg, s_=s_: e.activation(out=s_.t[:], in_=pg.t[:, :], func=AF.Sigmoid),
                                  reads=[pg], writes=[s_])
                            pbr = pall.next()
                            for pr in range(2):
                                S_.op("pe", lambda e, pr=pr, tc=tc, pbr=pbr, nch=nch: e.matmul(
                                    pbr.t[:, :], lhsT=wbr.t[:, pr, nch * 128:(nch + 1) * 128], rhs=ybT[:, pr, tc * 512:(tc + 1) * 512],
                                    start=(pr == 0), stop=(pr == 1)), reads=[wbr] + ybT_d, writes=[pbr], inc=(pr == 1))
                            t_ = tm.next()
                            S_.op("dve", lambda e, pbr=pbr, s_=s_, t_=t_: e.tensor_tensor(
                                out=t_.t[:], in0=pbr.t[:, :], in1=s_.t[:], op=ALU.mult), reads=[pbr, s_], writes=[t_])
                            S_.op("pool", lambda e, t_=t_, nch=nch, tc=tc: e.tensor_tensor(
                                out=bufB[:, nch, tc * 512:(tc + 1) * 512], in0=bufB[:, nch, tc * 512:(tc + 1) * 512], in1=t_.t[:], op=ALU.add),
                                reads=[t_, bufB_d[tc]], writes=[bufB_d[tc]])
                S_.op("dve", lambda e: e.tensor_tensor(
                    out=wo_pre1.t[:], in0=wo_pre1.t[:],
                    in1=gaBCm.t[:, 512:1024].unsqueeze(1).to_broadcast([128, KC, 512]), op=ALU.mult),
                    reads=[wo_pre1, gaBCm], writes=[wo_pre1])
            S_.barrier()
            dbg("merged", bufB_d, bufB[:], [128, KC, S], BF16)

            with ExitStack() as s2:
                x1 = sb("x1", [128, NT, D], F32, s2)
                x1_d = [T() for _ in range(NT)]
                WT4 = T(sb("wt4", [128, KC, 512], BF16, s2))
                for tt in range(NT):
                    S_.dma("sp", lambda e, tt=tt: e.dma_start(out=x1[:, tt, :], in_=x_d[tt * 128:(tt + 1) * 128, :]), writes=[x1_d[tt]])
                so = s2.enter_context(ExitStack())
                tmo = Rot([T(sb("tmo%d" % i, [128, 512], F32, so)) for i in range(3)])
                wo = [wo_pre0, wo_pre1]
                others = [t for t in WT if t is not wo_pre0 and t is not wo_pre1]
                pre_w = [load_w(wfg_d[:, 0:512], 512, tile=others[0]), load_w(wfu_d[:, 0:512], 512, tile=WT4)]
                wrot = Rot([wo_pre0, wo_pre1, others[0], WT4])

                def outproj_tile(tt):
                    for ch in range(2):
                        po = pall.next()
                        for kc in range(KC):
                            S_.op("pe", lambda e: e.matmul(
                                po.t[:, :], lhsT=bufB[:, kc, tt * 128:(tt + 1) * 128], rhs=wo[ch].t[:, kc, :],
                                start=(kc == 0), stop=(kc == KC - 1)), reads=[wo[ch], bufB_d[tt // 4]], writes=[po], inc=(kc == KC - 1))
                        S_.op("dve", lambda e: e.tensor_tensor(
                            out=x1[:, tt, ch * 512:(ch + 1) * 512], in0=po.t[:, :], in1=x1[:, tt, ch * 512:(ch + 1) * 512], op=ALU.add),
                            reads=[po, x1_d[tt]], writes=[x1_d[tt]])
                    return x1_d[tt], x1[:, tt, :]

                with ExitStack() as sn:
                    norm_to_T(outproj_tile, 1, bufA, bufA_d, 16, 8, sn)
                so.close()
                S_.barrier()
                dbg("x1", x1_d, x1[:], [128, NT, D])
                dbg("h2T", bufA_d, bufA[:], [128, KC, S], BF16)
                sf2 = s2.enter_context(ExitStack())
                tmo = Rot([T(sb("tmf%d" % i, [128, 512], F32, sf2)) for i in range(3)])

                wdt = [T(sb("wd%d" % i, [128, KC, D], BF16, sf2)) for i in range(1)]
                sa = Rot([T(sb("sa%d" % i, [128, 512], F32, sf2)) for i in range(2)])
                groups = [(0, 8), (8, 8), (16, 6)]
                sqf = T(sb("sqF", [128, NT], F32, sf2))

                def final_tile(tt):
                    col = 2 * NT + tt
                    jk = sa.next()
                    S_.op("act", lambda e: e.activation(out=jk.t[:].bitcast(BF16), in_=x1[:, tt, :], func=AF.Square,
                                                        accum_out=ssq.t[:, col:col + 1]), reads=[x1_d[tt]], writes=[jk, ssq])
                    S_.op("act", lambda e: e.activation(out=sqf.t[:, tt:tt + 1], in_=ssq.t[:, col:col + 1], func=AF.Sqrt,
                                                        scale=1.0 / D, bias=EPS), reads=[ssq], writes=[sqf])
                    S_.op("dve", lambda e: e.reciprocal(out=rstd.t[:, col:col + 1], in_=sqf.t[:, tt:tt + 1]), reads=[sqf], writes=[rstd])
                    for ch in range(2):
                        y = tmo.next()
                        S_.op("act", lambda e: e.activation(out=y.t[:], in_=x1[:, tt, ch * 512:(ch + 1) * 512], func=AF.Identity,
                                                            scale=rstd.t[:, col:col + 1]), reads=[x1_d[tt], rstd], writes=[y])
                        S_.op("dve", lambda e: e.tensor_tensor(out=y.t[:], in0=y.t[:], in1=gfinBC.t[:, ch * 512:(ch + 1) * 512],
                                                               op=ALU.mult), reads=[y, gfinBC], writes=[y])
                        S_.dma("sp", lambda e: e.dma_start(out=out_d[tt * 128:(tt + 1) * 128, ch * 512:(ch + 1) * 512], in_=y.t[:]),
                               reads=[y])
                for (f0, nf) in groups:
                    wd = wdt[0]
                    for q4 in range(0, nf, 4):
                        nq = min(4, nf - q4)
                        c0 = (f0 + q4) * 128
                        if pre_w:
                            wg, wu = pre_w
                            pre_w = None
                        else:
                            wg = load_w(wfg_d[:, c0:c0 + nq * 128], nq * 128)
                            wu = load_w(wfu_d[:, c0:c0 + nq * 128], nq * 128)
                        if q4 == 0:
                            S_.dma("pool", lambda e, f0=f0, nf=nf, wd=wd: e.dma_start(
                                out=wd.t[:, 0:nf, :], in_=wfd_d[f0 * 128:(f0 + nf) * 128, :].rearrange("(kc p) n -> p kc n", p=128)),
                                writes=[wd])
                            S_.op("pool", lambda e, nf=nf, wd=wd: e.tensor_tensor(
                                out=wd.t[:, 0:nf, :], in0=wd.t[:, 0:nf, :],
                                in1=gaBCf.t[:, :].unsqueeze(1).to_broadcast([128, nf, D]), op=ALU.mult),
                                reads=[wd, gaBCf], writes=[wd])
                        for jj in range(nq):
                            fl = q4 + jj
                            for tc in range(NC4):
                                pa = pall.next()
                                for kc in range(KC):
                                    S_.op("pe", lambda e, kc=kc, tc=tc, pa=pa, jj=jj, wg=wg: e.matmul(
                                        pa.t[:, :], lhsT=wg.t[:, kc, jj * 128:(jj + 1) * 128], rhs=bufA[:, kc, tc * 512:(tc + 1) * 512],
                                        start=(kc == 0), stop=(kc == KC - 1)), reads=[wg, bufA_d[tc]], writes=[pa], inc=(kc == KC - 1))
                                pu = pall.next()
                                for kc in range(KC):
                                    S_.op("pe", lambda e, kc=kc, tc=tc, pu=pu, jj=jj, wu=wu: e.matmul(
                                        pu.t[:, :], lhsT=wu.t[:, kc, jj * 128:(jj + 1) * 128], rhs=bufA[:, kc, tc * 512:(tc + 1) * 512],
                                        start=(kc == 0), stop=(kc == KC - 1)), reads=[wu, bufA_d[tc]], writes=[pu], inc=(kc == KC - 1))
                                s_ = sa.next()
                                S_.op("act", lambda e, pa=pa, s_=s_: e.activation(out=s_.t[:], in_=pa.t[:, :], func=AF.Silu),
                                      reads=[pa], writes=[s_])
                                S_.op("dve", lambda e, pu=pu, s_=s_, fl=fl, tc=tc: e.tensor_tensor(
                                    out=bufB[:, fl, tc * 512:(tc + 1) * 512], in0=pu.t[:, :], in1=s_.t[:], op=ALU.mult),
                                    reads=[pu, s_], writes=[bufB_d[tc]])
                    for tt in range(NT):
                        for ch in range(2):
                            po = pall.next()
                            for kc in range(nf):
                                S_.op("pe", lambda e, kc=kc, tt=tt, ch=ch, po=po, nf=nf, wd=wd: e.matmul(
                                    po.t[:, :], lhsT=bufB[:, kc, tt * 128:(tt + 1) * 128], rhs=wd.t[:, kc, ch * 512:(ch + 1) * 512],
                                    start=(kc == 0), stop=(kc == nf - 1)), reads=[wd, bufB_d[tt // 4]], writes=[po], inc=(kc == nf - 1))
                            S_.op("dve", lambda e, po=po, tt=tt, ch=ch: e.tensor_tensor(
                                out=x1[:, tt, ch * 512:(ch + 1) * 512], in0=po.t[:, :], in1=x1[:, tt, ch * 512:(ch + 1) * 512], op=ALU.add),
                                reads=[po, x1_d[tt]], writes=[x1_d[tt]])
                        if f0 + nf == NFF:
                            if tt >= 2:
                                final_tile(tt - 2)
                            if tt == NT - 1:
                                final_tile(NT - 2)
                                final_tile(NT - 1)

                dbg("x2", x1_d, x1[:], [128, NT, D])
                sf2.close()
        except _Stop:
            pass
        S_.finish()
        S_.emit()
    return nc, dbg_d


def _consts():
    ident = np.eye(128, dtype=np.float32)
    k = np.arange(128)[:, None]
    q = np.arange(128)[None, :]
    anti = np.where(k >= q, 0.0, NEG).astype(np.float32)
    caus = np.where(k <= q, 0.0, NEG).astype(np.float32)
    mask = np.concatenate([caus, anti, caus, caus, caus, caus], axis=1)
    sel = np.zeros((128, 8 * 128), np.float32)
    for h in range(8):
        sel[h, h * 128:(h + 1) * 128] = 1.0
    pm = np.zeros((128, 128), np.float32)
    cosf = np.ones((128, S), np.float32)
    sinf = np.zeros((128, S), np.float32)
    pos = np.arange(S, dtype=np.float32)
    inv_freq = (np.float32(500000.0) ** (-(np.arange(0, 16, 2, dtype=np.float32)) / np.float32(16))).astype(np.float32)
    ang = (pos[:, None] * inv_freq[None, :]).astype(np.float32)
    cs = np.cos(ang).astype(np.float32).T
    sn = np.sin(ang).astype(np.float32).T
    for hp in range(2):
        for dd in range(16):
            p = hp * 64 + dd
            i = dd % 8
            partner = p + 8 if dd < 8 else p - 8
            pm[partner, p] = 1.0
            cosf[p] = cs[i]
            sinf[p] = -sn[i] if dd < 8 else sn[i]
    ones = np.ones((128, 128), np.float32)
    return dict(k_ident=ident, k_mask=mask, k_sel=sel, k_pm=pm, k_cos=cosf, k_sin=sinf, k_ones=ones)


def _col(v, n):
    return np.ascontiguousarray(np.asarray(v, np.float32).reshape(n, 128).T)


def _prep_inputs(x, c, w_ada, b_ada, g_mix, w_in, b_fgate, w_br_a, w_br_b, w_out,
                 g_ffn, w_ffn_gate, w_ffn_up, w_ffn_down, g_final):
    f = lambda a: np.ascontiguousarray(np.asarray(a, dtype=np.float32))
    x, c = f(x), f(c)
    w_in0 = f(w_in)[0]
    cuts = np.cumsum([512, 512, 512, 8, 768, 768, 768, 1024, 1024])[:-1]
    qa, ka, va, fa, qb, kb, vb, ga, gb = [np.ascontiguousarray(p) for p in np.split(w_in0, cuts, axis=1)]
    b_ada0 = f(b_ada)[0]
    shared = dict(
        w_ada=f(w_ada)[0], b_ada_col=_col(b_ada0, 48),
        b_gam_bc=np.ascontiguousarray(np.broadcast_to(b_ada0[2 * D:3 * D], (128, D))),
        b_gaf_bc=np.ascontiguousarray(np.broadcast_to(b_ada0[5 * D:6 * D], (128, D))),
        g_mix_col=_col(f(g_mix)[0], KC), g_ffn_col=_col(f(g_ffn)[0], KC),
        g_fin_bc=np.ascontiguousarray(np.broadcast_to(f(g_final), (128, D))),
        b_fg_col=np.ascontiguousarray(f(b_fgate)[0].reshape(8, 1)),
        w_qa=qa, w_ka=ka, w_va=va, w_fa=np.ascontiguousarray(fa.reshape(KC, 128, 8).transpose(1, 0, 2).reshape(128, KC * 8)), w_qb=qb, w_kb=kb, w_vb=vb, w_ga=ga, w_gb=gb,
        w_br_a=f(w_br_a)[0], w_br_b=f(w_br_b)[0], w_out=f(w_out)[0],
        w_ffn_gate=f(w_ffn_gate)[0], w_ffn_up=f(w_ffn_up)[0], w_ffn_down=f(w_ffn_down)[0],
    )
    shared.update(_consts())
    in_maps = []
    for b in range(8):
        m = dict(shared)
        m["x"] = np.ascontiguousarray(x[b])
        m["c_col"] = _col(c[b], KC)
        in_maps.append(m)
    return in_maps


_NC_CACHE = {}


def kernel(**inputs):
    in_maps = _prep_inputs(**inputs)
    if "nc" not in _NC_CACHE:
        _NC_CACHE["nc"] = build_program()[0]
    nc = _NC_CACHE["nc"]
    res = run_bass_kernel_spmd(nc, in_maps, core_ids=list(range(8)))
    out = np.stack([np.asarray(r["out"], dtype=np.float32).reshape(S, D) for r in res.results], axis=0)
    return out
```

```python
import numpy as np
from contextlib import ExitStack
import concourse.bass as bass
import concourse.mybir as mybir
from concourse.bass_utils import run_bass_kernel_spmd

F32 = mybir.dt.float32
BF16 = mybir.dt.bfloat16
AF = mybir.ActivationFunctionType
ALU = mybir.AluOpType

D = 1024
S = 2048
NT = 16
NC4 = 4
KC = 8
DFF = 2816
NFF = 22
EPS = 1e-6
NEG = -30000.0
SCALE = 0.125
DIL = (1, 4, 16)
ENG = ("pe", "act", "dve", "pool", "sp")


class T:
    __slots__ = ("t", "w", "r", "ds", "excl")

    def __init__(self, t=None, excl=False):
        self.t = t
        self.excl = excl
        self.w = None
        self.r = {}
        self.ds = None


class Rec:
    def __init__(self):
        self.call = None

    def __getattr__(self, name):
        def f(*a, **k):
            self.call = (name, a, k)
        return f


def _record(fn):
    r = Rec()
    fn(r)
    assert r.call is not None
    return r.call


class Sched:
    def __init__(self, nc, es, n_dma=24):
        self.nc = nc
        self.sem = {e: es.enter_context(nc.semaphore("s_" + e)) for e in ENG}
        self.cnt = {e: 0 for e in ENG}
        self.es = es
        self.dsem = []
        self.dcnt = []
        self.dq = []
        self.inflight = {"pool": [], "sp": []}
        self.max_inflight = {"pool": 2, "sp": 4}
        self.streams = {e: [] for e in ENG}
        self.waited = {e: {} for e in ENG}
        self.stopped = False

    def _need(self, eng, deps):
        for key, val in deps:
            if key == eng and eng == "pe":
                continue
            if self.waited[eng].get(key, 0) >= val:
                continue
            self.waited[eng][key] = val
            self.streams[eng].append(("w", key, val))

    @staticmethod
    def _deps(reads, writes, eng=None):
        deps = []
        for t in reads:
            if t.w is not None:
                deps.append(t.w)
            if t.excl:
                deps.extend((k, v) for k, v in t.r.items() if k != eng)
        for t in writes:
            if t.w is not None:
                deps.append(t.w)
            deps.extend(t.r.items())
        return deps

    def op(self, eng, fn, reads=(), writes=(), inc=True):
        if self.stopped:
            return
        self._need(eng, self._deps(reads, writes, eng))
        val = self.cnt[eng] + 1
        if inc:
            self.cnt[eng] = val
        self.streams[eng].append(("o", _record(fn), inc))
        for t in reads:
            if t.r.get(eng, 0) < val:
                t.r[eng] = val
        for t in writes:
            t.w = (eng, val)
            t.r = {}

    def dma(self, q, fn, reads=(), writes=()):
        if self.stopped:
            return
        own = writes[0] if len(writes) else reads[0]
        if own.ds is None:
            own.ds = len(self.dsem)
            self.dsem.append(self.es.enter_context(self.nc.semaphore("d%d" % own.ds)))
            self.dcnt.append(0)
            self.dq.append(q)
        i = own.ds
        assert self.dq[i] == q
        key = ("d", i)
        deps = self._deps(reads, writes)
        if self.dcnt[i] > 0:
            deps.append((key, self.dcnt[i]))
        fl = self.inflight[q]
        while len(fl) >= self.max_inflight[q]:
            deps.append(fl.pop(0))
        self._need(q, deps)
        self.dcnt[i] += 16
        val = self.dcnt[i]
        fl.append((key, val))
        self.streams[q].append(("d", _record(fn), i))
        for t in reads:
            t.r[key] = val
        for t in writes:
            t.w = (key, val)
            t.r = {}

    def barrier(self):
        if self.stopped:
            return
        edeps = [(e, self.cnt[e]) for e in ENG if self.cnt[e] > 0]
        for q in ("sp", "pool"):
            ddeps = [(("d", i), v) for i, v in enumerate(self.dcnt) if v > 0 and self.dq[i] == q]
            self._need(q, edeps + ddeps)
            self.cnt[q] += 1
            self.streams[q].append(("o", ("nop", (), {}), True))
        edeps = [(e, self.cnt[e]) for e in ENG if self.cnt[e] > 0]
        for e in ENG:
            self._need(e, edeps)

    def finish(self):
        ddeps = [(("d", i), v) for i, v in enumerate(self.dcnt) if v > 0 and self.dq[i] == "pool"]
        if ddeps:
            self._need("pool", ddeps)
            self.cnt["pool"] += 1
            self.streams["pool"].append(("o", ("nop", (), {}), True))
        for i, v in enumerate(self.dcnt):
            if v > 0 and self.dq[i] == "sp":
                self._need("sp", [(("d", i), v)])
        for e in ENG:
            self._need("sp", [(e, self.cnt[e])] if self.cnt[e] > 0 else [])

    def emit(self):
        nc = self.nc
        needed = {e: set() for e in ENG}
        for e in ENG:
            for item in self.streams[e]:
                if item[0] == "w" and isinstance(item[1], str):
                    needed[item[1]].add(item[2])
        rank = {e: {v: i + 1 for i, v in enumerate(sorted(needed[e]))} for e in ENG}
        with nc.Block() as block:
            def mk(e):
                def body(eng):
                    prov = 0
                    for item in self.streams[e]:
                        if item[0] == "w":
                            key = item[1]
                            if isinstance(key, str):
                                eng.wait_ge(self.sem[key], rank[key][item[2]])
                            else:
                                eng.wait_ge(self.dsem[key[1]], item[2])
                        elif item[0] == "o":
                            c = item[1]
                            ins = getattr(eng, c[0])(*c[1], **c[2])
                            if item[2]:
                                prov += 1
                                if prov in needed[e]:
                                    ins.then_inc(self.sem[e], 1)
                        else:
                            c = item[1]
                            getattr(eng, c[0])(*c[1], **c[2]).then_inc(self.dsem[item[2]], 16)
                    assert prov == self.cnt[e], (e, prov, self.cnt[e])
                return body
            block.tensor(mk("pe"))
            block.scalar(mk("act"))
            block.vector(mk("dve"))
            block.gpsimd(mk("pool"))
            block.sync(mk("sp"))


class Rot:
    def __init__(self, items):
        self.items = items
        self.i = 0

    def next(self):
        x = self.items[self.i]
        self.i = (self.i + 1) % len(self.items)
        return x


def tok_view(ap2d, g, j):
    if g == 0:
        return ap2d[:, j * 512:(j + 1) * 512]
    if g == 1:
        return ap2d.rearrange("p (i r) -> p r i", r=4)[:, j, :]
    return ap2d.rearrange("p (i r) -> p r i", r=16)[:, 4 * j:4 * j + 4, :]


def cls_out(ap2d, g, tc):
    dd = DIL[g]
    if g == 0:
        return ap2d[:, tc * 512:(tc + 1) * 512]
    n = 512 // dd
    return ap2d.rearrange("p (r i) -> p r i", r=dd)[:, :, n * tc:n * (tc + 1)]


def nat_in(ap2d, g):
    if g == 0:
        return ap2d
    return ap2d.rearrange("p (i r) -> p r i", r=DIL[g])


def chunk_view(ap2d, g):
    if g == 2:
        return ap2d.rearrange("p (a b) -> p a b", a=4)
    return ap2d


class _Stop(Exception):
    pass


def build_program(debug=(), stop=None):
    nc = bass.Bass("TRN2", target_bir_lowering=False)

    def din(name, shape):
        return nc.dram_tensor(name, list(shape), F32, kind="ExternalInput").ap()

    x_d = din("x", [S, D])
    ccol_d = din("c_col", [128, KC])
    wada_d = din("w_ada", [D, 6 * D])
    badac_d = din("b_ada_col", [128, 48])
    bgam_d = din("b_gam_bc", [128, D])
    bgaf_d = din("b_gaf_bc", [128, D])
    gmix_d = din("g_mix_col", [128, KC])
    gffn_d = din("g_ffn_col", [128, KC])
    gfin_d = din("g_fin_bc", [128, D])
    bfg_d = din("b_fg_col", [8, 1])
    wqa_d = din("w_qa", [D, 512])
    wka_d = din("w_ka", [D, 512])
    wva_d = din("w_va", [D, 512])
    wfa_d = din("w_fa", [128, KC * 8])
    wqb_d = din("w_qb", [D, 768])
    wkb_d = din("w_kb", [D, 768])
    wvb_d = din("w_vb", [D, 768])
    wga_d = din("w_ga", [D, D])
    wgb_d = din("w_gb", [D, D])
    wbra_d = din("w_br_a", [512, D])
    wbrb_d = din("w_br_b", [256, D])
    wout_d = din("w_out", [D, D])
    wfg_d = din("w_ffn_gate", [D, DFF])
    wfu_d = din("w_ffn_up", [D, DFF])
    wfd_d = din("w_ffn_down", [DFF, D])
    ident_d = din("k_ident", [128, 128])
    mask_d = din("k_mask", [128, 768])
    sel_d = din("k_sel", [128, 8 * 128])
    pm_d = din("k_pm", [128, 128])
    cos_d = din("k_cos", [128, S])
    sin_d = din("k_sin", [128, S])
    ones_d = din("k_ones", [128, 128])
    out_d = nc.dram_tensor("out", [S, D], F32, kind="ExternalOutput").ap()
    dbg_d = {}

    with ExitStack() as es:
        S_ = Sched(nc, es)

        def sb(name, shape, dt, scope=es):
            return scope.enter_context(nc.sbuf_tensor(name, list(shape), dt))

        PB = [T(es.enter_context(nc.psum_tensor("pb%d" % i, [128, 512], F32)), excl=True) for i in range(8)]
        PTh = [PB[6], PB[7]]
        PTv = [PB[6].t[:, :].bitcast(BF16), PB[7].t[:, :].bitcast(BF16)]
        pall = Rot(PB)

        bufA = sb("bufA", [128, KC, S], BF16)
        bufB = sb("bufB", [128, KC, S], BF16)
        bufA_d = [T() for _ in range(NC4)]
        bufB_d = [T() for _ in range(NC4)]
        WT = [T(sb("wt%d" % i, [128, KC, 512], BF16)) for i in range(3)]
        wrot = Rot(WT)
        ident_bf = T(sb("ident_bf", [128, 128], BF16))
        ident_f = T(sb("ident_f", [128, 128], F32))
        maskX = T(sb("maskX", [128, 768], BF16))
        selones = T(sb("selones", [128, 8 * 128], BF16))
        pm_bf = T(sb("pm_bf", [128, 128], BF16))
        ones_bf = T(sb("ones_bf", [128, 128], BF16))
        gaBCm = T(sb("gaBCm", [128, D], F32))
        gaBCf = T(sb("gaBCf", [128, D], F32))
        gfinBC = T(sb("gfinBC", [128, D], F32))
        modT = T(sb("modT", [128, 32], F32))
        gsc = T(sb("gsc", [128, 16], F32))
        small = T(sb("small", [128, 64], F32))
        badac = T(sb("badac", [128, 48], F32))
        cs_bf = T(sb("cs_bf", [128, KC], BF16))
        csb_bf = T(sb("csb_bf", [128, KC, 128], BF16))
        ssq = T(sb("ssq", [128, 3 * NT], F32))
        rstd = T(sb("rstd", [128, 3 * NT], F32))

        def load_w(src_ap, ncols, kc=KC, tile=None):
            wt = wrot.next() if tile is None else tile
            S_.dma("pool", lambda e, wt=wt: e.dma_start(
                out=wt.t[:, 0:kc, 0:ncols], in_=src_ap.rearrange("(kc p) n -> p kc n", p=128)), writes=[wt])
            return wt

        def dbg(name, tiles, ap, shape, dt=F32):
            if name not in debug:
                if stop == name:
                    S_.stopped = True
                return
            d = nc.dram_tensor("dbg_" + name, list(shape), dt, kind="ExternalOutput").ap()
            dbg_d[name] = d
            S_.dma("sp", lambda e: e.dma_start(out=d, in_=ap), reads=[T()] + list(tiles))
            if stop == name:
                S_.stopped = True

        def chk(name):
            if stop == name:
                S_.stopped = True

        try:
            for (tl, src) in ((ident_bf, ident_d), (maskX, mask_d), (selones, sel_d), (pm_bf, pm_d), (ones_bf, ones_d)):
                S_.dma("pool", lambda e, tl=tl, src=src: e.dma_start(out=tl.t[:], in_=src), writes=[tl])
            S_.dma("sp", lambda e: e.dma_start(out=ident_f.t[:], in_=ident_d), writes=[ident_f])
            S_.dma("sp", lambda e: e.dma_start(out=small.t[:, 0:8], in_=ccol_d), writes=[small])
            S_.dma("sp", lambda e: e.dma_start(out=small.t[:, 8:16], in_=gmix_d), writes=[small])
            S_.dma("sp", lambda e: e.dma_start(out=small.t[:, 16:24], in_=gffn_d), writes=[small])
            S_.dma("sp", lambda e: e.dma_start(out=badac.t[:], in_=badac_d), writes=[badac])
            S_.op("act", lambda e: e.activation(out=cs_bf.t[:], in_=small.t[:, 0:8], func=AF.Silu), reads=[small], writes=[cs_bf])
            S_.op("dve", lambda e: e.tensor_copy(out=csb_bf.t[:], in_=cs_bf.t[:].unsqueeze(2).to_broadcast([128, KC, 128])),
                  reads=[cs_bf], writes=[csb_bf])

            def ada_cols(sec, dst0):
                pb = pall.next()
                for half in range(2):
                    wt = load_w(wada_d[:, sec * D + half * 512: sec * D + (half + 1) * 512], 512)
                    for j in range(4):
                        col = half * 4 + j
                        for kc in range(KC):
                            S_.op("pe", lambda e, wt=wt, j=j, kc=kc, col=col, pb=pb: e.matmul(
                                pb.t[:, col:col + 1], lhsT=wt.t[:, kc, j * 128:(j + 1) * 128], rhs=cs_bf.t[:, kc:kc + 1],
                                start=(kc == 0), stop=(kc == KC - 1)),
                                reads=[wt, cs_bf], writes=[pb], inc=(kc == KC - 1))
                S_.op("dve", lambda e, pb=pb: e.tensor_tensor(out=modT.t[:, dst0:dst0 + 8], in0=pb.t[:, 0:8],
                                                             in1=badac.t[:, sec * 8:(sec + 1) * 8], op=ALU.add),
                      reads=[pb, badac], writes=[modT])

            def ada_rows(sec, dstT):
                for half in range(2):
                    wt = load_w(wada_d[:, sec * D + half * 512: sec * D + (half + 1) * 512], 512)
                    pb = pall.next()
                    for kc in range(KC):
                        S_.op("pe", lambda e, wt=wt, kc=kc, pb=pb: e.matmul(
                            pb.t[:, :], lhsT=csb_bf.t[:, kc, :], rhs=wt.t[:, kc, :], start=(kc == 0), stop=(kc == KC - 1)),
                            reads=[wt, csb_bf], writes=[pb], inc=(kc == KC - 1))
                    S_.op("dve", lambda e, pb=pb, half=half: e.tensor_tensor(
                        out=dstT.t[:, half * 512:(half + 1) * 512], in0=pb.t[:, :], in1=dstT.t[:, half * 512:(half + 1) * 512],
                        op=ALU.add), reads=[pb, dstT], writes=[dstT])

            def ada_cols_half(sec, dst0, half, tile, ps=None):
                pb = (ps or pall).next()
                wt = load_w(wada_d[:, sec * D + half * 512: sec * D + (half + 1) * 512], 512, tile=tile)
                for j in range(4):
                    for kc in range(KC):
                        S_.op("pe", lambda e, wt=wt, j=j, kc=kc, pb=pb: e.matmul(
                            pb.t[:, j:j + 1], lhsT=wt.t[:, kc, j * 128:(j + 1) * 128], rhs=cs_bf.t[:, kc:kc + 1],
                            start=(kc == 0), stop=(kc == KC - 1)),
                            reads=[wt, cs_bf], writes=[pb], inc=(kc == KC - 1))
                c0 = dst0 + half * 4
                b0 = sec * 8 + half * 4
                S_.op("dve", lambda e, pb=pb: e.tensor_tensor(out=modT.t[:, c0:c0 + 4], in0=pb.t[:, 0:4],
                                                             in1=badac.t[:, b0:b0 + 4], op=ALU.add),
                      reads=[pb, badac], writes=[modT])

            def ada_rows_half(sec, dstT, half, tile, ps=None):
                wt = load_w(wada_d[:, sec * D + half * 512: sec * D + (half + 1) * 512], 512, tile=tile)
                pb = (ps or pall).next()
                for kc in range(KC):
                    S_.op("pe", lambda e, wt=wt, kc=kc, pb=pb: e.matmul(
                        pb.t[:, :], lhsT=csb_bf.t[:, kc, :], rhs=wt.t[:, kc, :], start=(kc == 0), stop=(kc == KC - 1)),
                        reads=[wt, csb_bf], writes=[pb], inc=(kc == KC - 1))
                S_.op("dve", lambda e, pb=pb, half=half: e.tensor_tensor(
                    out=dstT.t[:, half * 512:(half + 1) * 512], in0=pb.t[:, :], in1=dstT.t[:, half * 512:(half + 1) * 512],
                    op=ALU.add), reads=[pb, dstT], writes=[dstT])

            def make_gsc(which):
                sc0 = 8 if which == 0 else 24
                g0 = 8 if which == 0 else 16
                S_.op("dve", lambda e: e.scalar_tensor_tensor(
                    out=gsc.t[:, which * 8:(which + 1) * 8], in0=modT.t[:, sc0:sc0 + 8], scalar=1.0,
                    in1=small.t[:, g0:g0 + 8], op0=ALU.add, op1=ALU.mult), reads=[modT, small], writes=[gsc])

            def ada_first():
                ada_cols(0, 0)
                ada_cols(1, 8)
                make_gsc(0)

            def norm_to_T(src_tile_fn, nidx, dstbuf, dst_d, sh0, gs0, scope, hook=None, after_chunk=None):
                junk = T(sb("junk%d" % nidx, [128, D], BF16, scope))
                xs = [T(sb("xs%d_%d" % (nidx, i), [128, D], BF16, scope)) for i in range(8)]
                sq = T(sb("sq%d" % nidx, [128, 8], F32, scope))
                sq_c = [T() for _ in range(8)]
                ssq_c = [T() for _ in range(NT)]
                rstd_c = [T() for _ in range(NT)]

                def stats(tc):
                    for i in range(4):
                        tt = tc * 4 + i
                        col = nidx * NT + tt
                        b = (tc % 2) * 4 + i
                        xt, xap = src_tile_fn(tt)
                        S_.op("act", lambda e: e.activation(
                            out=junk.t[:], in_=xap, func=AF.Square, accum_out=ssq.t[:, col:col + 1]),
                            reads=[xt], writes=[junk, ssq_c[tt]])
                        S_.op("act", lambda e: e.activation(
                            out=sq.t[:, b:b + 1], in_=ssq.t[:, col:col + 1], func=AF.Sqrt, scale=1.0 / D, bias=EPS),
                            reads=[ssq_c[tt]], writes=[sq_c[b]])
                        S_.op("dve", lambda e: e.reciprocal(out=rstd.t[:, col:col + 1], in_=sq.t[:, b:b + 1]),
                              reads=[sq_c[b]], writes=[rstd_c[tt]])
                        S_.op("dve", lambda e: e.tensor_scalar_mul(
                            out=xs[b].t[:], in0=xap, scalar1=rstd.t[:, col:col + 1]),
                            reads=[xt, rstd_c[tt]], writes=[xs[b]])

                def tr_evac(tc):
                    for c in range(KC):
                        h = c % 2
                        pt = PTh[h]
                        for i in range(4):
                            b = (tc % 2) * 4 + i
                            S_.op("pe", lambda e: e.transpose(
                                PTv[h][:, i * 128:(i + 1) * 128], xs[b].t[:, c * 128:(c + 1) * 128], ident_bf.t[:]),
                                reads=[xs[b], ident_bf], writes=[pt], inc=(i == 3))
                        if h == 0:
                            S_.op("act", lambda e: e.activation(
                                out=dstbuf[:, c, tc * 512:(tc + 1) * 512], in_=PTv[h][:, 0:512], func=AF.Identity,
                                scale=gsc.t[:, gs0 + c:gs0 + c + 1], bias=modT.t[:, sh0 + c:sh0 + c + 1]),
                                reads=[pt, gsc, modT], writes=[dst_d[tc]])
                        else:
                            S_.op("dve", lambda e: e.tensor_scalar(
                                out=dstbuf[:, c, tc * 512:(tc + 1) * 512], in0=PTv[h][:, 0:512],
                                scalar1=gsc.t[:, gs0 + c:gs0 + c + 1], scalar2=modT.t[:, sh0 + c:sh0 + c + 1],
                                op0=ALU.mult, op1=ALU.add), reads=[pt, gsc, modT], writes=[dst_d[tc]])

                stats(0)
                stats(1)
                if hook is not None:
                    hook()
                for tc in range(NC4):
                    tr_evac(tc)
                    if tc + 2 < NC4:
                        stats(tc + 2)
                    if after_chunk is not None:
                        after_chunk(tc)

            sf = es.enter_context(ExitStack())
            Vall = T(sb("Vall", [128, NT, 512], BF16, sf))
            kz = [T(sb("kzA%d" % i, [128, S], BF16, sf)) for i in range(2)]
            qz = [T(sb("qzA%d" % i, [128, S], BF16, sf)) for i in range(2)]
            onesr = T(sb("onesr", [8, 512], F32, sf))
            Fb8 = T(sb("Fb8", [128, S], BF16, sf))
            wv_box = []

            def ada_first_and_wv():
                ada_first()
                wv_box.append(load_w(wva_d, 512, tile=WT[2]))
                S_.op("pool", lambda e: e.memset(qz[0].t[:], 0.0), writes=[qz[0]])
                S_.op("pool", lambda e: e.memset(qz[1].t[:], 0.0), writes=[qz[1]])
                S_.op("pool", lambda e: e.memset(Fb8.t[:], 0.0), writes=[Fb8])
                S_.op("pool", lambda e: e.memset(kz[0].t[:], 0.0), writes=[kz[0]])
                S_.op("pool", lambda e: e.memset(kz[1].t[:], 0.0), writes=[kz[1]])
                S_.op("pool", lambda e: e.memset(kz[0].t[64:65, :], 1.0), writes=[kz[0]])
                S_.op("pool", lambda e: e.memset(kz[1].t[0:1, :], 1.0), writes=[kz[1]])
                S_.op("pool", lambda e: e.memset(onesr.t[:], 1.0), writes=[onesr])

            def v_chunk(tc):
                wv = wv_box[0]
                for tt in range(4 * tc, 4 * tc + 4):
                    pb = pall.next()
                    for kc in range(KC):
                        S_.op("pe", lambda e, kc=kc, tt=tt, pb=pb: e.matmul(
                            pb.t[:, :], lhsT=bufA[:, kc, tt * 128:(tt + 1) * 128], rhs=wv.t[:, kc, :],
                            start=(kc == 0), stop=(kc == KC - 1)), reads=[wv, bufA_d[tc]], writes=[pb], inc=(kc == KC - 1))
                    S_.op("act", lambda e, tt=tt, pb=pb: e.activation(out=Vall.t[:, tt, :], in_=pb.t[:, :], func=AF.Copy),
                          reads=[pb], writes=[Vall])

            with ExitStack() as sc1:
                xin = [T(sb("xin%d" % i, [128, D], F32, sc1)) for i in range(4)]
                xrot = Rot(xin)

                def x_from_hbm(tt):
                    xt = xrot.next()
                    S_.dma("sp", lambda e, xt=xt, tt=tt: e.dma_start(out=xt.t[:], in_=x_d[tt * 128:(tt + 1) * 128, :]), writes=[xt])
                    return xt, xt.t[:]
                norm_to_T(x_from_hbm, 0, bufA, bufA_d, 0, 0, sc1, hook=ada_first_and_wv, after_chunk=v_chunk)
            S_.barrier()
            S_.dma("sp", lambda e: e.dma_start(out=gfinBC.t[:], in_=gfin_d), writes=[gfinBC])
            S_.dma("sp", lambda e: e.dma_start(out=gaBCm.t[:], in_=bgam_d), writes=[gaBCm])
            S_.dma("sp", lambda e: e.dma_start(out=gaBCf.t[:], in_=bgaf_d), writes=[gaBCf])
            dbg("hT", bufA_d, bufA[:], [128, KC, S], BF16)

            deferred = [lambda t, p: ada_rows_half(2, gaBCm, 0, t, p), lambda t, p: ada_rows_half(2, gaBCm, 1, t, p),
                        lambda t, p: ada_cols_half(3, 16, 0, t, p), lambda t, p: ada_cols_half(3, 16, 1, t, p),
                        lambda t, p: ada_cols_half(4, 24, 0, t, p), lambda t, p: (ada_cols_half(4, 24, 1, t, p), make_gsc(1)),
                        lambda t, p: ada_rows_half(5, gaBCf, 0, t, p), lambda t, p: ada_rows_half(5, gaBCf, 1, t, p)]

            STB = Rot([PB[0], PB[1], PB[2], PB[7]])
            ACC = Rot([(PB[3], PB[4]), (PB[5], PB[6])])
            if True:
                yaT = sb("yaT", [128, 4, S], BF16, sf)
                yaT_d = [T() for _ in range(NC4)]
                wbrA = T(sb("wbrA", [128, 4, D], BF16, sf))
                PTb = Rot([T(sb("PTbA%d" % i, [128, 512], BF16, sf)) for i in range(3)])
                rec = Rot([T(sb("recA%d" % i, [128, 512], F32, sf)) for i in range(2)])
                Grow = T(sb("Grow", [8, S], F32, sf))
                spr = T(sb("spr", [8, S], F32, sf))
                Gtok = T(sb("Gtok", [128, NT, 8], F32, sf))
                nbf = T(sb("nbf", [8, 2], F32, sf))
                etmp = T(sb("etmp", [8, 512], F32, sf))

                wf = WT[1]
                S_.dma("pool", lambda e: e.dma_start(out=wf.t[:, :, 0:8], in_=wfa_d.rearrange("p (kc n) -> p kc n", n=8)), writes=[wf])
                wq = load_w(wqa_d, 512, tile=WT[0])
                S_.dma("sp", lambda e: e.dma_start(out=nbf.t[:, 0:1], in_=bfg_d), writes=[nbf])
                S_.op("dve", lambda e: e.tensor_scalar_mul(out=nbf.t[:, 1:2], in0=nbf.t[:, 0:1], scalar1=-1.0),
                      reads=[nbf], writes=[nbf])

                for tc in range(NC4):
                    pb = pall.next()
                    for kc in range(KC):
                        S_.op("pe", lambda e, kc=kc, tc=tc, pb=pb: e.matmul(
                            pb.t[0:8, :], lhsT=wf.t[:, kc, 0:8], rhs=bufA[:, kc, tc * 512:(tc + 1) * 512],
                            start=(kc == 0), stop=(kc == KC - 1)), reads=[wf, bufA_d[tc]], writes=[pb], inc=(kc == KC - 1))
                    S_.op("act", lambda e, pb=pb: e.activation(out=etmp.t[:], in_=pb.t[0:8, :], func=AF.Exp, scale=-1.0,
                                                               bias=nbf.t[:, 1:2]), reads=[pb, nbf], writes=[etmp])
                    S_.op("act", lambda e, tc=tc: e.activation(out=spr.t[:, tc * 512:(tc + 1) * 512], in_=etmp.t[:], func=AF.Ln,
                                                               bias=1.0), reads=[etmp], writes=[spr])
                    if tc == 0:
                        S_.op("dve", lambda e: e.tensor_tensor_scan(out=Grow.t[:, 0:512], data0=onesr.t[:], data1=spr.t[:, 0:512],
                                                                    initial=0.0, op0=ALU.mult, op1=ALU.add),
                              reads=[onesr, spr], writes=[Grow])
                    else:
                        S_.op("dve", lambda e, tc=tc: e.tensor_tensor_scan(
                            out=Grow.t[:, tc * 512:(tc + 1) * 512], data0=onesr.t[:], data1=spr.t[:, tc * 512:(tc + 1) * 512],
                            initial=Grow.t[:, tc * 512 - 1:tc * 512], op0=ALU.mult, op1=ALU.add),
                            reads=[onesr, spr, Grow], writes=[Grow])
                S_.op("dve", lambda e: e.tensor_scalar_mul(out=Fb8.t[0:8, :], in0=Grow.t[:], scalar1=-8.0),
                      reads=[Grow], writes=[Fb8])
                pb = pall.next()
                for tt in range(NT):
                    S_.op("pe", lambda e, tt=tt, pb=pb: e.transpose(pb.t[:, tt * 8:(tt + 1) * 8], Grow.t[0:8, tt * 128:(tt + 1) * 128],
                                                                   ident_f.t[0:8, 0:8]),
                          reads=[Grow, ident_f], writes=[pb], inc=(tt == NT - 1))
                S_.op("dve", lambda e, pb=pb: e.tensor_copy(out=Gtok.t[:].rearrange("p a b -> p (a b)"), in_=pb.t[:, 0:128]),
                      reads=[pb], writes=[Gtok])
                dbg("Grow", [Grow], Grow.t[:], [8, S])

                wk = load_w(wka_d, 512, tile=WT[1])
                S_.dma("pool", lambda e: e.dma_start(out=wbrA.t[:], in_=wbra_d.rearrange("(pr p) n -> p pr n", p=128)), writes=[wbrA])

                for pr in range(4):
                    for tc in range(NC4):
                        pb = pall.next()
                        for kc in range(KC):
                            S_.op("pe", lambda e, kc=kc, tc=tc, pb=pb, pr=pr: e.matmul(
                                pb.t[:, :], lhsT=wq.t[:, kc, pr * 128:(pr + 1) * 128], rhs=bufA[:, kc, tc * 512:(tc + 1) * 512],
                                start=(kc == 0), stop=(kc == KC - 1)), reads=[wq, bufA_d[tc]], writes=[pb], inc=(kc == KC - 1))
                        for hp in range(2):
                            S_.op("act", lambda e, tc=tc, pb=pb, hp=hp: e.activation(
                                out=qz[hp].t[hp * 64:(hp + 1) * 64, tc * 512:(tc + 1) * 512], in_=pb.t[hp * 64:(hp + 1) * 64, :],
                                func=AF.Copy), reads=[pb], writes=[qz[hp]])
                        pb = pall.next()
                        for kc in range(KC):
                            S_.op("pe", lambda e, kc=kc, tc=tc, pb=pb, pr=pr: e.matmul(
                                pb.t[:, :], lhsT=wk.t[:, kc, pr * 128:(pr + 1) * 128], rhs=bufA[:, kc, tc * 512:(tc + 1) * 512],
                                start=(kc == 0), stop=(kc == KC - 1)), reads=[wk, bufA_d[tc]], writes=[pb], inc=(kc == KC - 1))
                        for hp in range(2):
                            S_.op("dve", lambda e, tc=tc, pb=pb, hp=hp: e.tensor_copy(
                                out=kz[hp].t[hp * 64:(hp + 1) * 64, tc * 512:(tc + 1) * 512], in_=pb.t[hp * 64:(hp + 1) * 64, :]),
                                reads=[pb], writes=[kz[hp]])
                        for hp in range(2):
                            h = pr * 2 + hp
                            ar = 64 if hp == 0 else 0
                            pb = pall.next()
                            S_.op("pe", lambda e, pb=pb, h=h, tc=tc: e.matmul(
                                pb.t[:, :], lhsT=selones.t[:, h * 128:(h + 1) * 128], rhs=Fb8.t[:, tc * 512:(tc + 1) * 512],
                                start=True, stop=True), reads=[selones, Fb8], writes=[pb])
                            S_.op("act", lambda e, pb=pb, hp=hp, ar=ar, tc=tc: e.activation(
                                out=qz[hp].t[ar:ar + 1, tc * 512:(tc + 1) * 512], in_=pb.t[ar:ar + 1, :], func=AF.Copy),
                                reads=[pb], writes=[qz[hp]])
                    if pr == 3:
                        wga_pre = [load_w(wga_d[:, 0:512], 512, tile=WT[0]), load_w(wga_d[:, 512:1024], 512, tile=WT[1])]
                    its = []
                    for hp in range(2):
                        for g in range(4):
                            for kb in range(4 * g + 4):
                                its.append(dict(hp=hp, g=g, kb=kb, nkb=4 * g + 4))

                    def qk(it):
                        hp, g, kb = it["hp"], it["g"], it["kb"]
                        if kb == 0:
                            it["acc"] = ACC.next()
                        else:
                            it["acc"] = it["prev"]["acc"]
                        c0 = max(0, kb - 4 * g) * 128
                        q0 = g * 512 + c0
                        st = STB.next()
                        it["st"], it["c0"] = st, c0
                        diag = kb >= 4 * g
                        S_.op("pe", lambda e: e.matmul(
                            st.t[:, c0:512], lhsT=kz[hp].t[:, kb * 128:(kb + 1) * 128], rhs=qz[hp].t[:, q0:(g + 1) * 512],
                            start=True, stop=(not diag)), reads=[kz[hp], qz[hp]], writes=[st], inc=(not diag))
                        if diag:
                            S_.op("pe", lambda e: e.matmul(
                                st.t[:, c0:c0 + 128], lhsT=ident_bf.t[:], rhs=maskX.t[:, 0:128], start=False, stop=True),
                                reads=[ident_bf, maskX], writes=[st], inc=True)

                    def ex_pv(it):
                        hp, g, kb, nkb = it["hp"], it["g"], it["kb"], it["nkb"]
                        h = pr * 2 + hp
                        st, c0 = it["st"], it["c0"]
                        num, den = it["acc"]
                        pt = PTb.next()
                        S_.op("act", lambda e: e.activation(
                            out=pt.t[:, c0:512], in_=st.t[:, c0:512], func=AF.Exp, scale=SCALE, bias=Gtok.t[:, kb, h:h + 1]),
                            reads=[st, Gtok], writes=[pt])
                        S_.op("pe", lambda e: e.matmul(
                            num.t[:, c0:512], lhsT=Vall.t[:, kb, pr * 128:(pr + 1) * 128], rhs=pt.t[:, c0:512],
                            start=(kb == 0), stop=(kb == nkb - 1), skip_group_check=True),
                            reads=[Vall, pt], writes=[num], inc=False)
                        S_.op("pe", lambda e: e.matmul(
                            den.t[:, c0:512], lhsT=ones_bf.t[:], rhs=pt.t[:, c0:512],
                            start=(kb == 0), stop=(kb == nkb - 1), skip_group_check=True),
                            reads=[ones_bf, pt], writes=[den], inc=True)
                        if kb == nkb - 1:
                            lanes = slice(hp * 64, (hp + 1) * 64)
                            rc = rec.next()
                            S_.op("dve", lambda e: e.reciprocal(out=rc.t[lanes, :], in_=den.t[lanes, :]), reads=[den], writes=[rc])
                            S_.op("dve", lambda e: e.tensor_tensor(
                                out=yaT[lanes, pr, g * 512:(g + 1) * 512], in0=num.t[lanes, :], in1=rc.t[lanes, :], op=ALU.mult),
                                reads=[num, rc], writes=[yaT_d[g]])

                    for i, it in enumerate(its):
                        it["prev"] = its[i - 1] if i > 0 else None
                    AH = 2
                    for i in range(AH):
                        qk(its[i])
                    for i, it in enumerate(its):
                        if i + AH < len(its):
                            qk(its[i + AH])
                        ex_pv(it)
                        if i == len(its) // 2 or i == len(its) - 1:
                            deferred.pop(0)(WT[2], STB)
                dbg("yaT", yaT_d, yaT[:], [128, 4, S], BF16)

                wbr = wbrA
                sg = Rot([T(sb("sgA%d" % i, [128, 512], F32, sf)) for i in range(2)])
                wqb_pre = load_w(wqb_d[:, 0:512], 512, tile=WT[2])
                for half in range(2):
                    wg = wga_pre[half]
                    if half == 1:
                        wkb_pre = load_w(wkb_d[:, 0:512], 512, tile=WT[0])
                    for j in range(4):
                        nch = half * 4 + j
                        for tc in range(NC4):
                            pg = pall.next()
                            for kc in range(KC):
                                S_.op("pe", lambda e, kc=kc, tc=tc, pg=pg, j=j, wg=wg: e.matmul(
                                    pg.t[:, :], lhsT=wg.t[:, kc, j * 128:(j + 1) * 128], rhs=bufA[:, kc, tc * 512:(tc + 1) * 512],
                                    start=(kc == 0), stop=(kc == KC - 1)), reads=[wg, bufA_d[tc]], writes=[pg], inc=(kc == KC - 1))
                            s_ = sg.next()
                            S_.op("act", lambda e, pg=pg, s_=s_: e.activation(out=s_.t[:], in_=pg.t[:, :], func=AF.Sigmoid),
                                  reads=[pg], writes=[s_])
                            pbr = pall.next()
                            for pr in range(4):
                                S_.op("pe", lambda e, pr=pr, tc=tc, pbr=pbr, nch=nch: e.matmul(
                                    pbr.t[:, :], lhsT=wbr.t[:, pr, nch * 128:(nch + 1) * 128], rhs=yaT[:, pr, tc * 512:(tc + 1) * 512],
                                    start=(pr == 0), stop=(pr == 3)), reads=[wbr, yaT_d[tc]], writes=[pbr], inc=(pr == 3))
                            S_.op("dve", lambda e, pbr=pbr, s_=s_, nch=nch, tc=tc: e.tensor_tensor(
                                out=bufB[:, nch, tc * 512:(tc + 1) * 512], in0=pbr.t[:, :], in1=s_.t[:], op=ALU.mult),
                                reads=[pbr, s_], writes=[bufB_d[tc]])
            sf.close()
            S_.barrier()
            dbg("mA", bufB_d, bufB[:], [128, KC, S], BF16)

            with ExitStack() as sd:
                ybT = sb("ybT", [128, 2, S], BF16, sd)
                ybT_d = [T() for _ in range(2)]
                wbrB = T(sb("wbrB", [128, 2, D], BF16, sd))
                S_.dma("pool", lambda e: e.dma_start(out=wbrB.t[:], in_=wbrb_d.rearrange("(pr p) n -> p pr n", p=128)), writes=[wbrB])
                sda = sd.enter_context(ExitStack())
                cosT = T(sb("cosT", [128, S], F32, sda))
                sinT = T(sb("sinT", [128, S], F32, sda))
                S_.dma("sp", lambda e: e.dma_start(out=cosT.t[:], in_=cos_d), writes=[cosT])
                S_.dma("sp", lambda e: e.dma_start(out=sinT.t[:], in_=sin_d), writes=[sinT])
                accN = T(sb("accN", [128, S], F32, sda))
                accD = T(sb("accD", [128, S], F32, sda))
                kp = T(sb("kpB", [128, S], BF16, sda))
                qz = [T(sb("qzB%d" % i, [128, S], BF16, sda)) for i in range(2)]
                Vp = T(sb("VpB", [128, NT, 128], BF16, sda))
                PTb = Rot([T(sb("PTbB%d" % i, [128, 512], BF16, sda)) for i in range(3)])
                qtmp = Rot([T(sb("qtmp%d" % i, [128, 512], BF16, sda)) for i in range(2)])
                t1r = Rot([T(sb("t1r%d" % i, [128, 512], F32, sda)) for i in range(2)])
                t2r = Rot([T(sb("t2r%d" % i, [128, 512], F32, sda)) for i in range(2)])
                S_.op("dve", lambda e: e.memset(qz[0].t[:], 0.0), writes=[qz[0]])
                S_.op("dve", lambda e: e.memset(qz[1].t[:], 0.0), writes=[qz[1]])
                wqb = [wqb_pre, None]
                wkb = [wkb_pre, None]
                wvb = [load_w(wvb_d[:, 0:512], 512, tile=WT[1]), None]
                WX = [T(sb("wx%d" % i, [128, KC, 256], BF16, sda)) for i in range(3)]
                for i, src in enumerate((wqb_d, wkb_d, wvb_d)):
                    S_.dma("pool", lambda e, i=i, src=src: e.dma_start(
                        out=WX[i].t[:], in_=src[:, 512:768].rearrange("(kc p) n -> p kc n", p=128)), writes=[WX[i]])

                def wsel(main, extra, pc):
                    if pc < 4:
                        return main, main.t, pc * 128
                    return extra, extra.t, (pc - 4) * 128

                chk("c1")
                def make_step(m, g, qz, kp):
                    d = DIL[g]
                    pc = 2 * g + m
                    nbk = 16 // d
                    wq_t, wq_ap, wq_c = wsel(wqb[0], WX[0], pc)
                    wk_t, wk_ap, wk_c = wsel(wkb[0], WX[1], pc)
                    wv_t, wv_ap, wv_c = wsel(wvb[0], WX[2], pc)
                    def f_proj():
                        pend = []
                        for which in range(2):
                            w_t, w_ap, w_c = (wq_t, wq_ap, wq_c) if which == 0 else (wk_t, wk_ap, wk_c)
                            for j in range(4):
                                pq = pall.next()
                                for kc in range(KC):
                                    S_.op("pe", lambda e, kc=kc, j=j, pq=pq, w_ap=w_ap, w_c=w_c, g=g: e.matmul(
                                        pq.t[:, :], lhsT=w_ap[:, kc, w_c:w_c + 128], rhs=bufA[:, kc, j * 512:(j + 1) * 512],
                                        start=(kc == 0), stop=(kc == KC - 1)), reads=[w_t, bufA_d[j]], writes=[pq], inc=(kc == KC - 1))
                                chk("c2")
                                qt = qtmp.next()
                                S_.op("act", lambda e, pq=pq, qt=qt: e.activation(out=qt.t[:], in_=pq.t[:, :], func=AF.Copy),
                                      reads=[pq], writes=[qt])
                                def post(pq=pq, qt=qt, j=j, which=which):
                                    psw = pall.next()
                                    S_.op("pe", lambda e, psw=psw, qt=qt: e.matmul(psw.t[:, :], lhsT=pm_bf.t[:], rhs=qt.t[:], start=True, stop=True),
                                          reads=[pm_bf, qt], writes=[psw])
                                    chk("c4")
                                    t1 = t1r.next()
                                    t2 = t2r.next()
                                    S_.op("dve", lambda e, pq=pq, t1=t1, g=g, j=j: e.tensor_tensor(
                                        out=t1.t[:], in0=pq.t[:, :], in1=cosT.t[:, j * 512:(j + 1) * 512], op=ALU.mult),
                                        reads=[pq, cosT], writes=[t1])
                                    chk("c4b")
                                    S_.op("dve", lambda e, psw=psw, t2=t2, g=g, j=j: e.tensor_tensor(
                                        out=t2.t[:], in0=psw.t[:, :], in1=sinT.t[:, j * 512:(j + 1) * 512], op=ALU.mult),
                                        reads=[psw, sinT], writes=[t2])
                                    chk("c5")
                                    if which == 0:
                                        for hp in range(2):
                                            S_.op("pool", lambda e, t1=t1, t2=t2, hp=hp, j=j: e.tensor_tensor(
                                                out=cls_out(qz[hp].t[hp * 64:(hp + 1) * 64, :], g, j), in0=nat_in(t1.t[hp * 64:(hp + 1) * 64, :], g),
                                                in1=nat_in(t2.t[hp * 64:(hp + 1) * 64, :], g), op=ALU.add), reads=[t1, t2], writes=[qz[hp]])
                                    else:
                                        S_.op("pool", lambda e, t1=t1, t2=t2, j=j: e.tensor_tensor(
                                            out=cls_out(kp.t[:, :], g, j), in0=nat_in(t1.t[:], g), in1=nat_in(t2.t[:], g), op=ALU.add),
                                            reads=[t1, t2], writes=[kp])
                                pend.append(post)
                                if len(pend) > 1:
                                    pend.pop(0)()
                        while pend:
                            pend.pop(0)()
                        chk("dq%d%d" % (m, g))
                    def f_v():
                        for cb4 in range(4):
                            pv = pall.next()
                            for i in range(4):
                                cb = cb4 * 4 + i
                                r, n_ = cb // nbk, cb % nbk
                                t0 = d * 128 * n_ + r
                                for kc in range(KC):
                                    S_.op("pe", lambda e, kc=kc, i=i, pv=pv, t0=t0, d=d, wv_ap=wv_ap, wv_c=wv_c: e.matmul(
                                        pv.t[:, i * 128:(i + 1) * 128], lhsT=bufA[:, kc, t0:t0 + 127 * d + 1:d], rhs=wv_ap[:, kc, wv_c:wv_c + 128],
                                        start=(kc == 0), stop=(kc == KC - 1)), reads=[wv_t] + bufA_d, writes=[pv],
                                        inc=(kc == KC - 1 and i == 3))
                            S_.op("act", lambda e, pv=pv, cb4=cb4: e.activation(
                                out=Vp.t[:, cb4 * 4:(cb4 + 1) * 4, :].rearrange("p a b -> p (a b)"), in_=pv.t[:, :], func=AF.Copy),
                                reads=[pv], writes=[Vp])
                        chk("dv%d%d" % (m, g))
                    def f_att():
                        units = []
                        for hp in range(2):
                            if g < 2:
                                for cb in range(16):
                                    hasnext = (cb % nbk) < nbk - 1
                                    units.append(dict(hp=hp, kbs=[cb], qcols=(cb * 128, (cb + (2 if hasnext else 1)) * 128),
                                                      mask0=0))
                            else:
                                for j in range(4):
                                    units.append(dict(hp=hp, kbs=[4 * j + i for i in range(4)], qcols=(j * 512, (j + 1) * 512),
                                                      mask0=256))
                        accs = {}

                        def acc_of(hp, j):
                            if (hp, j) not in accs:
                                accs[(hp, j)] = ACC.next()
                            return accs[(hp, j)]

                        def d_qk(u):
                            hp = u["hp"]
                            st = STB.next()
                            u["st"] = st
                            q0, q1 = u["qcols"]
                            n = q1 - q0
                            u["n"] = n
                            if g < 2:
                                cb = u["kbs"][0]
                                S_.op("pe", lambda e: e.matmul(
                                    st.t[:, 0:n], lhsT=kp.t[:, cb * 128:(cb + 1) * 128], rhs=qz[hp].t[:, q0:q1],
                                    start=True, stop=False), reads=[kp, qz[hp]], writes=[st], inc=False)
                            else:
                                for i, cb in enumerate(u["kbs"]):
                                    S_.op("pe", lambda e: e.matmul(
                                        st.t[:, i * 128:(i + 1) * 128], lhsT=kp.t[:, cb * 128:(cb + 1) * 128],
                                        rhs=qz[hp].t[:, cb * 128:(cb + 1) * 128], start=(i == 0), stop=False),
                                        reads=[kp, qz[hp]], writes=[st], inc=False)
                            m0 = u["mask0"]
                            S_.op("pe", lambda e: e.matmul(
                                st.t[:, 0:n], lhsT=ident_bf.t[:], rhs=maskX.t[:, m0:m0 + n], start=False, stop=True),
                                reads=[ident_bf, maskX], writes=[st], inc=True)

                        def d_pv(u):
                            hp, st, n = u["hp"], u["st"], u["n"]
                            lanes = slice(hp * 64, (hp + 1) * 64)
                            pt = PTb.next()
                            S_.op("act", lambda e: e.activation(out=pt.t[:, 0:n], in_=st.t[:, 0:n], func=AF.Exp, scale=SCALE),
                                  reads=[st], writes=[pt])
                            contribs = []
                            if g < 2:
                                cb = u["kbs"][0]
                                contribs.append((cb, 0, cb, (cb % nbk) == 0))
                                if n == 256:
                                    contribs.append((cb, 128, cb + 1, True))
                            else:
                                for i, cb in enumerate(u["kbs"]):
                                    contribs.append((cb, i * 128, cb, True))
                            for (kb_, pc, qb_, first) in contribs:
                                num, den = acc_of(hp, qb_ // 4)
                                cols = slice((qb_ % 4) * 128, (qb_ % 4 + 1) * 128)
                                last = (kb_ == qb_)
                                S_.op("pe", lambda e: e.matmul(
                                    num.t[:, cols], lhsT=Vp.t[:, kb_, :], rhs=pt.t[:, pc:pc + 128], start=first, stop=last),
                                    reads=[Vp, pt], writes=[num], inc=False)
                                S_.op("pe", lambda e: e.matmul(
                                    den.t[:, cols], lhsT=ones_bf.t[:], rhs=pt.t[:, pc:pc + 128], start=first, stop=last),
                                    reads=[ones_bf, pt], writes=[den], inc=True)
                                if last and qb_ % 4 == 3:
                                    j = qb_ // 4
                                    for (acc, src) in ((accN, num), (accD, den)):
                                        if g == 0:
                                            S_.op("dve", lambda e: e.tensor_copy(
                                                out=tok_view(acc.t[lanes, :], g, j), in_=chunk_view(src.t[lanes, :], g)),
                                                reads=[src], writes=[acc])
                                        else:
                                            S_.op("dve", lambda e: e.tensor_tensor(
                                                out=tok_view(acc.t[lanes, :], g, j), in0=chunk_view(src.t[lanes, :], g),
                                                in1=tok_view(acc.t[lanes, :], g, j), op=ALU.add), reads=[src, acc], writes=[acc])

                        AHEAD = 2
                        for i in range(min(AHEAD, len(units))):
                            d_qk(units[i])
                        for i, u in enumerate(units):
                            if i + AHEAD < len(units):
                                d_qk(units[i + AHEAD])
                            d_pv(u)
                        chk("da%d%d" % (m, g))
                    return f_proj, f_v, f_att

                def finalize(m):
                    S_.op("act", lambda e: e.activation(out=accD.t[:], in_=accD.t[:], func=AF.Ln), reads=[accD], writes=[accD])
                    S_.op("act", lambda e: e.activation(out=accD.t[:], in_=accD.t[:], func=AF.Exp, scale=-1.0), reads=[accD], writes=[accD])
                    S_.op("dve", lambda e, m=m: e.tensor_tensor(out=ybT[:, m, :], in0=accN.t[:], in1=accD.t[:], op=ALU.mult),
                          reads=[accN, accD], writes=[ybT_d[m]])

                qzs = [qz, [T(sb("qzC%d" % i, [128, S], BF16, sda)) for i in range(2)]]
                kps = [kp, T(sb("kpC", [128, S], BF16, sda))]
                S_.op("dve", lambda e: e.memset(qzs[1][0].t[:], 0.0), writes=[qzs[1][0]])
                S_.op("dve", lambda e: e.memset(qzs[1][1].t[:], 0.0), writes=[qzs[1][1]])
                steps = [make_step(m_, g_, qzs[(m_ * 3 + g_) % 2], kps[(m_ * 3 + g_) % 2]) for m_ in range(2) for g_ in range(3)]
                steps[0][0]()
                steps[0][1]()
                for s_i in range(6):
                    if s_i + 1 < 6:
                        steps[s_i + 1][0]()
                    if s_i == 4:
                        wgb_pre = [load_w(wgb_d[:, 0:512], 512, tile=wqb[0]), load_w(wgb_d[:, 512:1024], 512, tile=wkb[0])]
                    if s_i == 5:
                        wo_pre0 = load_w(wout_d[:, 0:512], 512, tile=wvb[0])
                    steps[s_i][2]()
                    if s_i % 3 == 2:
                        finalize(s_i // 3)
                    if s_i + 1 < 6:
                        steps[s_i + 1][1]()
                sda.close()
                S_.barrier()
                dbg("ybT", ybT_d, ybT[:], [128, 2, S], BF16)

                wbr = wbrB
                sg = Rot([T(sb("sgB%d" % i, [128, 512], F32, sd)) for i in range(2)])
                tm = Rot([T(sb("tmB%d" % i, [128, 512], F32, sd)) for i in range(2)])
                S_.op("dve", lambda e: e.tensor_tensor(
                    out=wo_pre0.t[:], in0=wo_pre0.t[:],
                    in1=gaBCm.t[:, 0:512].unsqueeze(1).to_broadcast([128, KC, 512]), op=ALU.mult),
                    reads=[wo_pre0, gaBCm], writes=[wo_pre0])
                for half in range(2):
                    wg = wgb_pre[half]
                    if half == 1:
                        wo_pre1 = load_w(wout_d[:, 512:1024], 512, tile=wgb_pre[0])
                    for j in range(4):
                        nch = half * 4 + j
                        for tc in range(NC4):
                            pg = pall.next()
                            for kc in range(KC):
                                S_.op("pe", lambda e, kc=kc, tc=tc, pg=pg, j=j, wg=wg: e.matmul(
                                    pg.t[:, :], lhsT=wg.t[:, kc, j * 128:(j + 1) * 128], rhs=bufA[:, kc, tc * 512:(tc + 1) * 512],
                                    start=(kc == 0), stop=(kc == KC - 1)), reads=[wg, bufA_d[tc]], writes=[pg], inc=(kc == KC - 1))
                            s_ = sg.next()
                            S_.op("act", lambda e, pg=pg, s_=s_: e.activation(out=s_.t[:], in_=pg.t[:, :], func=AF.Sigmoid),
                                  reads=[pg], writes=[s_])
                            pbr = pall.next()
                            for pr in range(2):
                                S_.op("pe", lambda e, pr=pr, tc=tc, pbr=pbr, nch=nch: e.matmul(
                                    pbr.t[:, :], lhsT=wbr.t[:, pr, nch * 128:(nch + 1) * 128], rhs=ybT[:, pr, tc * 512:(tc + 1) * 512],
                                    start=(pr == 0), stop=(pr == 1)), reads=[wbr] + ybT_d, writes=[pbr], inc=(pr == 1))
                            t_ = tm.next()
                            S_.op("dve", lambda e, pbr=pbr, s_=s_, t_=t_: e.tensor_tensor(
                                out=t_.t[:], in0=pbr.t[:, :], in1=s_.t[:], op=ALU.mult), reads=[pbr, s_], writes=[t_])
                            S_.op("pool", lambda e, t_=t_, nch=nch, tc=tc: e.tensor_tensor(
                                out=bufB[:, nch, tc * 512:(tc + 1) * 512], in0=bufB[:, nch, tc * 512:(tc + 1) * 512], in1=t_.t[:], op=ALU.add),
                                reads=[t_, bufB_d[tc]], writes=[bufB_d[tc]])
                S_.op("dve", lambda e: e.tensor_tensor(
                    out=wo_pre1.t[:], in0=wo_pre1.t[:],
                    in1=gaBCm.t[:, 512:1024].unsqueeze(1).to_broadcast([128, KC, 512]), op=ALU.mult),
                    reads=[wo_pre1, gaBCm], writes=[wo_pre1])
            S_.barrier()
            dbg("merged", bufB_d, bufB[:], [128, KC, S], BF16)

            with ExitStack() as s2:
                x1 = sb("x1", [128, NT, D], F32, s2)
                x1_d = [T() for _ in range(NT)]
                WT4 = T(sb("wt4", [128, KC, 512], BF16, s2))
                for tt in range(NT):
                    S_.dma("sp", lambda e, tt=tt: e.dma_start(out=x1[:, tt, :], in_=x_d[tt * 128:(tt + 1) * 128, :]), writes=[x1_d[tt]])
                so = s2.enter_context(ExitStack())
                tmo = Rot([T(sb("tmo%d" % i, [128, 512], F32, so)) for i in range(3)])
                wo = [wo_pre0, wo_pre1]
                others = [t for t in WT if t is not wo_pre0 and t is not wo_pre1]
                pre_w = [load_w(wfg_d[:, 0:512], 512, tile=others[0]), load_w(wfu_d[:, 0:512], 512, tile=WT4)]
                wrot = Rot([wo_pre0, wo_pre1, others[0], WT4])

                def outproj_tile(tt):
                    for ch in range(2):
                        po = pall.next()
                        for kc in range(KC):
                            S_.op("pe", lambda e: e.matmul(
                                po.t[:, :], lhsT=bufB[:, kc, tt * 128:(tt + 1) * 128], rhs=wo[ch].t[:, kc, :],
                                start=(kc == 0), stop=(kc == KC - 1)), reads=[wo[ch], bufB_d[tt // 4]], writes=[po], inc=(kc == KC - 1))
                        S_.op("dve", lambda e: e.tensor_tensor(
                            out=x1[:, tt, ch * 512:(ch + 1) * 512], in0=po.t[:, :], in1=x1[:, tt, ch * 512:(ch + 1) * 512], op=ALU.add),
                            reads=[po, x1_d[tt]], writes=[x1_d[tt]])
                    return x1_d[tt], x1[:, tt, :]

                with ExitStack() as sn:
                    norm_to_T(outproj_tile, 1, bufA, bufA_d, 16, 8, sn)
                so.close()
                S_.barrier()
                dbg("x1", x1_d, x1[:], [128, NT, D])
                dbg("h2T", bufA_d, bufA[:], [128, KC, S], BF16)
                sf2 = s2.enter_context(ExitStack())
                tmo = Rot([T(sb("tmf%d" % i, [128, 512], F32, sf2)) for i in range(3)])

                wdt = [T(sb("wd%d" % i, [128, KC, D], BF16, sf2)) for i in range(1)]
                sa = Rot([T(sb("sa%d" % i, [128, 512], F32, sf2)) for i in range(2)])
                groups = [(0, 8), (8, 8), (16, 6)]
                sqf = T(sb("sqF", [128, NT], F32, sf2))

                def final_tile(tt):
                    col = 2 * NT + tt
                    jk = sa.next()
                    S_.op("act", lambda e: e.activation(out=jk.t[:].bitcast(BF16), in_=x1[:, tt, :], func=AF.Square,
                                                        accum_out=ssq.t[:, col:col + 1]), reads=[x1_d[tt]], writes=[jk, ssq])
                    S_.op("act", lambda e: e.activation(out=sqf.t[:, tt:tt + 1], in_=ssq.t[:, col:col + 1], func=AF.Sqrt,
                                                        scale=1.0 / D, bias=EPS), reads=[ssq], writes=[sqf])
                    S_.op("dve", lambda e: e.reciprocal(out=rstd.t[:, col:col + 1], in_=sqf.t[:, tt:tt + 1]), reads=[sqf], writes=[rstd])
                    for ch in range(2):
                        y = tmo.next()
                        S_.op("act", lambda e: e.activation(out=y.t[:], in_=x1[:, tt, ch * 512:(ch + 1) * 512], func=AF.Identity,
                                                            scale=rstd.t[:, col:col + 1]), reads=[x1_d[tt], rstd], writes=[y])
                        S_.op("dve", lambda e: e.tensor_tensor(out=y.t[:], in0=y.t[:], in1=gfinBC.t[:, ch * 512:(ch + 1) * 512],
                                                               op=ALU.mult), reads=[y, gfinBC], writes=[y])
                        S_.dma("sp", lambda e: e.dma_start(out=out_d[tt * 128:(tt + 1) * 128, ch * 512:(ch + 1) * 512], in_=y.t[:]),
                               reads=[y])
                for (f0, nf) in groups:
                    wd = wdt[0]
                    for q4 in range(0, nf, 4):
                        nq = min(4, nf - q4)
                        c0 = (f0 + q4) * 128
                        if pre_w:
                            wg, wu = pre_w
                            pre_w = None
                        else:
                            wg = load_w(wfg_d[:, c0:c0 + nq * 128], nq * 128)
                            wu = load_w(wfu_d[:, c0:c0 + nq * 128], nq * 128)
                        if q4 == 0:
                            S_.dma("pool", lambda e, f0=f0, nf=nf, wd=wd: e.dma_start(
                                out=wd.t[:, 0:nf, :], in_=wfd_d[f0 * 128:(f0 + nf) * 128, :].rearrange("(kc p) n -> p kc n", p=128)),
                                writes=[wd])
                            S_.op("pool", lambda e, nf=nf, wd=wd: e.tensor_tensor(
                                out=wd.t[:, 0:nf, :], in0=wd.t[:, 0:nf, :],
                                in1=gaBCf.t[:, :].unsqueeze(1).to_broadcast([128, nf, D]), op=ALU.mult),
                                reads=[wd, gaBCf], writes=[wd])
                        for jj in range(nq):
                            fl = q4 + jj
                            for tc in range(NC4):
                                pa = pall.next()
                                for kc in range(KC):
                                    S_.op("pe", lambda e, kc=kc, tc=tc, pa=pa, jj=jj, wg=wg: e.matmul(
                                        pa.t[:, :], lhsT=wg.t[:, kc, jj * 128:(jj + 1) * 128], rhs=bufA[:, kc, tc * 512:(tc + 1) * 512],
                                        start=(kc == 0), stop=(kc == KC - 1)), reads=[wg, bufA_d[tc]], writes=[pa], inc=(kc == KC - 1))
                                pu = pall.next()
                                for kc in range(KC):
                                    S_.op("pe", lambda e, kc=kc, tc=tc, pu=pu, jj=jj, wu=wu: e.matmul(
                                        pu.t[:, :], lhsT=wu.t[:, kc, jj * 128:(jj + 1) * 128], rhs=bufA[:, kc, tc * 512:(tc + 1) * 512],
                                        start=(kc == 0), stop=(kc == KC - 1)), reads=[wu, bufA_d[tc]], writes=[pu], inc=(kc == KC - 1))
                                s_ = sa.next()
                                S_.op("act", lambda e, pa=pa, s_=s_: e.activation(out=s_.t[:], in_=pa.t[:, :], func=AF.Silu),
                                      reads=[pa], writes=[s_])
                                S_.op("dve", lambda e, pu=pu, s_=s_, fl=fl, tc=tc: e.tensor_tensor(
                                    out=bufB[:, fl, tc * 512:(tc + 1) * 512], in0=pu.t[:, :], in1=s_.t[:], op=ALU.mult),
                                    reads=[pu, s_], writes=[bufB_d[tc]])
                    for tt in range(NT):
                        for ch in range(2):
                            po = pall.next()
                            for kc in range(nf):
                                S_.op("pe", lambda e, kc=kc, tt=tt, ch=ch, po=po, nf=nf, wd=wd: e.matmul(
                                    po.t[:, :], lhsT=bufB[:, kc, tt * 128:(tt + 1) * 128], rhs=wd.t[:, kc, ch * 512:(ch + 1) * 512],
                                    start=(kc == 0), stop=(kc == nf - 1)), reads=[wd, bufB_d[tt // 4]], writes=[po], inc=(kc == nf - 1))
                            S_.op("dve", lambda e, po=po, tt=tt, ch=ch: e.tensor_tensor(
                                out=x1[:, tt, ch * 512:(ch + 1) * 512], in0=po.t[:, :], in1=x1[:, tt, ch * 512:(ch + 1) * 512], op=ALU.add),
                                reads=[po, x1_d[tt]], writes=[x1_d[tt]])
                        if f0 + nf == NFF:
                            if tt >= 2:
                                final_tile(tt - 2)
                            if tt == NT - 1:
                                final_tile(NT - 2)
                                final_tile(NT - 1)

                dbg("x2", x1_d, x1[:], [128, NT, D])
                sf2.close()
        except _Stop:
            pass
        S_.finish()
        S_.emit()
    return nc, dbg_d


def _consts():
    ident = np.eye(128, dtype=np.float32)
    k = np.arange(128)[:, None]
    q = np.arange(128)[None, :]
    anti = np.where(k >= q, 0.0, NEG).astype(np.float32)
    caus = np.where(k <= q, 0.0, NEG).astype(np.float32)
    mask = np.concatenate([caus, anti, caus, caus, caus, caus], axis=1)
    sel = np.zeros((128, 8 * 128), np.float32)
    for h in range(8):
        sel[h, h * 128:(h + 1) * 128] = 1.0
    pm = np.zeros((128, 128), np.float32)
    cosf = np.ones((128, S), np.float32)
    sinf = np.zeros((128, S), np.float32)
    pos = np.arange(S, dtype=np.float32)
    inv_freq = (np.float32(500000.0) ** (-(np.arange(0, 16, 2, dtype=np.float32)) / np.float32(16))).astype(np.float32)
    ang = (pos[:, None] * inv_freq[None, :]).astype(np.float32)
    cs = np.cos(ang).astype(np.float32).T
    sn = np.sin(ang).astype(np.float32).T
    for hp in range(2):
        for dd in range(16):
            p = hp * 64 + dd
            i = dd % 8
            partner = p + 8 if dd < 8 else p - 8
            pm[partner, p] = 1.0
            cosf[p] = cs[i]
            sinf[p] = -sn[i] if dd < 8 else sn[i]
    ones = np.ones((128, 128), np.float32)
    return dict(k_ident=ident, k_mask=mask, k_sel=sel, k_pm=pm, k_cos=cosf, k_sin=sinf, k_ones=ones)


def _col(v, n):
    return np.ascontiguousarray(np.asarray(v, np.float32).reshape(n, 128).T)


def _prep_inputs(x, c, w_ada, b_ada, g_mix, w_in, b_fgate, w_br_a, w_br_b, w_out,
                 g_ffn, w_ffn_gate, w_ffn_up, w_ffn_down, g_final):
    f = lambda a: np.ascontiguousarray(np.asarray(a, dtype=np.float32))
    x, c = f(x), f(c)
    w_in0 = f(w_in)[0]
    cuts = np.cumsum([512, 512, 512, 8, 768, 768, 768, 1024, 1024])[:-1]
    qa, ka, va, fa, qb, kb, vb, ga, gb = [np.ascontiguousarray(p) for p in np.split(w_in0, cuts, axis=1)]
    b_ada0 = f(b_ada)[0]
    shared = dict(
        w_ada=f(w_ada)[0], b_ada_col=_col(b_ada0, 48),
        b_gam_bc=np.ascontiguousarray(np.broadcast_to(b_ada0[2 * D:3 * D], (128, D))),
        b_gaf_bc=np.ascontiguousarray(np.broadcast_to(b_ada0[5 * D:6 * D], (128, D))),
        g_mix_col=_col(f(g_mix)[0], KC), g_ffn_col=_col(f(g_ffn)[0], KC),
        g_fin_bc=np.ascontiguousarray(np.broadcast_to(f(g_final), (128, D))),
        b_fg_col=np.ascontiguousarray(f(b_fgate)[0].reshape(8, 1)),
        w_qa=qa, w_ka=ka, w_va=va, w_fa=np.ascontiguousarray(fa.reshape(KC, 128, 8).transpose(1, 0, 2).reshape(128, KC * 8)), w_qb=qb, w_kb=kb, w_vb=vb, w_ga=ga, w_gb=gb,
        w_br_a=f(w_br_a)[0], w_br_b=f(w_br_b)[0], w_out=f(w_out)[0],
        w_ffn_gate=f(w_ffn_gate)[0], w_ffn_up=f(w_ffn_up)[0], w_ffn_down=f(w_ffn_down)[0],
    )
    shared.update(_consts())
    in_maps = []
    for b in range(8):
        m = dict(shared)
        m["x"] = np.ascontiguousarray(x[b])
        m["c_col"] = _col(c[b], KC)
        in_maps.append(m)
    return in_maps


_NC_CACHE = {}


def kernel(**inputs):
    in_maps = _prep_inputs(**inputs)
    if "nc" not in _NC_CACHE:
        _NC_CACHE["nc"] = build_program()[0]
    nc = _NC_CACHE["nc"]
    res = run_bass_kernel_spmd(nc, in_maps, core_ids=list(range(8)))
    out = np.stack([np.asarray(r["out"], dtype=np.float32).reshape(S, D) for r in res.results], axis=0)
    return out
```
